# Optimizing a Trainium2 kernel written in Bass

```python
import jax, jax.numpy as jnp
from jax import lax
import numpy as np

D_MODEL = 1024
BATCH = 4
SEQ = 8192
DEPTH = 1

HEAD_DIM = 64
ATTN_GROUPS = ((128, 1), (512, 4), (2048, 16))
N_GROUPS = len(ATTN_GROUPS)
ATTN_HEADS_PER_GROUP = 4
ATTN_GROUP_WIDTH = ATTN_HEADS_PER_GROUP * HEAD_DIM
Q_BLOCK = 128
ROPE_THETA = 10000.0
RWKV_HEADS = 8
RWKV_WIDTH = RWKV_HEADS * HEAD_DIM
DECAY_LORA = 32
ICLR_LORA = 32
GATE_LORA = 96
RWKV_GN_EPS = 64e-5
D_FF = 4 * D_MODEL
PLE_DIM = 256
NORM_EPS = 1e-6

ATTN_QKV_COLS = 3 * N_GROUPS * ATTN_GROUP_WIDTH
RWKV_SPLITS = (RWKV_WIDTH, 2 * RWKV_WIDTH, 3 * RWKV_WIDTH,
               3 * RWKV_WIDTH + DECAY_LORA, 3 * RWKV_WIDTH + DECAY_LORA + ICLR_LORA)
RWKV_COLS = 3 * RWKV_WIDTH + DECAY_LORA + ICLR_LORA + GATE_LORA
GATE_COLS = 2 * D_MODEL
IN_COLS = ATTN_QKV_COLS + RWKV_COLS + GATE_COLS

kernel_name = 'hybrid_dilated_attn_rwkv7_gated_block'


def rmsnorm(x, g):
    xf = x.astype(jnp.float32)
    y = xf * lax.rsqrt(jnp.mean(xf * xf, axis=-1, keepdims=True) + NORM_EPS)
    return (y * g.astype(jnp.float32)).astype(x.dtype)


def rope(x, pos):
    half = HEAD_DIM // 2
    inv_freq = ROPE_THETA ** (-jnp.arange(0, HEAD_DIM, 2, dtype=jnp.float32) / HEAD_DIM)
    ang = pos[:, None] * inv_freq[None, :]
    ang = ang.reshape((pos.shape[0],) + (1,) * (x.ndim - 3) + (half,))
    c, s = jnp.cos(ang), jnp.sin(ang)
    xf = x.astype(jnp.float32)
    x1, x2 = xf[..., :half], xf[..., half:]
    return jnp.concatenate([x1 * c - x2 * s, x2 * c + x1 * s], axis=-1).astype(x.dtype)


def token_shift(z):
    return jnp.pad(z[:, :-1], ((0, 0), (1, 0), (0, 0)))


def dilated_attention_branch(z_attn):
    B, S, _ = z_attn.shape
    qkv = z_attn.reshape(B, S, 3, N_GROUPS, ATTN_HEADS_PER_GROUP, HEAD_DIM)
    pos = jnp.arange(S, dtype=jnp.float32)
    q = rope(qkv[:, :, 0], pos)
    k = rope(qkv[:, :, 1], pos)
    v = qkv[:, :, 2]
    qs = [q[:, :, g] for g in range(N_GROUPS)]
    ks = [k[:, :, g] for g in range(N_GROUPS)]
    vs = [v[:, :, g] for g in range(N_GROUPS)]
    scale = HEAD_DIM ** -0.5

    def block(bi):
        start = bi * Q_BLOCK
        t = start + jnp.arange(Q_BLOCK)
        outs, lses = [], []
        for g, (window, dil) in enumerate(ATTN_GROUPS):
            j = jnp.arange(window // dil + 1)
            idx = t[:, None] - dil * j[None, :]
            valid = idx >= 0
            idx_c = jnp.maximum(idx, 0)
            qb = lax.dynamic_slice_in_dim(qs[g], start, Q_BLOCK, axis=1)
            kg = jnp.take(ks[g], idx_c, axis=1)
            vg = jnp.take(vs[g], idx_c, axis=1)
            s = jnp.einsum('bqhd,bqjhd->bqhj', qb, kg, preferred_element_type=jnp.float32) * scale
            s = jnp.where(valid[None, :, None, :], s, -jnp.inf)
            m = jnp.max(s, axis=-1, keepdims=True)
            e = jnp.exp(s - m)
            den = jnp.sum(e, axis=-1)
            o = jnp.einsum('bqhj,bqjhd->bqhd', e, vg.astype(jnp.float32)) / den[..., None]
            outs.append(o)
            lses.append(m[..., 0] + jnp.log(den))
        alpha = jax.nn.softmax(jnp.stack(lses, axis=0), axis=0)
        o = alpha[0][..., None] * outs[0]
        for g in range(1, N_GROUPS):
            o = o + alpha[g][..., None] * outs[g]
        return o.astype(z_attn.dtype)

    out = lax.map(block, jnp.arange(S // Q_BLOCK))
    return out.transpose(1, 0, 2, 3, 4).reshape(B, S, ATTN_GROUP_WIDTH)


def rwkv7_time_mix(z, mu, w0, w2, a0, a2, g2, k_k, k_a, r_k, ln_w, ln_b):
    B, S, _ = z.shape
    f32 = jnp.float32
    z = z + (token_shift(z) - z) * mu
    r, k, v, xw, xa, xg = jnp.split(z, RWKV_SPLITS, axis=-1)
    logw = -jax.nn.softplus(-(w0 + jnp.tanh(xw) @ w2)) - 0.5
    decay = jnp.exp(-jnp.exp(logw.astype(f32)))
    a = jax.nn.sigmoid(a0 + xa @ a2)
    g = jax.nn.sigmoid(xg) @ g2
    heads = lambda t: t.reshape(B, S, RWKV_HEADS, HEAD_DIM).astype(f32)
    kk = heads(k * k_k)
    kk = kk / jnp.maximum(jnp.sqrt(jnp.sum(kk * kk, axis=-1, keepdims=True)), 1e-12)
    k = k * (1.0 + (a - 1.0) * k_a)
    r_h, k_h, v_h, a_h, w_h = heads(r), heads(k), heads(v), heads(a), heads(decay)
    seq = tuple(jnp.moveaxis(t, 1, 0) for t in (r_h, w_h, k_h, v_h, -kk, kk * a_h))

    def step(state, inp):
        r_t, w_t, k_t, v_t, a_t, b_t = inp
        sa = jnp.einsum('bhvk,bhk->bhv', state, a_t)
        state = (state * w_t[:, :, None, :] + sa[..., None] * b_t[:, :, None, :]
                 + v_t[..., None] * k_t[:, :, None, :])
        y = jnp.einsum('bhvk,bhk->bhv', state, r_t)
        return state, y

    state0 = jnp.zeros((B, RWKV_HEADS, HEAD_DIM, HEAD_DIM), f32)
    _, y = lax.scan(step, state0, seq)
    y = jnp.moveaxis(y, 0, 1)
    mean = jnp.mean(y, axis=-1, keepdims=True)
    var = jnp.mean(jnp.square(y - mean), axis=-1, keepdims=True)
    y = (y - mean) * lax.rsqrt(var + RWKV_GN_EPS)
    y = y.reshape(B, S, RWKV_WIDTH) * ln_w.astype(f32) + ln_b.astype(f32)
    bonus = jnp.sum(r_h * k_h * r_k.astype(f32), axis=-1, keepdims=True) * v_h
    y = (y + bonus.reshape(B, S, RWKV_WIDTH)) * g.astype(f32)
    return y.astype(z.dtype)


def setup_inputs(seed: int = 0) -> dict:
    key = jax.random.key(seed)
    ks = jax.random.split(key, 32)
    f32 = jnp.float32
    nrm = lambda k, shape, scale: jax.random.normal(k, shape, f32) * scale
    gain = lambda k, n: 1.0 + 0.02 * jax.random.normal(k, (DEPTH, n), f32)
    L = DEPTH
    return {
        'x': jax.random.normal(ks[0], (BATCH, SEQ, D_MODEL), f32),
        'p': jax.random.normal(ks[1], (DEPTH, BATCH, SEQ, PLE_DIM), f32),
        'mix_pre_norm': gain(ks[2], D_MODEL),
        'w_in': nrm(ks[3], (L, D_MODEL, IN_COLS), D_MODEL ** -0.5),
        'rwkv_mu': jax.random.uniform(ks[4], (L, RWKV_COLS), f32, 0.2, 0.8),
        'rwkv_w0': jax.random.uniform(ks[5], (L, RWKV_WIDTH), f32, -4.0, 0.0),
        'rwkv_w2': nrm(ks[6], (L, DECAY_LORA, RWKV_WIDTH), 0.5 * DECAY_LORA ** -0.5),
        'rwkv_a0': nrm(ks[7], (L, RWKV_WIDTH), 0.1),
        'rwkv_a2': nrm(ks[8], (L, ICLR_LORA, RWKV_WIDTH), 0.5 * ICLR_LORA ** -0.5),
        'rwkv_g2': nrm(ks[9], (L, GATE_LORA, RWKV_WIDTH), GATE_LORA ** -0.5),
        'rwkv_k_k': 0.85 + 0.02 * jax.random.normal(ks[10], (L, RWKV_WIDTH), f32),
        'rwkv_k_a': 1.0 + 0.02 * jax.random.normal(ks[11], (L, RWKV_WIDTH), f32),
        'rwkv_r_k': nrm(ks[12], (L, RWKV_HEADS, HEAD_DIM), 0.1),
        'rwkv_ln_w': gain(ks[13], RWKV_WIDTH),
        'rwkv_ln_b': nrm(ks[14], (L, RWKV_WIDTH), 0.02),
        'w_attn_up': nrm(ks[15], (L, ATTN_GROUP_WIDTH, D_MODEL), ATTN_GROUP_WIDTH ** -0.5),
        'w_rwkv_up': nrm(ks[16], (L, RWKV_WIDTH, D_MODEL), RWKV_WIDTH ** -0.5),
        'w_out': nrm(ks[17], (L, D_MODEL, D_MODEL), D_MODEL ** -0.5),
        'mix_post_norm': gain(ks[18], D_MODEL),
        'ffn_pre_norm': gain(ks[19], D_MODEL),
        'w_ff_in': nrm(ks[20], (L, D_MODEL, D_FF), D_MODEL ** -0.5),
        'w_ff_out': nrm(ks[21], (L, D_FF, D_MODEL), D_FF ** -0.5),
        'ffn_post_norm': gain(ks[22], D_MODEL),
        'w_ple': nrm(ks[23], (L, PLE_DIM, D_MODEL), PLE_DIM ** -0.5),
        'w_ple_gate': nrm(ks[24], (L, D_MODEL, D_MODEL), D_MODEL ** -0.5),
        'ple_post_norm': gain(ks[25], D_MODEL),
    }


def reference(x, p, mix_pre_norm, w_in, rwkv_mu, rwkv_w0, rwkv_w2, rwkv_a0, rwkv_a2, rwkv_g2,
              rwkv_k_k, rwkv_k_a, rwkv_r_k, rwkv_ln_w, rwkv_ln_b, w_attn_up, w_rwkv_up, w_out,
              mix_post_norm, ffn_pre_norm, w_ff_in, w_ff_out, ffn_post_norm, w_ple, w_ple_gate,
              ple_post_norm):
    h = x
    for i in range(DEPTH):
        u = rmsnorm(h, mix_pre_norm[i])
        z = u @ w_in[i]
        z_attn = z[..., :ATTN_QKV_COLS]
        z_rwkv = z[..., ATTN_QKV_COLS:ATTN_QKV_COLS + RWKV_COLS]
        z_gate = z[..., ATTN_QKV_COLS + RWKV_COLS:]
        o_attn = dilated_attention_branch(z_attn)
        o_rwkv = rwkv7_time_mix(z_rwkv, rwkv_mu[i], rwkv_w0[i], rwkv_w2[i], rwkv_a0[i],
                                rwkv_a2[i], rwkv_g2[i], rwkv_k_k[i], rwkv_k_a[i], rwkv_r_k[i],
                                rwkv_ln_w[i], rwkv_ln_b[i])
        gate_attn, gate_rwkv = jnp.split(jax.nn.sigmoid(z_gate), 2, axis=-1)
        merged = gate_attn * (o_attn @ w_attn_up[i]) + gate_rwkv * (o_rwkv @ w_rwkv_up[i])
        h = h + rmsnorm(merged @ w_out[i], mix_post_norm[i])
        f = rmsnorm(h, ffn_pre_norm[i])
        f = jnp.square(jax.nn.relu(f @ w_ff_in[i])) @ w_ff_out[i]
        h = h + rmsnorm(f, ffn_post_norm[i])
        e = (p[i] @ w_ple[i]) * jax.nn.sigmoid(h @ w_ple_gate[i])
        h = h + rmsnorm(e, ple_post_norm[i])
    return h
```

```python
import os
import numpy as np
import concourse.bass as bass
import concourse.mybir as mybir
from concourse.bass_utils import run_bass_kernel_spmd
from contextlib import ExitStack

F32 = mybir.dt.float32
F32R = mybir.dt.float32r
AF = mybir.ActivationFunctionType
ALU = mybir.AluOpType
AX = mybir.AxisListType
ENGS = ('sp', 'pe', 'act', 'dve', 'pool')
DEBUG = None


class Sched:
    def __init__(self, nc, es, n_dma=8):
        self.nc = nc
        self.sem = {e: es.enter_context(nc.semaphore('s_' + e)) for e in ('pe', 'act', 'dve', 'pool')}
        self.dsem = [es.enter_context(nc.semaphore('sd%d' % i)) for i in range(n_dma)]
        self.cnt = {e: 0 for e in self.sem}
        self.dcnt = [0] * n_dma
        self.dnext = 0
        self.streams = {e: [] for e in ENGS}
        self.waited = {e: {} for e in ENGS}
        self.lastw = {}
        self.readers = {}
        self.nops = 0

    def _semh(self, k):
        return self.dsem[k[1]] if isinstance(k, tuple) else self.sem[k]

    def _deps(self, reads, writes):
        deps = {}

        def add(k, v):
            if deps.get(k, 0) < v:
                deps[k] = v
        for r in reads:
            t = self.lastw.get(r)
            if t:
                add(*t)
        for w in writes:
            t = self.lastw.get(w)
            if t:
                add(*t)
            for k, v in self.readers.get(w, {}).items():
                add(k, v)
        return deps

    def _record(self, tok, reads, writes):
        for r in reads:
            d = self.readers.setdefault(r, {})
            if d.get(tok[0], 0) < tok[1]:
                d[tok[0]] = tok[1]
        for w in writes:
            self.lastw[w] = tok
            self.readers[w] = {}

    def _waits(self, eng, deps):
        out = []
        wd = self.waited[eng]
        for k, v in deps.items():
            if eng == 'pe' and k == 'pe':
                continue
            if wd.get(k, 0) >= v:
                continue
            wd[k] = v
            out.append((k, v))
        return out

    def begin_capture(self):
        self.cap = []

    def end_capture(self):
        c = self.cap
        self.cap = None
        return c

    def replay(self, items):
        for it in items:
            if it[0] == 'op':
                self.op(it[1], it[2], it[3], it[4])
            else:
                self.dma(it[1], it[2], it[3], it[4], it[5], **it[6])

    def op(self, eng, fn, reads=(), writes=()):
        if getattr(self, 'cap', None) is not None:
            self.cap.append(('op', eng, fn, tuple(reads), tuple(writes)))
            return None
        ex = [r for r in reads if isinstance(r, str) and r[:2] in ('cb', 'db', 'pt', 'pz', 'ps', 'pk', 'pq') and r not in writes]
        if ex:
            writes = list(writes) + ex
        deps = self._deps(reads, writes)
        waits = self._waits(eng, deps)
        self.cnt[eng] += 1
        tok = (eng, self.cnt[eng])
        self.streams[eng].append((waits, fn, (eng, 1)))
        self._record(tok, reads, writes)
        self.nops += 1
        return tok

    def dma(self, out, in_, reads=(), writes=(), q='sp', **kw):
        if getattr(self, 'cap', None) is not None:
            self.cap.append(('dma', out, in_, tuple(reads), tuple(writes), q, kw))
            return None
        slot = self.dnext
        self.dnext = (self.dnext + 1) % len(self.dsem)
        deps = self._deps(reads, writes)
        k = ('d', slot)
        if self.dcnt[slot] > 0 and deps.get(k, 0) < self.dcnt[slot]:
            deps[k] = self.dcnt[slot]
        waits = self._waits(q, deps)
        self.dcnt[slot] += 16
        tok = (k, self.dcnt[slot])
        self.streams[q].append((waits, lambda e: e.dma_start(out=out, in_=in_, **kw), (k, 16)))
        self._record(tok, reads, writes)
        self.nops += 1
        return tok

    def barrier(self):
        toks = {e: c for e, c in self.cnt.items() if c > 0}
        for i, c in enumerate(self.dcnt):
            if c > 0:
                toks[('d', i)] = c
        for e in ENGS:
            waits = self._waits(e, {k: v for k, v in toks.items() if k != e})
            if waits:
                self.streams[e].append((waits, None, None))
        self.lastw = {}
        self.readers = {}

    def finish(self):
        toks = {}
        for i, c in enumerate(self.dcnt):
            if c > 0:
                toks[('d', i)] = c
        waits = self._waits('sp', toks)
        self.streams['sp'].append((waits, None, None))

    def flush(self):
        nc = self.nc
        with nc.Block() as block:
            for ename, deco in (('sp', block.sync), ('pe', block.tensor), ('act', block.scalar),
                                ('dve', block.vector), ('pool', block.gpsimd)):
                stream = self.streams[ename]

                def body(eng, stream=stream):
                    for waits, fn, inc in stream:
                        for (k, v) in waits:
                            eng.wait_ge(self._semh(k), v)
                        if fn is not None:
                            ins = fn(eng)
                            if inc is not None:
                                ins.then_inc(self._semh(inc[0]), inc[1])
                deco(body)
        self.streams = {e: [] for e in ENGS}


NL, NPRE, NO, KV0, NK = 8192, 4096, 4096, 2048, 6144
EPS = 1e-6
GN_EPS = 64e-5
WCOLS = 6048


class K:
    def __init__(self, nc, es):
        self.nc, self.es = nc, es
        self.S = Sched(nc, es)
        self.uid = 0

    def act(self, out, in_, func, r, w, **kw):
        self.S.op('act', lambda e: e.activation(out=out, in_=in_, func=func, **kw), r, w)

    def tt(self, eng, out, in0, in1, op, r, w):
        self.S.op(eng, lambda e: e.tensor_tensor(out=out, in0=in0, in1=in1, op=op), r, w)

    def ts(self, eng, out, in0, s1, s2, op0, op1, r, w):
        if s2 is None:
            self.S.op(eng, lambda e: e.tensor_scalar(out=out, in0=in0, scalar1=s1, scalar2=None, op0=op0), r, w)
        else:
            self.S.op(eng, lambda e: e.tensor_scalar(out=out, in0=in0, scalar1=s1, scalar2=s2, op0=op0, op1=op1), r, w)

    def stt(self, out, in0, scalar, in1, op0, op1, r, w):
        self.S.op('dve', lambda e: e.scalar_tensor_tensor(out=out, in0=in0, scalar=scalar, in1=in1, op0=op0, op1=op1), r, w)

    def mm(self, out, lhsT, rhs, start, stop, r, w, fast=False):
        if fast:
            lhsT = lhsT.bitcast(F32R)
            rhs = rhs.bitcast(F32R)
        self.S.op('pe', lambda e: e.matmul(out, lhsT=lhsT, rhs=rhs, start=start, stop=stop, skip_group_check=True), r, w)

    def tr(self, out, in_, r, w):
        ident = self.ident
        self.S.op('pe', lambda e: e.transpose(out=out, in_=in_, identity=ident[:]), list(r) + ['ident'], w)

    def dma(self, out, in_, r, w):
        self.S.dma(out, in_, r, w)

    def rms_rstd(self, src, junk, ss, rstd, n, eps, r, names):
        jn, sn, rn = names
        self.act(junk, src, AF.Square, r, [jn, sn], accum_out=ss)
        self.ts('dve', rstd, ss, 1.0 / n, eps, ALU.mult, ALU.add, [sn], [rn])
        self.act(rstd, rstd, AF.Sqrt, [rn], [rn])
        self.S.op('dve', lambda e: e.reciprocal(out=rstd, in_=rstd), [rn], [rn])


def build_program():
    nc = bass.Bass("TRN2", target_bir_lowering=False)
    D = {}

    def din(name, shape):
        D[name] = nc.dram_tensor(name, list(shape), F32, kind="ExternalInput").ap()

    def dscr(name, shape):
        kind = "ExternalOutput" if DEBUG else "Internal"
        D[name] = nc.dram_tensor(name, list(shape), F32, kind=kind).ap()

    din("xloc", [NL, 1024]); din("pown", [NO, 256])
    din("w_in", [1024, WCOLS]); din("w_au", [256, 1024]); din("w_ru", [512, 1024]); din("w_out", [1024, 1024])
    din("w_ffi", [1024, 4096]); din("w_ffo", [4096, 1024]); din("w_ple", [256, 1024]); din("w_pg", [1024, 1024])
    din("gpre", [128, 8]); din("gffn", [128, 8])
    din("g_mpost", [1, 1024]); din("g_fpost", [1, 1024]); din("g_ppost", [1, 1024])
    din("mu", [1, 1696])
    for nm in ("w0", "a0", "k_k", "k_a", "ln_w", "ln_b", "r_k"):
        din(nm, [1, 512])
    din("w2pad", [64, 512]); din("a2pad", [64, 512]); din("g2", [96, 512])
    din("cosT", [NK, 384]); din("sinT", [NK, 384])
    din("ident", [128, 128]); din("maskA4", [128, 1024]); din("maskR", [128, 512]); din("tri", [128, 128])
    din("blockmask4", [128, 512]); din("onesPad", [128, 256]); din("onesPre", [128, 256])
    D["out"] = nc.dram_tensor("out", [NO, 1024], F32, kind="ExternalOutput").ap()
    dscr("S_q", [NO, 768]); dscr("S_k", [NK, 768]); dscr("S_v", [NK, 768])
    dscr("S_r", [NL + 1, 1696]); dscr("S_g", [NO, 2048])
    dscr("S_oaT", [256, NO]); dscr("S_orT", [512, NO])
    if DEBUG:
        dscr("dbgC", [24, 128, 512])

    with ExitStack() as es0:
        k = K(nc, es0)
        S = k.S
        ident = es0.enter_context(nc.sbuf_tensor("ident_sb", [128, 128], F32))
        k.ident = ident
        S.dma(ident[:], D["ident"][:, :], writes=['ident'])
        phases = DEBUG or "ABCD"
        if "A" in phases:
            phase_a(nc, k, D)
            S.barrier(); S.flush()
        if "B" in phases:
            phase_b(nc, k, D)
            S.barrier(); S.flush()
        if "C" in phases:
            phase_c(nc, k, D)
            S.barrier(); S.flush()
        if "D" in phases:
            phase_d(nc, k, D)
        S.barrier()
        S.finish()
        S.flush()
    return nc


COLBLOCKS = {
    'Q0': (0, 512, 'Q', 0), 'Q1': (512, 256, 'Q', 512),
    'K0': (768, 512, 'K', 0), 'K1': (1280, 256, 'K', 512),
    'V0': (1536, 512, 'V', 0), 'V1': (2048, 256, 'V', 512),
    'R0': (2304, 512, 'R', 0), 'R1': (2816, 512, 'R', 512), 'R2': (3328, 512, 'R', 1024), 'R3': (3840, 160, 'R', 1536),
    'G0': (4000, 512, 'G', 0), 'G1': (4512, 512, 'G', 512), 'G2': (5024, 512, 'G', 1024), 'G3': (5536, 512, 'G', 1536),
}


def phase_a(nc, k, D):
    S = k.S
    with ExitStack() as es:
        T = lambda name, shape: es.enter_context(nc.sbuf_tensor(name, shape, F32))
        P = lambda name: es.enter_context(nc.psum_tensor(name, [128, 512], F32))
        xt = [T("a_xt%d" % i, [128, 1024]) for i in range(2)]
        junk = T("a_junk", [128, 1024]); xs = T("a_xs", [128, 1024])
        ss = T("a_ss", [128, 1]); rstd = T("a_rstd", [128, 1])
        uT = [T("a_uT%d" % i, [128, 8, 1024]) for i in range(2)]
        wb = [T("a_wb%d" % i, [128, 8, 512]) for i in range(2)]
        wraw = [T("a_wraw%d" % i, [128, 8, 512]) for i in range(2)]
        raw = [T("a_raw%d" % i, [128, 512]) for i in range(2)]
        stg = [T("a_stg%d" % i, [128, 512]) for i in range(3)]
        cblk = T("a_cblk", [128, 8, 384]); sblk = T("a_sblk", [128, 8, 384])
        tmp = [T("a_tmp%d" % i, [128, 256]) for i in range(4)]
        gpre = T("a_gpre", [128, 8]); zrow = T("a_zrow", [1, 1696])
        pt = [P("a_pt%d" % i) for i in range(2)]
        pz = [P("a_pz%d" % i) for i in range(3)]
        k.dma(gpre[:], D["gpre"][:, :], [], ['gpre'])
        S.op('pool', lambda e: e.memset(zrow[:], 0.0), [], ['zrow'])
        k.dma(D["S_r"][0:1, :], zrow[:], ['zrow'], ['S_r0'])

        items = []
        for blk in range(8):
            own = blk >= 4
            kv = blk >= 2
            cbs = []
            if own:
                cbs += ['Q0', 'Q1']
            if kv:
                cbs += ['K0', 'K1', 'V0', 'V1']
            cbs += ['R0', 'R1', 'R2', 'R3']
            if own:
                cbs += ['G0', 'G1', 'G2', 'G3']
            for i, cb in enumerate(cbs):
                items.append((blk, cb, i == 0))

        def load_w(idx):
            blk, cb, _ = items[idx]
            c0, ncols, _, _ = COLBLOCKS[cb]
            s = idx % 2
            k.dma(wraw[s][:, :, 0:ncols], D["w_in"][:, c0:c0 + ncols].rearrange("(c p) n -> p c n", p=128), [], ['wraw%d' % s])
            k.act(wb[s][:, :, 0:ncols].bitcast(F32R), wraw[s][:, :, 0:ncols], AF.Copy, ['wraw%d' % s], ['wb%d' % s])

        load_w(0)
        pzi = 0; rawi = 0; stgi = 0
        for idx, (blk, cb, first) in enumerate(items):
            ub = uT[blk % 2]; ubn = 'uT%d' % (blk % 2)
            own = blk >= 4; kv = blk >= 2
            if first:
                for j in range(8):
                    row0 = blk * 1024 + j * 128
                    x_ = xt[j % 2]; xn = 'xt%d' % (j % 2)
                    k.dma(x_[:], D["xloc"][row0:row0 + 128, :], [], [xn])
                    k.rms_rstd(x_[:], junk[:], ss[:], rstd[:], 1024, EPS, [xn], ('junk', 'ss', 'rstd'))
                    k.ts('dve', xs[:], x_[:], rstd[:, 0:1], None, ALU.mult, None, [xn, 'rstd'], ['xs'])
                    for c in range(8):
                        k.tr(pt[c // 4][:, (c % 4) * 128:(c % 4 + 1) * 128], xs[:, c * 128:(c + 1) * 128], ['xs'], ['pt%d' % (c // 4)])
                    for hf in range(2):
                        k.tt('dve', ub[:, hf * 4:(hf + 1) * 4, j * 128:(j + 1) * 128].bitcast(F32R),
                             pt[hf][:].rearrange("p (c t) -> p c t", c=4),
                             gpre[:, hf * 4:(hf + 1) * 4].unsqueeze(2).to_broadcast([128, 4, 128]), ALU.mult,
                             ['pt%d' % hf, 'gpre'], [ubn + '_%d' % j])
                if kv:
                    r0 = (blk - 2) * 1024
                    k.dma(cblk[:], D["cosT"][r0:r0 + 1024, :].rearrange("(j p) n -> p j n", p=128), [], ['cblk'])
                    k.dma(sblk[:], D["sinT"][r0:r0 + 1024, :].rearrange("(j p) n -> p j n", p=128), [], ['sblk'])
            if idx + 1 < len(items):
                load_w(idx + 1)
            c0, ncols, kind, dcol = COLBLOCKS[cb]
            ws = idx % 2
            for j in range(8):
                pzn = pzi % 3; pzi += 1
                for c in range(8):
                    k.mm(pz[pzn][:, 0:ncols], ub[:, c, j * 128:(j + 1) * 128], wb[ws][:, c, 0:ncols], c == 0, c == 7,
                         [ubn + '_%d' % j, 'wb%d' % ws], ['pz%d' % pzn], fast=True)
                si = stgi % 3; stgi += 1
                sg = stg[si]; sgn = 'stg%d' % si
                lrow = blk * 1024 + j * 128
                if kind in ('Q', 'K'):
                    ri = rawi % 2; rawi += 1
                    rw = raw[ri]; rwn = 'raw%d' % ri
                    k.act(rw[:, 0:ncols], pz[pzn][:, 0:ncols], AF.Copy, ['pz%d' % pzn], [rwn])
                    nh = ncols // 64
                    r3 = rw[:, 0:ncols].rearrange("p (h two f) -> p h two f", two=2, f=32)
                    s3 = sg[:, 0:ncols].rearrange("p (h two f) -> p h two f", two=2, f=32)
                    x1, x2 = r3[:, :, 0, :], r3[:, :, 1, :]
                    c3 = cblk[:, j, 0:nh * 32].rearrange("p (h f) -> p h f", f=32)
                    n3 = sblk[:, j, 0:nh * 32].rearrange("p (h f) -> p h f", f=32)
                    tv = [t[:, 0:nh * 32].rearrange("p (h f) -> p h f", f=32) for t in tmp]
                    k.tt('dve', tv[0], x1, c3, ALU.mult, [rwn, 'cblk'], ['tmp0'])
                    k.tt('pool', tv[1], x2, n3, ALU.mult, [rwn, 'sblk'], ['tmp1'])
                    k.tt('dve', s3[:, :, 0, :], tv[0], tv[1], ALU.subtract, ['tmp0', 'tmp1'], [sgn])
                    k.tt('pool', tv[2], x2, c3, ALU.mult, [rwn, 'cblk'], ['tmp2'])
                    k.tt('dve', tv[3], x1, n3, ALU.mult, [rwn, 'sblk'], ['tmp3'])
                    k.tt('pool', s3[:, :, 1, :], tv[2], tv[3], ALU.add, ['tmp2', 'tmp3'], [sgn])
                    if kind == 'Q':
                        o0 = lrow - NPRE
                        k.dma(D["S_q"][o0:o0 + 128, dcol:dcol + ncols], sg[:, 0:ncols], [sgn], ['S_q'])
                    else:
                        o0 = lrow - KV0
                        k.dma(D["S_k"][o0:o0 + 128, dcol:dcol + ncols], sg[:, 0:ncols], [sgn], ['S_k'])
                elif kind == 'V':
                    k.act(sg[:, 0:ncols], pz[pzn][:, 0:ncols], AF.Copy, ['pz%d' % pzn], [sgn])
                    o0 = lrow - KV0
                    k.dma(D["S_v"][o0:o0 + 128, dcol:dcol + ncols], sg[:, 0:ncols], [sgn], ['S_v'])
                elif kind == 'R':
                    k.act(sg[:, 0:ncols], pz[pzn][:, 0:ncols], AF.Copy, ['pz%d' % pzn], [sgn])
                    k.dma(D["S_r"][1 + lrow:1 + lrow + 128, dcol:dcol + ncols], sg[:, 0:ncols], [sgn], ['S_r'])
                else:
                    k.act(sg[:, 0:ncols], pz[pzn][:, 0:ncols], AF.Sigmoid, ['pz%d' % pzn], [sgn])
                    o0 = lrow - NPRE
                    k.dma(D["S_g"][o0:o0 + 128, dcol:dcol + ncols], sg[:, 0:ncols], [sgn], ['S_g'])


def phase_b(nc, k, D):
    S = k.S
    with ExitStack() as es:
        T = lambda name, shape: es.enter_context(nc.sbuf_tensor(name, shape, F32))
        P = lambda name: es.enter_context(nc.psum_tensor(name, [128, 512], F32))
        acc = T("b_acc", [128, 4, NO])
        kt = [T("b_kt%d" % i, [128, 256]) for i in range(3)]
        vpad = [T("b_vpad%d" % i, [128, 4, 128]) for i in range(3)]
        ktp = [T("b_ktp%d" % i, [128, 4, 128]) for i in range(3)]
        qt = [T("b_qt%d" % i, [128, 256]) for i in range(2)]
        qT = [T("b_qT%d" % i, [128, 2, 128]) for i in range(2)]
        pexp = [T("b_pexp%d" % i, [128, 4, 256]) for i in range(2)]
        pm = [T("b_pm%d" % i, [128, 4, 256]) for i in range(2)]
        maskA = T("b_maskA", [128, 4, 256]); onesPad = T("b_onesPad", [128, 2, 128]); onesPre = T("b_onesPre", [128, 2, 128])
        ptr = P("b_ptr"); ptrq = P("b_ptrq")
        pss = [[P("b_ps%d_%d" % (i, j)) for j in range(2)] for i in range(2)]
        pso = [P("b_pso%d" % i) for i in range(2)]
        k.dma(maskA[:], D["maskA4"].rearrange("p (h n) -> p h n", h=4), [], ['maskA'])
        k.dma(onesPad[:], D["onesPad"].rearrange("p (e n) -> p e n", e=2), [], ['onesPad'])
        k.dma(onesPre[:], D["onesPre"].rearrange("p (e n) -> p e n", e=2), [], ['onesPre'])
        for i in range(3):
            S.op('pool', lambda e, i=i: e.memset(vpad[i][:], 0.0), [], ['vpad%d' % i])
            S.op('pool', lambda e, i=i: e.memset(ktp[i][:], 0.0), [], ['ktp%d' % i])
        ptrv = ptr[:].rearrange("p (c t) -> p c t", c=4)
        ptrqv = ptrq[:].rearrange("p (c t) -> p c t", c=4)
        iters = []
        it = 0
        for g, d in enumerate((1, 4, 16)):
            nsub = 32 // d
            for r in range(d):
                for n in range(-1, nsub):
                    qs = None
                    if n >= 0:
                        qs = it % 2; it += 1
                    iters.append((g, d, r, n, qs))

        def front(i):
            g, d, r, n, qs = iters[i]
            slot = (n + 1) % 3
            start = KV0 + n * 128 * d + r
            rows = slice(start, start + 127 * d + 1, d)
            k.dma(kt[slot][:], D["S_k"][rows, g * 256:(g + 1) * 256], ['S_k'], ['kt%d' % slot])
            vsrc = D["S_v"][rows, g * 256:(g + 1) * 256].rearrange("k (p e d) -> k p e d", p=2, e=2)
            vdst = vpad[slot][:].rearrange("k (p e) (f d) -> k p e f d", e=2, f=2)
            for e_ in range(2):
                k.dma(vdst[:, :, e_, e_, :], vsrc[:, :, e_, :], ['S_v'], ['vpad%d' % slot])
            for p in range(2):
                k.tr(ptrv[:, p, :], kt[slot][:, p * 128:(p + 1) * 128], ['kt%d' % slot], ['pkb'])
            kdst = ktp[slot][:].rearrange("k (p e) s -> k p e s", e=2)
            k.act(kdst[0:64, :, 0, :], ptrv[0:64, 0:2, :], AF.Copy, ['pkb'], ['ktp%d' % slot])
            k.S.op('dve', lambda e, kdst=kdst: e.tensor_copy(out=kdst[64:128, :, 1, :], in_=ptrv[64:128, 0:2, :]), ['pkb'], ['ktp%d' % slot])
            if n < 0:
                return
            q0 = n * 128 * d + r
            toks = slice(q0, q0 + 127 * d + 1, d)
            k.dma(qt[qs][:], D["S_q"][toks, g * 256:(g + 1) * 256], ['S_q'], ['qt%d' % qs])
            for p in range(2):
                k.tr(ptrqv[:, p, :], qt[qs][:, p * 128:(p + 1) * 128], ['qt%d' % qs], ['pqb'])
            k.S.op('dve', lambda e, qs=qs: e.tensor_copy(out=qT[qs][:], in_=ptrqv[:, 0:2, :]), ['pqb'], ['qT%d' % qs])
            prev, cur = n % 3, slot
            for h in range(4):
                p = h // 2
                bank = pss[qs][h // 2]; bn = 'pss%d_%d' % (qs, h // 2)
                bv = bank[:].rearrange("k (h n) -> k h n", h=2)
                k.mm(bv[:, h % 2, 0:128], ktp[prev][:, h, :], qT[qs][:, p, :], True, True, ['ktp%d' % prev, 'qT%d' % qs], [bn])
                k.mm(bv[:, h % 2, 128:256], ktp[cur][:, h, :], qT[qs][:, p, :], True, True, ['ktp%d' % cur, 'qT%d' % qs], [bn])

        def back(i):
            g, d, r, n, qs = iters[i]
            if n < 0:
                return
            slot = (n + 1) % 3
            prev, cur = n % 3, slot
            q0 = n * 128 * d + r
            toks = slice(q0, q0 + 127 * d + 1, d)
            for hb in range(2):
                k.act(pexp[qs][:, 2 * hb:2 * hb + 2, :], pss[qs][hb][:].rearrange("k (h n) -> k h n", h=2), AF.Exp,
                      ['pss%d_%d' % (qs, hb)], ['pexp%d' % qs], scale=0.125)
            k.tt('pool', pm[qs][:], pexp[qs][:], maskA[:], ALU.mult, ['pexp%d' % qs, 'maskA'], ['pm%d' % qs])
            po = pso[qs]; pon = 'pso%d' % qs
            pov = po[:].rearrange("k (c t) -> k c t", c=4)
            first = True
            nmm = 0
            for p in range(2):
                for e_ in range(2):
                    h = 2 * p + e_
                    for which, sl in ((0, prev), (1, cur)):
                        ones = onesPre if (which == 0 and n == 0) else onesPad
                        rhs = pm[qs][:, h, which * 128:(which + 1) * 128]
                        nmm += 2
                        k.mm(pov[:, p, :], vpad[sl][:, h, :], rhs, first, False, ['vpad%d' % sl, 'pm%d' % qs], [pon])
                        first = False
                        k.mm(pov[:, 2 + p, :], ones[:, e_, :], rhs, False, nmm == 16, ['onesPad', 'onesPre', 'pm%d' % qs], [pon])
            if g == 0:
                k.act(acc[:, :, toks], pov, AF.Copy, [pon], ['acc'])
            else:
                k.tt('dve', acc[:, :, toks], acc[:, :, toks], pov, ALU.add, [pon, 'acc'], ['acc'])

        def capture(fn, i):
            S.begin_capture()
            fn(i)
            return S.end_capture()

        def merge(lb, la):
            out = []
            ia = 0
            nb, na = len(lb), len(la)
            for ib, item in enumerate(lb):
                out.append(item)
                tgt = ((ib + 1) * na) // max(nb, 1)
                while ia < tgt:
                    out.append(la[ia]); ia += 1
            out.extend(la[ia:])
            return out

        S.replay(capture(front, 0))
        for i in range(len(iters)):
            lb = capture(back, i)
            la = capture(front, i + 1) if i + 1 < len(iters) else []
            S.replay(merge(lb, la))
        for q4 in range(4):
            sl = slice(q4 * 1024, (q4 + 1) * 1024)
            k.S.op('dve', lambda e, sl=sl: e.reciprocal(out=acc[:, 2:4, sl], in_=acc[:, 2:4, sl]), ['acc'], ['acc'])
            k.tt('dve', acc[:, 0:2, sl], acc[:, 0:2, sl], acc[:, 2:4, sl], ALU.mult, ['acc'], ['acc'])
        k.dma(D["S_oaT"].rearrange("(p k) t -> k p t", p=2), acc[:, 0:2, :], ['acc'], ['S_oaT'])


def phase_c(nc, k, D):
    S = k.S
    NCH = int(os.environ.get('C_NCH', NL // 128))
    STG = int(os.environ.get('C_STAGE', 99))
    SUB = int(os.environ.get('C_SUB', 99))
    with ExitStack() as es:
        T = lambda name, shape: es.enter_context(nc.sbuf_tensor(name, shape, F32))
        P = lambda name: es.enter_context(nc.psum_tensor(name, [128, 512], F32))
        B = [P("c_b%d" % i) for i in range(8)]
        bn = ['cb%d' % i for i in range(8)]
        zc = [T("c_zc%d" % i, [128, 1696]) for i in range(2)]
        zp = T("c_zp", [128, 1696])
        bc = {}
        mu = T("c_mu", [128, 1696])
        k.dma(mu[:], D["mu"].partition_broadcast(128), [], ['cst'])
        for nm in ("w0", "a0", "k_k", "k_a", "ln_w", "ln_b", "r_k"):
            bc[nm] = T("c_" + nm, [128, 512])
            k.dma(bc[nm][:], D[nm].partition_broadcast(128), [], ['cst'])
        w2pad = T("c_w2pad", [64, 512]); a2pad = T("c_a2pad", [64, 512]); g2 = T("c_g2", [96, 512])
        for dst_, nm_, np_ in ((w2pad, "w2pad", 64), (a2pad, "a2pad", 64), (g2, "g2", 96)):
            k.dma(zp[0:np_, 0:512], D[nm_][:, :], [], ['zp'])
            k.act(dst_[:].bitcast(F32R), zp[0:np_, 0:512], AF.Copy, ['zp'], ['cst'])
        maskR = T("c_maskR", [128, 512]); tri = T("c_tri", [128, 128]); bm4 = T("c_bm4", [128, 4, 128]); onec = T("c_onec", [128, 1])
        k.dma(maskR[:], D["maskR"][:, :], [], ['cst']); k.dma(tri[:], D["tri"][:, :], [], ['cst'])
        k.dma(bm4[:], D["blockmask4"].rearrange("p (c n) -> p c n", c=4), [], ['cst'])
        S.op('pool', lambda e: e.memset(onec[:], 1.0), [], ['onec'])
        lor = T("c_lor", [128, 256]); lorT1 = T("c_lorT1", [64, 128]); lorT2 = T("c_lorT2", [96, 128])
        names = ["dws", "lw", "aa", "kk", "kkn", "k2", "bv", "rkr", "ecum", "einv", "eex", "rt", "ysb", "yc"]
        t = {nm: T("c_" + nm, [128, 512]) for nm in names}
        atd = [T("c_at%d" % i, [128, 512]) for i in range(2)]
        bonus2 = [T("c_bonus%d" % i, [128, 512]) for i in range(2)]
        gsb2 = [T("c_gsb%d" % i, [128, 512]) for i in range(2)]
        btd = [T("c_btd%d" % i, [128, 512]) for i in range(2)]
        ktd = [T("c_ktd%d" % i, [128, 512]) for i in range(2)]
        sm = {nm: T("c_" + nm, [128, 8]) for nm in ("ss8", "nrm", "rs8", "s1", "s2")}
        art = [T("c_art%d" % i, [128, 4, 2, 128]) for i in range(2)]
        btTp = T("c_btTp", [128, 8, 128]); ktTp = T("c_ktTp", [128, 8, 128])
        A1 = T("c_A1", [128, 8, 2, 128])
        A2 = [T("c_A2%d" % i, [128, 8, 2, 128]) for i in range(2)]
        Lb = [T("c_L%d" % i, [128, 8, 128]) for i in range(2)]
        LTb = [T("c_LT%d" % i, [128, 8, 128]) for i in range(2)]
        Mb = [T("c_M%d" % i, [128, 8, 128]) for i in range(2)]
        WT = [T("c_WT%d" % i, [128, 4, 128]) for i in range(2)]
        X0 = [T("c_X0%d" % i, [128, 512]) for i in range(2)]
        U0 = [T("c_U0%d" % i, [128, 512]) for i in range(2)]
        U = [T("c_U%d" % i, [128, 512]) for i in range(2)]
        Tb = [T("c_T%d" % i, [128, 4, 128]) for i in range(2)]
        maskP = [T("c_maskP%d" % i, [128, 4, 128]) for i in range(2)]
        pc = T("c_pc", [128, 4])
        orT = T("c_orT", [128, 4, 128])
        S.op('pool', lambda e: e.memset(Tb[0][:], 0.0), [], ['T0'])
        S.op('pool', lambda e: e.memset(lor[:], 0.0), [], ['lor'])
        S.op('pool', lambda e: e.memset(btTp[:], 0.0), [], ['btTp'])
        S.op('pool', lambda e: e.memset(ktTp[:], 0.0), [], ['ktTp'])
        S.op('dve', lambda e: e.tensor_copy(out=btTp[:].bitcast(F32R), in_=btTp[:]), ['btTp'], ['btTp'])
        S.op('dve', lambda e: e.tensor_copy(out=ktTp[:].bitcast(F32R), in_=ktTp[:]), ['ktTp'], ['ktTp'])
        h8 = lambda ap: ap.rearrange("p (h d) -> p h d", h=8)
        bc8 = lambda ap: ap.unsqueeze(2).to_broadcast([128, 8, 64])
        c4 = lambda ap: ap.rearrange("p (c t) -> p c t", c=4)

        def stage_a(c):
            s = c % 2
            own = c >= NCH // 2
            t0 = c * 128
            z = zc[s]; zn = 'zc%d' % s
            k.dma(z[:], D["S_r"][1 + t0:1 + t0 + 128, :], ['S_r', 'S_r0'], [zn])
            k.dma(zp[:], D["S_r"][t0:t0 + 128, :], ['S_r', 'S_r0'], ['zp'])
            k.tt('pool', zp[:], zp[:], z[:], ALU.subtract, ['zp', zn], ['zp'])
            k.tt('pool', zp[:], zp[:], mu[:], ALU.mult, ['zp', 'cst'], ['zp'])
            k.tt('dve', z[:], z[:], zp[:], ALU.add, ['zp', zn], [zn])
            r_, k_, v_ = z[:, 0:512], z[:, 512:1024], z[:, 1024:1536]
            k.act(lor[:, 0:32], z[:, 1536:1568], AF.Tanh, [zn], ['lor'])
            k.act(lor[:, 32:64], z[:, 1568:1600], AF.Copy, [zn], ['lor'])
            k.act(lor[:, 128:224], z[:, 1600:1696], AF.Sigmoid, [zn], ['lor'])
            k.tr(B[0][:, 0:128], lor[:, 0:128], ['lor'], [bn[0]])
            k.tr(B[0][:, 128:256], lor[:, 128:256], ['lor'], [bn[0]])
            k.act(lorT1[:].bitcast(F32R), B[0][0:64, 0:128], AF.Copy, [bn[0]], ['lorT1'])
            k.S.op('dve', lambda e: e.tensor_copy(out=lorT2[:].bitcast(F32R), in_=B[0][0:96, 128:256]), [bn[0]], ['lorT2'])
            k.mm(B[1][:], lorT1[:], w2pad[:], True, True, ['lorT1', 'cst'], [bn[1]], fast=True)
            k.mm(B[2][:], lorT1[:], a2pad[:], True, True, ['lorT1', 'cst'], [bn[2]], fast=True)
            gs = gsb2[s]; gsn = 'gsb%d' % s
            if own:
                k.mm(B[0][:], lorT2[:], g2[:], True, True, ['lorT2', 'cst'], [bn[0]], fast=True)
                k.act(gs[:], B[0][:], AF.Copy, [bn[0]], [gsn])
            k.tt('dve', t["dws"][:], B[1][:], bc["w0"][:], ALU.add, [bn[1], 'cst'], ['dws'])
            k.act(t["lw"][:], t["dws"][:], AF.Sigmoid, ['dws'], ['lw'])
            k.S.op('act', lambda e: e.mul(out=t["lw"][:], in_=t["lw"][:], mul=-float(np.exp(-0.5))), ['lw'], ['lw'])
            k.tt('dve', t["dws"][:], B[2][:], bc["a0"][:], ALU.add, [bn[2], 'cst', 'lw'], ['dws'])
            k.act(t["aa"][:], t["dws"][:], AF.Sigmoid, ['dws'], ['aa'])
            k.tt('pool', t["kk"][:], k_, bc["k_k"][:], ALU.mult, [zn, 'cst'], ['kk'])
            k.tt('pool', t["rkr"][:], t["kk"][:], t["kk"][:], ALU.mult, ['kk'], ['rkr'])
            k.S.op('dve', lambda e: e.tensor_reduce(out=sm["ss8"][:], in_=h8(t["rkr"][:]), axis=AX.X, op=ALU.add), ['rkr'], ['ss8'])
            k.act(sm["nrm"][:], sm["ss8"][:], AF.Sqrt, ['ss8'], ['nrm'])
            k.ts('dve', sm["nrm"][:], sm["nrm"][:], 1e-12, None, ALU.max, None, ['nrm'], ['nrm'])
            k.S.op('dve', lambda e: e.reciprocal(out=sm["nrm"][:], in_=sm["nrm"][:]), ['nrm'], ['nrm'])
            k.tt('dve', h8(t["kkn"][:]), h8(t["kk"][:]), bc8(sm["nrm"][:]), ALU.mult, ['kk', 'nrm'], ['kkn'])
            k.stt(t["k2"][:], t["aa"][:], -1.0, bc["k_a"][:], ALU.add, ALU.mult, ['aa', 'cst'], ['k2'])
            k.stt(t["k2"][:], t["k2"][:], 1.0, k_, ALU.add, ALU.mult, ['k2', zn], ['k2'])
            k.tt('pool', t["bv"][:], t["kkn"][:], t["aa"][:], ALU.mult, ['kkn', 'aa'], ['bv'])
            bo = bonus2[s]; bon = 'bonus%d' % s
            if own:
                k.tt('pool', t["rkr"][:], r_, t["k2"][:], ALU.mult, [zn, 'k2', 'ss8'], ['rkr'])
                k.tt('pool', t["rkr"][:], t["rkr"][:], bc["r_k"][:], ALU.mult, ['rkr', 'cst'], ['rkr'])
                k.S.op('dve', lambda e: e.tensor_reduce(out=sm["rs8"][:], in_=h8(t["rkr"][:]), axis=AX.X, op=ALU.add), ['rkr'], ['rs8'])
                k.tt('dve', h8(bo[:]), h8(v_), bc8(sm["rs8"][:]), ALU.mult, [zn, 'rs8'], [bon])
            k.mm(B[1][:], tri[:], t["lw"][:], True, True, ['cst', 'lw'], [bn[1]])
            k.act(t["ecum"][:], B[1][:], AF.Exp, [bn[1]], ['ecum'])
            k.act(t["einv"][:], B[1][:], AF.Exp, [bn[1]], ['einv'], scale=-1.0)
            k.tt('dve', t["dws"][:], B[1][:], t["lw"][:], ALU.subtract, [bn[1], 'lw', 'aa'], ['dws'])
            k.act(t["eex"][:], t["dws"][:], AF.Exp, ['dws'], ['eex'])
            for p in range(4):
                k.mm(B[0][:, 256 + p:257 + p], t["lw"][:, p * 128:(p + 1) * 128], onec[:], True, True, ['lw', 'onec'], [bn[0]])
            k.act(pc[:], B[0][:, 256:260], AF.Exp, [bn[0]], ['pc'])
            mP = maskP[s]; mPn = 'maskP%d' % s
            k.tt('pool', mP[:], bm4[:], pc[:].unsqueeze(2).to_broadcast([128, 4, 128]), ALU.mult, ['cst', 'pc'], [mPn])
            bt_, kt_, at_ = btd[s], ktd[s], atd[s]
            btn, ktn, atn = 'btd%d' % s, 'ktd%d' % s, 'atd%d' % s
            k.tt('dve', t["rt"][:], r_, t["ecum"][:], ALU.mult, [zn, 'ecum'], ['rt'])
            k.stt(at_[:], t["kkn"][:], -1.0, t["eex"][:], ALU.mult, ALU.mult, ['kkn', 'eex'], [atn])
            k.tt('pool', bt_[:], t["bv"][:], t["einv"][:], ALU.mult, ['bv', 'einv'], [btn])
            k.tt('pool', kt_[:], t["k2"][:], t["einv"][:], ALU.mult, ['k2', 'einv'], [ktn])
            ar = art[s]; arn = 'art%d' % s
            for p in range(4):
                k.tr(B[0][:, p * 128:(p + 1) * 128], at_[:, p * 128:(p + 1) * 128], [atn], [bn[0]])
            k.act(ar[:, :, 0, :].bitcast(F32R), c4(B[0][:]), AF.Copy, [bn[0]], [arn])
            for p in range(4):
                k.tr(B[1][:, p * 128:(p + 1) * 128], t["rt"][:, p * 128:(p + 1) * 128], ['rt'], [bn[1]])
            k.S.op('dve', lambda e: e.tensor_copy(out=ar[:, :, 1, :].bitcast(F32R), in_=c4(B[1][:])), [bn[1]], [arn])
            for (src, srcn, dst, dn, bi) in ((bt_, btn, btTp, 'btTp', 2), (kt_, ktn, ktTp, 'ktTp', 0)):
                for p in range(4):
                    k.tr(B[bi][:, p * 128:(p + 1) * 128], src[:, p * 128:(p + 1) * 128], [srcn], [bn[bi]])
                dv = dst[:].rearrange("k (p e) s -> k p e s", e=2)
                sv = c4(B[bi][:])
                k.act(dv[0:64, :, 0, :].bitcast(F32R), sv[0:64, :, :], AF.Copy, [bn[bi]], [dn])
                k.S.op('dve', lambda e, dv=dv, sv=sv: e.tensor_copy(out=dv[64:128, :, 1, :].bitcast(F32R), in_=sv[64:128, :, :]), [bn[bi]], [dn])
            a2 = A2[s]; a2n = 'A2%d' % s
            mstrict = maskR[:, 0:128].unsqueeze(1).to_broadcast([128, 2, 128])
            mincl = maskR[:, 128:256].unsqueeze(1).to_broadcast([128, 2, 128])
            for h in range(8):
                p = h // 2
                bi = h % 3
                rhs = ar[:, p, :, :].rearrange("k a t -> k (a t)")
                k.mm(B[bi][:, 0:256], btTp[:, h, :], rhs, True, True, ['btTp', arn], [bn[bi]], fast=True)
                k.mm(B[bi][:, 256:512], ktTp[:, h, :], rhs, True, True, ['ktTp', arn], [bn[bi]], fast=True)
                bv4 = B[bi][:].rearrange("k (x a t) -> k x a t", x=2, a=2)
                k.tt('dve', A1[:, h, :, :], bv4[:, :, 0, :], mstrict, ALU.mult, [bn[bi], 'cst'], ['A1'])
                if own:
                    k.tt('dve', a2[:, h, :, :], bv4[:, :, 1, :], mincl, ALU.mult, [bn[bi], 'cst'], [a2n])
            for h in range(8):
                k.mm(B[0][:, h * 64:(h + 1) * 64], A1[:, h, 1, :], z[:, 1024 + h * 64:1024 + (h + 1) * 64], True, True, ['A1', zn], [bn[0]])
            k.act(X0[s][:], B[0][:], AF.Copy, [bn[0]], ['X0%d' % s])

        def stage_b(c):
            s = c % 2
            own = c >= NCH // 2
            z = zc[s]; zn = 'zc%d' % s
            ar = art[s]; arn = 'art%d' % s
            a2 = A2[s]; a2n = 'A2%d' % s
            bt_, kt_, at_ = btd[s], ktd[s], atd[s]
            btn, ktn, atn = 'btd%d' % s, 'ktd%d' % s, 'atd%d' % s
            mP = maskP[s]; mPn = 'maskP%d' % s
            gs = gsb2[s]; gsn = 'gsb%d' % s
            bo = bonus2[s]; bon = 'bonus%d' % s
            L, LT, M = Lb[0], LTb[0], Mb[0]
            k.S.op('pool', lambda e, L=L: e.tensor_copy(out=L[:], in_=A1[:, :, 0, :]), ['A1'], ['L0'])
            for hh in range(2):
                bb = 3 + hh
                for h in range(4 * hh, 4 * hh + 4):
                    k.tr(B[bb][:, (h % 4) * 128:(h % 4 + 1) * 128], A1[:, h, 0, :], ['A1'], [bn[bb]])
                k.act(LT[:, 4 * hh:4 * hh + 4, :], c4(B[bb][:]), AF.Copy, [bn[bb]], ['LT0'])
            k.tt('pool', M[:], L[:], k.ident[:].unsqueeze(1).to_broadcast([128, 8, 128]), ALU.add, ['L0', 'ident'], ['M0'])
            cur = 0
            for rnd in range(6):
                nx = 1 - cur
                L, LT, M = Lb[cur], LTb[cur], Mb[cur]
                Ln, LTn, Mn = Lb[nx], LTb[nx], Mb[nx]
                last = rnd == 5
                for hh in range(2):
                    bL, bLT, bM = (3, 4, 5) if hh == 0 else (6, 7, 5)
                    hs = range(4 * hh, 4 * hh + 4)
                    if not last:
                        for h in hs:
                            k.mm(B[bL][:, (h % 4) * 128:(h % 4 + 1) * 128], LT[:, h, :], L[:, h, :], True, True, ['L%d' % cur, 'LT%d' % cur], [bn[bL]])
                    for h in hs:
                        k.mm(B[bLT][:, (h % 4) * 128:(h % 4 + 1) * 128], L[:, h, :], LT[:, h, :], True, True, ['L%d' % cur, 'LT%d' % cur], [bn[bLT]])
                    if not last:
                        k.act(Ln[:, 4 * hh:4 * hh + 4, :], c4(B[bL][:]), AF.Copy, [bn[bL]], ['L%d' % nx])
                    k.S.op('dve', lambda e, LTn=LTn, hh=hh, bLT=bLT: e.tensor_copy(out=LTn[:, 4 * hh:4 * hh + 4, :], in_=c4(B[bLT][:])),
                           [bn[bLT]], ['LT%d' % nx])
                    for h in hs:
                        k.mm(B[bM][:, (h % 4) * 128:(h % 4 + 1) * 128], LTn[:, h, :], M[:, h, :], True, True, ['LT%d' % nx, 'M%d' % cur], [bn[bM]])
                    k.tt('dve', Mn[:, 4 * hh:4 * hh + 4, :], M[:, 4 * hh:4 * hh + 4, :], c4(B[bM][:]), ALU.add,
                         [bn[bM], 'M%d' % cur], ['M%d' % nx])
                cur = nx
            Minv = Mb[cur]; Mn_ = 'M%d' % cur
            for h in range(8):
                k.mm(B[3][:, h * 64:(h + 1) * 64], Minv[:, h, :], X0[s][:, h * 64:(h + 1) * 64], True, True, [Mn_, 'X0%d' % s], [bn[3]])
            k.act(U0[s][:], B[3][:], AF.Copy, [bn[3]], ['U0%d' % s])
            for h in range(8):
                p = h // 2
                bi = 6 + h // 4
                k.mm(B[bi][:, (h % 4) * 128:(h % 4 + 1) * 128], at_[:, p * 128:(p + 1) * 128], Minv[:, h, :], True, True, [atn, Mn_], [bn[bi]])
            for hh in range(2):
                sv = B[6 + hh][:].rearrange("k (p e t) -> k p e t", p=2, e=2)
                k.act(WT[s][0:64, 2 * hh:2 * hh + 2, :], sv[0:64, :, 0, :], AF.Copy, [bn[6 + hh]], ['WT%d' % s])
                k.S.op('dve', lambda e, sv=sv, hh=hh, s=s: e.tensor_copy(out=WT[s][64:128, 2 * hh:2 * hh + 2, :], in_=sv[64:128, :, 1, :]), [bn[6 + hh]], ['WT%d' % s])
            Tc, Tn = Tb[c % 2], Tb[(c + 1) % 2]
            Tcn, Tnn = 'T%d' % (c % 2), 'T%d' % ((c + 1) % 2)
            for p in range(4):
                k.mm(B[3][:, p * 128:(p + 1) * 128], WT[s][:, p, :], Tc[:, p, :], True, True, ['WT%d' % s, Tcn], [bn[3]])
            k.tt('dve', U[s][:], B[3][:], U0[s][:], ALU.add, [bn[3], 'U0%d' % s], ['U%d' % s])
            if own:
                first = True
                for p in range(4):
                    k.mm(B[5][:, p * 128:(p + 1) * 128], ar[:, p, 1, :], Tc[:, p, :], first, False, [arn, Tcn], [bn[5]])
                    first = False
                for h in range(8):
                    k.mm(B[5][:, h * 64:(h + 1) * 64], a2[:, h, 0, :], U[s][:, h * 64:(h + 1) * 64], False, False, [a2n, 'U%d' % s], [bn[5]])
                    k.mm(B[5][:, h * 64:(h + 1) * 64], a2[:, h, 1, :], z[:, 1024 + h * 64:1024 + (h + 1) * 64], False, h == 7, [a2n, zn], [bn[5]])
            k.mm(B[4][:], k.ident[:], Tc[:].rearrange("k p n -> k (p n)"), True, False, ['ident', Tcn], [bn[4]])
            for p in range(4):
                k.mm(B[4][:, p * 128:(p + 1) * 128], kt_[:, p * 128:(p + 1) * 128], z[:, 1024 + p * 128:1024 + (p + 1) * 128], False, False, [ktn, zn], [bn[4]])
            for p in range(4):
                k.mm(B[4][:, p * 128:(p + 1) * 128], bt_[:, p * 128:(p + 1) * 128], U[s][:, p * 128:(p + 1) * 128], False, p == 3, [btn, 'U%d' % s], [bn[4]])
            k.tt('dve', Tn[:].rearrange("k p n -> k (p n)"), B[4][:], mP[:].rearrange("k p n -> k (p n)"), ALU.mult, [bn[4], mPn], [Tnn])
            if own:
                y, yc = t["ysb"], t["yc"]
                k.act(y[:], B[5][:], AF.Copy, [bn[5]], ['ysb'])
                k.S.op('dve', lambda e: e.tensor_reduce(out=sm["s1"][:], in_=h8(y[:]), axis=AX.X, op=ALU.add), ['ysb'], ['s1'])
                k.ts('dve', sm["s1"][:], sm["s1"][:], -1.0 / 64, None, ALU.mult, None, ['s1'], ['s1'])
                k.tt('dve', h8(yc[:]), h8(y[:]), bc8(sm["s1"][:]), ALU.add, ['ysb', 's1'], ['yc'])
                k.tt('pool', y[:], yc[:], yc[:], ALU.mult, ['yc'], ['ysb'])
                k.S.op('dve', lambda e: e.tensor_reduce(out=sm["s2"][:], in_=h8(y[:]), axis=AX.X, op=ALU.add), ['ysb'], ['s2'])
                k.ts('dve', sm["s2"][:], sm["s2"][:], 1.0 / 64, GN_EPS, ALU.mult, ALU.add, ['s2'], ['s2'])
                k.act(sm["s2"][:], sm["s2"][:], AF.Sqrt, ['s2'], ['s2'])
                k.S.op('dve', lambda e: e.reciprocal(out=sm["s2"][:], in_=sm["s2"][:]), ['s2'], ['s2'])
                k.tt('dve', h8(yc[:]), h8(yc[:]), bc8(sm["s2"][:]), ALU.mult, ['yc', 's2'], ['yc'])
                k.tt('pool', yc[:], yc[:], bc["ln_w"][:], ALU.mult, ['yc', 'cst'], ['yc'])
                k.tt('pool', yc[:], yc[:], bc["ln_b"][:], ALU.add, ['yc', 'cst'], ['yc'])
                k.tt('pool', yc[:], yc[:], bo[:], ALU.add, ['yc', bon], ['yc'])
                k.tt('dve', yc[:], yc[:], gs[:], ALU.mult, ['yc', gsn], ['yc'])
                for p in range(4):
                    k.tr(B[5][:, p * 128:(p + 1) * 128], yc[:, p * 128:(p + 1) * 128], ['yc'], [bn[5]])
                k.act(orT[:], c4(B[5][:]), AF.Copy, [bn[5]], ['orT'])
                o0 = (c - NCH // 2) * 128
                k.dma(D["S_orT"][:, o0:o0 + 128].rearrange("(c p) t -> p c t", p=128), orT[:], ['orT'], ['S_orT'])

        def capture(fn, c):
            S.begin_capture()
            fn(c)
            return S.end_capture()

        def merge(lb, la):
            out = []
            ia = 0
            nb, na = len(lb), len(la)
            for ib, it in enumerate(lb):
                out.append(it)
                tgt = ((ib + 1) * na) // max(nb, 1)
                while ia < tgt:
                    out.append(la[ia]); ia += 1
            out.extend(la[ia:])
            return out

        S.replay(capture(stage_a, 0))
        for c in range(NCH):
            lb = capture(stage_b, c)
            la = capture(stage_a, c + 1) if c + 1 < NCH else []
            S.replay(merge(lb, la))


def phase_d(nc, k, D):
    S = k.S
    TB = 256
    NB = NO // TB
    with ExitStack() as es:
        T = lambda name, shape: es.enter_context(nc.sbuf_tensor(name, shape, F32))
        P = lambda name: es.enter_context(nc.psum_tensor(name, [128, 512], F32))
        B = [P("d_b%d" % i) for i in range(8)]
        bn = ['db%d' % i for i in range(8)]
        NW = 2
        wst = [T("d_wst%d" % i, [128, 4096]) for i in range(NW)]
        wraw = [T("d_wraw%d" % i, [128, 4096]) for i in range(3)]
        oaTr = T("d_oaTr", [128, 2, 128]); orTr = T("d_orTr", [128, 4, 128])
        wau = T("d_wau", [128, 2, 1024]); wru = T("d_wru", [128, 4, 1024]); wple = T("d_wple", [128, 2, 1024])
        k.dma(wraw[0][:, 0:2048].rearrange("p (c n) -> p c n", c=2), D["w_au"].rearrange("(c p) n -> p c n", p=128), [], ['wraw0'])
        k.act(wau[:].bitcast(F32R), wraw[0][:, 0:2048].rearrange("p (c n) -> p c n", c=2), AF.Copy, ['wraw0'], ['cst'])
        k.dma(wraw[0][:, 0:4096].rearrange("p (c n) -> p c n", c=4), D["w_ru"].rearrange("(c p) n -> p c n", p=128), [], ['wraw0'])
        k.act(wru[:].bitcast(F32R), wraw[0][:, 0:4096].rearrange("p (c n) -> p c n", c=4), AF.Copy, ['wraw0'], ['cst'])
        k.dma(wraw[0][:, 0:2048].rearrange("p (c n) -> p c n", c=2), D["w_ple"].rearrange("(c p) n -> p c n", p=128), [], ['wraw0'])
        k.act(wple[:].bitcast(F32R), wraw[0][:, 0:2048].rearrange("p (c n) -> p c n", c=2), AF.Copy, ['wraw0'], ['cst'])
        gffn = T("d_gffn", [128, 8]); k.dma(gffn[:], D["gffn"][:, :], [], ['cst'])
        gm = T("d_gm", [128, 1024]); gf = T("d_gf", [128, 1024]); gp = T("d_gp", [128, 1024])
        k.dma(gm[:], D["g_mpost"].partition_broadcast(128), [], ['cst'])
        k.dma(gf[:], D["g_fpost"].partition_broadcast(128), [], ['cst'])
        k.dma(gp[:], D["g_ppost"].partition_broadcast(128), [], ['cst'])
        hT = T("d_hT", [128, 32, TB]); fT = T("d_fT", [128, 8, TB])
        hres = [T("d_h%d" % i, [128, 1024]) for i in range(2)]
        gates = T("d_gates", [128, 2048])
        oaT = T("d_oaT", [128, 2, 128]); orT = T("d_orT", [128, 4, 128])
        mer = T("d_mer", [128, 1024]); mer2 = T("d_mer2", [128, 1024]); junk = T("d_junk", [128, 1024])
        xT = T("d_xT", [128, 8, 128])
        pt_ = T("d_pt", [128, 256]); pT = T("d_pT", [128, 2, 128])
        ss = T("d_ss", [128, 1]); rstd = T("d_rstd", [128, 1])
        rel = [T("d_rel%d" % i, [128, TB]) for i in range(2)]
        wi = [0]

        def wload(src_ap, shape_kind):
            s = wi[0] % NW; s2 = wi[0] % 3; wi[0] += 1
            if shape_kind == 'col':
                k.dma(wraw[s2][:].rearrange("p (c n) -> p c n", c=8), src_ap.rearrange("(c p) n -> p c n", p=128), [], ['wraw%d' % s2])
            else:
                k.dma(wraw[s2][:].rearrange("p (c n) -> p c n", c=4), src_ap.rearrange("(c p) n -> p c n", p=128), [], ['wraw%d' % s2])
            if (wi[0] % 2) == 0:
                k.act(wst[s][:].bitcast(F32R), wraw[s2][:], AF.Copy, ['wraw%d' % s2], ['wst%d' % s])
            else:
                k.S.op('dve', lambda e: e.tensor_copy(out=wst[s][:].bitcast(F32R), in_=wraw[s2][:]), ['wraw%d' % s2], ['wst%d' % s])
            return s

        def transpose8(src, srcn, dst3, dstn, scale_ap=None):
            for c in range(8):
                k.tr(B[4 + c // 4][:, (c % 4) * 128:(c % 4 + 1) * 128], src[:, c * 128:(c + 1) * 128], [srcn], [bn[4 + c // 4]])
            for hf in range(2):
                pv = B[4 + hf][:].rearrange("p (c t) -> p c t", c=4)
                if scale_ap is None:
                    if hf == 0:
                        k.act(dst3[:, 0:4, :].bitcast(F32R), pv, AF.Copy, [bn[4]], [dstn])
                    else:
                        k.S.op('dve', lambda e, pv=pv: e.tensor_copy(out=dst3[:, 4:8, :].bitcast(F32R), in_=pv), [bn[5]], [dstn])
                else:
                    k.tt('dve', dst3[:, hf * 4:(hf + 1) * 4, :].bitcast(F32R), pv, scale_ap[:, hf * 4:(hf + 1) * 4].unsqueeze(2).to_broadcast([128, 4, 128]),
                         ALU.mult, [bn[4 + hf], 'cst'], [dstn])

        def dense1024(lhs3, lhsn, nk, wslots_or_res, banks):
            for hf in range(2):
                for c in range(nk):
                    wap, wn = wslots_or_res(c, hf)
                    k.mm(B[banks[hf]][:], lhs3[:, c, :], wap, c == 0, c == nk - 1, [lhsn, wn], [bn[banks[hf]]], fast=True)

        for blk in range(NB):
            ws_out = [wload(D["w_out"][:, hf * 512:(hf + 1) * 512], 'col') for hf in range(2)]
            for j in range(TB // 128):
                o0 = blk * TB + j * 128
                h_ = hres[j]; hn = 'h%d' % j
                k.dma(h_[:], D["xloc"][NPRE + o0:NPRE + o0 + 128, :], [], [hn])
                k.dma(oaTr[:], D["S_oaT"][:, o0:o0 + 128].rearrange("(c p) t -> p c t", p=128), ['S_oaT'], ['oaTr'])
                k.dma(orTr[:], D["S_orT"][:, o0:o0 + 128].rearrange("(c p) t -> p c t", p=128), ['S_orT'], ['orTr'])
                k.S.op('dve', lambda e: e.tensor_copy(out=oaT[:].bitcast(F32R), in_=oaTr[:]), ['oaTr'], ['oaT'])
                k.S.op('dve', lambda e: e.tensor_copy(out=orT[:].bitcast(F32R), in_=orTr[:]), ['orTr'], ['orT'])
                k.dma(gates[:], D["S_g"][o0:o0 + 128, :], ['S_g'], ['gates'])
                dense1024(oaT, 'oaT', 2, lambda c, hf: (wau[:, c, hf * 512:(hf + 1) * 512], 'cst'), (0, 1))
                dense1024(orT, 'orT', 4, lambda c, hf: (wru[:, c, hf * 512:(hf + 1) * 512], 'cst'), (2, 3))
                for hf in range(2):
                    sl = slice(hf * 512, (hf + 1) * 512)
                    k.tt('dve', mer[:, sl], B[hf][:], gates[:, sl], ALU.mult, [bn[hf], 'gates'], ['mer'])
                    k.tt('dve', mer2[:, sl], B[2 + hf][:], gates[:, 1024 + hf * 512:1024 + (hf + 1) * 512], ALU.mult, [bn[2 + hf], 'gates'], ['mer2'])
                k.tt('pool', mer[:], mer[:], mer2[:], ALU.add, ['mer', 'mer2'], ['mer'])
                transpose8(mer, 'mer', xT, 'xT')
                dense1024(xT, 'xT', 8, lambda c, hf: (wst[ws_out[hf]][:].rearrange("p (c n) -> p c n", c=8)[:, c, :], 'wst%d' % ws_out[hf]), (6, 7))
                pso = B[6][:]
                k.act(junk[:, 0:512], B[6][:], AF.Square, [bn[6]], ['junk', 'ss'], accum_out=ss[:])
                k.act(junk[:, 512:1024], B[7][:], AF.Square, [bn[7]], ['junk', 'rstd'], accum_out=rstd[:])
                k.tt('dve', ss[:], ss[:], rstd[:], ALU.add, ['ss', 'rstd'], ['ss'])
                k.ts('dve', rstd[:], ss[:], 1.0 / 1024, EPS, ALU.mult, ALU.add, ['ss'], ['rstd'])
                k.act(rstd[:], rstd[:], AF.Sqrt, ['rstd'], ['rstd'])
                k.S.op('dve', lambda e: e.reciprocal(out=rstd[:], in_=rstd[:]), ['rstd'], ['rstd'])
                for hf in range(2):
                    sl = slice(hf * 512, (hf + 1) * 512)
                    k.stt(mer[:, sl], B[6 + hf][:], rstd[:, 0:1], gm[:, sl], ALU.mult, ALU.mult, [bn[6 + hf], 'rstd', 'cst'], ['mer'])
                k.tt('pool', h_[:], h_[:], mer[:], ALU.add, [hn, 'mer'], [hn])
                k.rms_rstd(h_[:], junk[:], ss[:], rstd[:], 1024, EPS, [hn], ('junk', 'ss', 'rstd'))
                k.ts('dve', mer2[:], h_[:], rstd[:, 0:1], None, ALU.mult, None, [hn, 'rstd'], ['mer2'])
                transpose8(mer2, 'mer2', fT[:, :, j * 128:(j + 1) * 128], 'fT', scale_ap=gffn)
            for cb in range(8):
                s = wload(D["w_ffi"][:, cb * 512:(cb + 1) * 512], 'col')
                wv = wst[s][:].rearrange("p (c n) -> p c n", c=8)
                for q in range(4):
                    ffc = cb * 4 + q
                    bi = 4 + ffc % 2
                    for c in range(8):
                        k.mm(B[bi][:, 0:TB], wv[:, c, q * 128:(q + 1) * 128], fT[:, c, :], c == 0, c == 7, ['wst%d' % s, 'fT'], [bn[bi]], fast=True)
                    rl = rel[ffc % 2]; rln = 'rel%d' % (ffc % 2)
                    k.act(rl[:], B[bi][:, 0:TB], AF.Relu, [bn[bi]], [rln])
                    k.tt('dve', hT[:, ffc, :].bitcast(F32R), rl[:], rl[:], ALU.mult, [rln], ['hT'])
            for rb in range(8):
                s = wload(D["w_ffo"][rb * 512:(rb + 1) * 512, :], 'row')
                wv = wst[s][:].rearrange("p (c n) -> p c n", c=4)
                for j in range(TB // 128):
                    for hf in range(2):
                        bi = 2 * j + hf
                        for q in range(4):
                            ffc = rb * 4 + q
                            k.mm(B[bi][:], hT[:, ffc, j * 128:(j + 1) * 128], wv[:, q, hf * 512:(hf + 1) * 512], ffc == 0, ffc == 31,
                                 ['hT', 'wst%d' % s], [bn[bi]], fast=True)
            ws_pg = [wload(D["w_pg"][:, hf * 512:(hf + 1) * 512], 'col') for hf in range(2)]
            for j in range(TB // 128):
                o0 = blk * TB + j * 128
                h_ = hres[j]; hn = 'h%d' % j
                b0, b1 = 2 * j, 2 * j + 1
                k.act(junk[:, 0:512], B[b0][:], AF.Square, [bn[b0]], ['junk', 'ss'], accum_out=ss[:])
                k.act(junk[:, 512:1024], B[b1][:], AF.Square, [bn[b1]], ['junk', 'rstd'], accum_out=rstd[:])
                k.tt('dve', ss[:], ss[:], rstd[:], ALU.add, ['ss', 'rstd'], ['ss'])
                k.ts('dve', rstd[:], ss[:], 1.0 / 1024, EPS, ALU.mult, ALU.add, ['ss'], ['rstd'])
                k.act(rstd[:], rstd[:], AF.Sqrt, ['rstd'], ['rstd'])
                k.S.op('dve', lambda e: e.reciprocal(out=rstd[:], in_=rstd[:]), ['rstd'], ['rstd'])
                for hf, bb in enumerate((b0, b1)):
                    sl = slice(hf * 512, (hf + 1) * 512)
                    k.stt(mer[:, sl], B[bb][:], rstd[:, 0:1], gf[:, sl], ALU.mult, ALU.mult, [bn[bb], 'rstd', 'cst'], ['mer'])
                k.tt('pool', h_[:], h_[:], mer[:], ALU.add, [hn, 'mer'], [hn])
                transpose8(h_, hn, xT, 'xT')
                dense1024(xT, 'xT', 8, lambda c, hf: (wst[ws_pg[hf]][:].rearrange("p (c n) -> p c n", c=8)[:, c, :], 'wst%d' % ws_pg[hf]), (6, 7))
                for hf in range(2):
                    k.act(mer2[:, hf * 512:(hf + 1) * 512], B[6 + hf][:], AF.Sigmoid, [bn[6 + hf]], ['mer2'])
                k.dma(pt_[:], D["pown"][o0:o0 + 128, :], [], ['pt'])
                for c in range(2):
                    k.tr(B[4][:, c * 128:(c + 1) * 128], pt_[:, c * 128:(c + 1) * 128], ['pt'], [bn[4]])
                k.act(pT[:].bitcast(F32R), B[4][:, 0:256].rearrange("p (c t) -> p c t", c=2), AF.Copy, [bn[4]], ['pT'])
                dense1024(pT, 'pT', 2, lambda c, hf: (wple[:, c, hf * 512:(hf + 1) * 512], 'cst'), (6, 7))
                for hf in range(2):
                    sl = slice(hf * 512, (hf + 1) * 512)
                    k.tt('dve', mer[:, sl], B[6 + hf][:], mer2[:, sl], ALU.mult, [bn[6 + hf], 'mer2'], ['mer'])
                k.rms_rstd(mer[:], junk[:], ss[:], rstd[:], 1024, EPS, ['mer'], ('junk', 'ss', 'rstd'))
                k.stt(mer[:], mer[:], rstd[:, 0:1], gp[:], ALU.mult, ALU.mult, ['mer', 'rstd', 'cst'], ['mer'])
                k.tt('pool', mer[:], mer[:], h_[:], ALU.add, ['mer', hn], ['mer'])
                k.dma(D["out"][o0:o0 + 128, :], mer[:], ['mer'], ['out'])


def _consts():
    i = np.arange(128)
    kq = i[:, None]; qq = i[None, :]
    mprev = (kq >= qq).astype(np.float32); mcur = (kq <= qq).astype(np.float32)
    maskA = np.concatenate([mprev, mcur], 1)
    maskA4 = np.tile(maskA, (1, 4))
    strict = (kq < qq).astype(np.float32); incl = (kq <= qq).astype(np.float32)
    maskR = np.concatenate([strict, incl, strict, incl], 1)
    tri = (kq <= qq).astype(np.float32)
    bm = ((kq // 64) == (qq // 64)).astype(np.float32)
    bm4 = np.tile(bm, (1, 4))
    onesPad = np.zeros((128, 2, 128), np.float32)
    onesPad[:, 0, 0:64] = 1.0; onesPad[:, 1, 64:128] = 1.0
    return dict(ident=np.eye(128, dtype=np.float32), maskA4=maskA4, maskR=maskR, tri=tri, blockmask4=bm4,
                onesPad=onesPad.reshape(128, 256))


def make_in_maps(inp):
    f = lambda a: np.ascontiguousarray(np.asarray(a, dtype=np.float32))
    x = f(inp['x']); p = f(inp['p'])
    cst = _consts()
    shared = dict(
        w_in=f(inp['w_in'][0]), w_au=f(inp['w_attn_up'][0]), w_ru=f(inp['w_rwkv_up'][0]), w_out=f(inp['w_out'][0]),
        w_ffi=f(inp['w_ff_in'][0]), w_ffo=f(inp['w_ff_out'][0]), w_ple=f(inp['w_ple'][0]), w_pg=f(inp['w_ple_gate'][0]),
        gpre=f(np.asarray(inp['mix_pre_norm'][0]).reshape(8, 128).T), gffn=f(np.asarray(inp['ffn_pre_norm'][0]).reshape(8, 128).T),
        g_mpost=f(np.asarray(inp['mix_post_norm'][0]).reshape(1, 1024)), g_fpost=f(np.asarray(inp['ffn_post_norm'][0]).reshape(1, 1024)),
        g_ppost=f(np.asarray(inp['ple_post_norm'][0]).reshape(1, 1024)),
        mu=f(np.asarray(inp['rwkv_mu'][0]).reshape(1, 1696)),
        w0=f(np.asarray(inp['rwkv_w0'][0]).reshape(1, 512)), a0=f(np.asarray(inp['rwkv_a0'][0]).reshape(1, 512)),
        k_k=f(np.asarray(inp['rwkv_k_k'][0]).reshape(1, 512)), k_a=f(np.asarray(inp['rwkv_k_a'][0]).reshape(1, 512)),
        ln_w=f(np.asarray(inp['rwkv_ln_w'][0]).reshape(1, 512)), ln_b=f(np.asarray(inp['rwkv_ln_b'][0]).reshape(1, 512)),
        r_k=f(np.asarray(inp['rwkv_r_k'][0]).reshape(1, 512)),
        g2=f(inp['rwkv_g2'][0]),
    )
    w2pad = np.zeros((64, 512), np.float32); w2pad[0:32] = np.asarray(inp['rwkv_w2'][0])
    a2pad = np.zeros((64, 512), np.float32); a2pad[32:64] = np.asarray(inp['rwkv_a2'][0])
    shared.update(w2pad=w2pad, a2pad=a2pad)
    shared.update({kk: vv for kk, vv in cst.items()})
    inv_freq = (10000.0 ** (-np.arange(0, 64, 2, dtype=np.float32) / np.float32(64))).astype(np.float32)
    maps = []
    for c in range(8):
        b, half = c // 2, c % 2
        xloc = np.zeros((NL, 1024), np.float32)
        if half == 0:
            xloc[NPRE:] = x[b, 0:4096]
        else:
            xloc[:] = x[b]
        pos = (half * 4096 - 2048 + np.arange(NK)).astype(np.float32)
        ang = (pos[:, None] * inv_freq[None, :]).astype(np.float32)
        m = dict(shared)
        m.update(xloc=xloc, pown=f(p[0, b, half * 4096:(half + 1) * 4096]),
                 cosT=f(np.tile(np.cos(ang), (1, 12))), sinT=f(np.tile(np.sin(ang), (1, 12))),
                 onesPre=f(cst['onesPad'] * float(half)))
        maps.append(m)
    return maps


def kernel(**inputs):
    nc = build_program()
    maps = make_in_maps(inputs)
    res = run_bass_kernel_spmd(nc, maps, core_ids=list(range(8)))
    out = np.zeros((4, 8192, 1024), np.float32)
    for c in range(8):
        b, half = c // 2, c % 2
        out[b, half * 4096:(half + 1) * 4096] = res.results[c]["out"]
    return out
```

```python
import os
import numpy as np
import concourse.bass as bass
import concourse.mybir as mybir
from concourse.bass_utils import run_bass_kernel_spmd
from contextlib import ExitStack

F32 = mybir.dt.float32
F32R = mybir.dt.float32r
AF = mybir.ActivationFunctionType
ALU = mybir.AluOpType
AX = mybir.AxisListType
ENGS = ('sp', 'pe', 'act', 'dve', 'pool')
DEBUG = None


class Sched:
    def __init__(self, nc, es, n_dma=8):
        self.nc = nc
        self.sem = {e: es.enter_context(nc.semaphore('s_' + e)) for e in ('pe', 'act', 'dve', 'pool')}
        self.dsem = [es.enter_context(nc.semaphore('sd%d' % i)) for i in range(n_dma)]
        self.cnt = {e: 0 for e in self.sem}
        self.dcnt = [0] * n_dma
        self.dnext = 0
        self.streams = {e: [] for e in ENGS}
        self.waited = {e: {} for e in ENGS}
        self.lastw = {}
        self.readers = {}
        self.nops = 0

    def _semh(self, k):
        return self.dsem[k[1]] if isinstance(k, tuple) else self.sem[k]

    def _deps(self, reads, writes):
        deps = {}

        def add(k, v):
            if deps.get(k, 0) < v:
                deps[k] = v
        for r in reads:
            t = self.lastw.get(r)
            if t:
                add(*t)
        for w in writes:
            t = self.lastw.get(w)
            if t:
                add(*t)
            for k, v in self.readers.get(w, {}).items():
                add(k, v)
        return deps

    def _record(self, tok, reads, writes):
        for r in reads:
            d = self.readers.setdefault(r, {})
            if d.get(tok[0], 0) < tok[1]:
                d[tok[0]] = tok[1]
        for w in writes:
            self.lastw[w] = tok
            self.readers[w] = {}

    def _waits(self, eng, deps):
        out = []
        wd = self.waited[eng]
        for k, v in deps.items():
            if eng == 'pe' and k == 'pe':
                continue
            if wd.get(k, 0) >= v:
                continue
            wd[k] = v
            out.append((k, v))
        return out

    def begin_capture(self):
        self.cap = []

    def end_capture(self):
        c = self.cap
        self.cap = None
        return c

    def replay(self, items):
        for it in items:
            if it[0] == 'op':
                self.op(it[1], it[2], it[3], it[4])
            else:
                self.dma(it[1], it[2], it[3], it[4], it[5], **it[6])

    def op(self, eng, fn, reads=(), writes=()):
        if getattr(self, 'cap', None) is not None:
            self.cap.append(('op', eng, fn, tuple(reads), tuple(writes)))
            return None
        ex = [r for r in reads if isinstance(r, str) and r[:2] in ('cb', 'db', 'pt', 'pz', 'ps', 'pk', 'pq') and r not in writes]
        if ex:
            writes = list(writes) + ex
        deps = self._deps(reads, writes)
        waits = self._waits(eng, deps)
        self.cnt[eng] += 1
        tok = (eng, self.cnt[eng])
        self.streams[eng].append((waits, fn, (eng, 1)))
        self._record(tok, reads, writes)
        self.nops += 1
        return tok

    def dma(self, out, in_, reads=(), writes=(), q='sp', **kw):
        if getattr(self, 'cap', None) is not None:
            self.cap.append(('dma', out, in_, tuple(reads), tuple(writes), q, kw))
            return None
        slot = self.dnext
        self.dnext = (self.dnext + 1) % len(self.dsem)
        deps = self._deps(reads, writes)
        k = ('d', slot)
        if self.dcnt[slot] > 0 and deps.get(k, 0) < self.dcnt[slot]:
            deps[k] = self.dcnt[slot]
        waits = self._waits(q, deps)
        self.dcnt[slot] += 16
        tok = (k, self.dcnt[slot])
        self.streams[q].append((waits, lambda e: e.dma_start(out=out, in_=in_, **kw), (k, 16)))
        self._record(tok, reads, writes)
        self.nops += 1
        return tok

    def barrier(self):
        toks = {e: c for e, c in self.cnt.items() if c > 0}
        for i, c in enumerate(self.dcnt):
            if c > 0:
                toks[('d', i)] = c
        for e in ENGS:
            waits = self._waits(e, {k: v for k, v in toks.items() if k != e})
            if waits:
                self.streams[e].append((waits, None, None))
        self.lastw = {}
        self.readers = {}

    def finish(self):
        toks = {}
        for i, c in enumerate(self.dcnt):
            if c > 0:
                toks[('d', i)] = c
        waits = self._waits('sp', toks)
        self.streams['sp'].append((waits, None, None))

    def flush(self):
        nc = self.nc
        with nc.Block() as block:
            for ename, deco in (('sp', block.sync), ('pe', block.tensor), ('act', block.scalar),
                                ('dve', block.vector), ('pool', block.gpsimd)):
                stream = self.streams[ename]

                def body(eng, stream=stream):
                    for waits, fn, inc in stream:
                        for (k, v) in waits:
                            eng.wait_ge(self._semh(k), v)
                        if fn is not None:
                            ins = fn(eng)
                            if inc is not None:
                                ins.then_inc(self._semh(inc[0]), inc[1])
                deco(body)
        self.streams = {e: [] for e in ENGS}


NL, NPRE, NO, KV0, NK = 8192, 4096, 4096, 2048, 6144
EPS = 1e-6
GN_EPS = 64e-5
WCOLS = 6048


class K:
    def __init__(self, nc, es):
        self.nc, self.es = nc, es
        self.S = Sched(nc, es)
        self.uid = 0

    def act(self, out, in_, func, r, w, **kw):
        self.S.op('act', lambda e: e.activation(out=out, in_=in_, func=func, **kw), r, w)

    def tt(self, eng, out, in0, in1, op, r, w):
        self.S.op(eng, lambda e: e.tensor_tensor(out=out, in0=in0, in1=in1, op=op), r, w)

    def ts(self, eng, out, in0, s1, s2, op0, op1, r, w):
        if s2 is None:
            self.S.op(eng, lambda e: e.tensor_scalar(out=out, in0=in0, scalar1=s1, scalar2=None, op0=op0), r, w)
        else:
            self.S.op(eng, lambda e: e.tensor_scalar(out=out, in0=in0, scalar1=s1, scalar2=s2, op0=op0, op1=op1), r, w)

    def stt(self, out, in0, scalar, in1, op0, op1, r, w):
        self.S.op('dve', lambda e: e.scalar_tensor_tensor(out=out, in0=in0, scalar=scalar, in1=in1, op0=op0, op1=op1), r, w)

    def mm(self, out, lhsT, rhs, start, stop, r, w, fast=False):
        if fast:
            lhsT = lhsT.bitcast(F32R)
            rhs = rhs.bitcast(F32R)
        self.S.op('pe', lambda e: e.matmul(out, lhsT=lhsT, rhs=rhs, start=start, stop=stop, skip_group_check=True), r, w)

    def tr(self, out, in_, r, w):
        ident = self.ident
        self.S.op('pe', lambda e: e.transpose(out=out, in_=in_, identity=ident[:]), list(r) + ['ident'], w)

    def dma(self, out, in_, r, w):
        self.S.dma(out, in_, r, w)

    def rms_rstd(self, src, junk, ss, rstd, n, eps, r, names):
        jn, sn, rn = names
        self.act(junk, src, AF.Square, r, [jn, sn], accum_out=ss)
        self.ts('dve', rstd, ss, 1.0 / n, eps, ALU.mult, ALU.add, [sn], [rn])
        self.act(rstd, rstd, AF.Sqrt, [rn], [rn])
        self.S.op('dve', lambda e: e.reciprocal(out=rstd, in_=rstd), [rn], [rn])


def build_program():
    nc = bass.Bass("TRN2", target_bir_lowering=False)
    D = {}

    def din(name, shape):
        D[name] = nc.dram_tensor(name, list(shape), F32, kind="ExternalInput").ap()

    def dscr(name, shape):
        kind = "ExternalOutput" if DEBUG else "Internal"
        D[name] = nc.dram_tensor(name, list(shape), F32, kind=kind).ap()

    din("xloc", [NL, 1024]); din("pown", [NO, 256])
    din("w_in", [1024, WCOLS]); din("w_au", [256, 1024]); din("w_ru", [512, 1024]); din("w_out", [1024, 1024])
    din("w_ffi", [1024, 4096]); din("w_ffo", [4096, 1024]); din("w_ple", [256, 1024]); din("w_pg", [1024, 1024])
    din("gpre", [128, 8]); din("gffn", [128, 8])
    din("g_mpost", [1, 1024]); din("g_fpost", [1, 1024]); din("g_ppost", [1, 1024])
    din("mu", [1, 1696])
    for nm in ("w0", "a0", "k_k", "k_a", "ln_w", "ln_b", "r_k"):
        din(nm, [1, 512])
    din("w2pad", [64, 512]); din("a2pad", [64, 512]); din("g2", [96, 512])
    din("cosT", [NK, 384]); din("sinT", [NK, 384])
    din("ident", [128, 128]); din("maskA4", [128, 1024]); din("maskR", [128, 512]); din("tri", [128, 128])
    din("blockmask4", [128, 512]); din("onesPad", [128, 256]); din("onesPre", [128, 256])
    D["out"] = nc.dram_tensor("out", [NO, 1024], F32, kind="ExternalOutput").ap()
    dscr("S_q", [NO, 768]); dscr("S_k", [NK, 768]); dscr("S_v", [NK, 768])
    dscr("S_r", [NL + 1, 1696]); dscr("S_g", [NO, 2048])
    dscr("S_oaT", [256, NO]); dscr("S_orT", [512, NO])
    if DEBUG:
        dscr("dbgC", [24, 128, 512])

    with ExitStack() as es0:
        k = K(nc, es0)
        S = k.S
        ident = es0.enter_context(nc.sbuf_tensor("ident_sb", [128, 128], F32))
        k.ident = ident
        S.dma(ident[:], D["ident"][:, :], writes=['ident'])
        phases = DEBUG or "ABCD"
        if "A" in phases:
            phase_a(nc, k, D)
            S.barrier(); S.flush()
        if "B" in phases:
            phase_b(nc, k, D)
            S.barrier(); S.flush()
        if "C" in phases:
            phase_c(nc, k, D)
            S.barrier(); S.flush()
        if "D" in phases:
            phase_d(nc, k, D)
        S.barrier()
        S.finish()
        S.flush()
    return nc


COLBLOCKS = {
    'Q0': (0, 512, 'Q', 0), 'Q1': (512, 256, 'Q', 512),
    'K0': (768, 512, 'K', 0), 'K1': (1280, 256, 'K', 512),
    'V0': (1536, 512, 'V', 0), 'V1': (2048, 256, 'V', 512),
    'R0': (2304, 512, 'R', 0), 'R1': (2816, 512, 'R', 512), 'R2': (3328, 512, 'R', 1024), 'R3': (3840, 160, 'R', 1536),
    'G0': (4000, 512, 'G', 0), 'G1': (4512, 512, 'G', 512), 'G2': (5024, 512, 'G', 1024), 'G3': (5536, 512, 'G', 1536),
}


def phase_a(nc, k, D):
    S = k.S
    with ExitStack() as es:
        T = lambda name, shape: es.enter_context(nc.sbuf_tensor(name, shape, F32))
        P = lambda name: es.enter_context(nc.psum_tensor(name, [128, 512], F32))
        xt = [T("a_xt%d" % i, [128, 1024]) for i in range(2)]
        junk = T("a_junk", [128, 1024]); xs = T("a_xs", [128, 1024])
        ss = T("a_ss", [128, 1]); rstd = T("a_rstd", [128, 1])
        uT = [T("a_uT%d" % i, [128, 8, 1024]) for i in range(2)]
        wb = [T("a_wb%d" % i, [128, 8, 512]) for i in range(2)]
        wraw = [T("a_wraw%d" % i, [128, 8, 512]) for i in range(3)]
        raw = [T("a_raw%d" % i, [128, 512]) for i in range(2)]
        stg = [T("a_stg%d" % i, [128, 512]) for i in range(3)]
        cblk = T("a_cblk", [128, 8, 384]); sblk = T("a_sblk", [128, 8, 384])
        tmp = [T("a_tmp%d" % i, [128, 256]) for i in range(4)]
        gpre = T("a_gpre", [128, 8]); zrow = T("a_zrow", [1, 1696])
        pt = [P("a_pt%d" % i) for i in range(2)]
        pz = [P("a_pz%d" % i) for i in range(3)]
        k.dma(gpre[:], D["gpre"][:, :], [], ['gpre'])
        S.op('pool', lambda e: e.memset(zrow[:], 0.0), [], ['zrow'])
        k.dma(D["S_r"][0:1, :], zrow[:], ['zrow'], ['S_r0'])

        items = []
        for blk in range(8):
            own = blk >= 4
            kv = blk >= 2
            cbs = []
            if own:
                cbs += ['Q0', 'Q1']
            if kv:
                cbs += ['K0', 'K1', 'V0', 'V1']
            cbs += ['R0', 'R1', 'R2', 'R3']
            if own:
                cbs += ['G0', 'G1', 'G2', 'G3']
            for i, cb in enumerate(cbs):
                items.append((blk, cb, i == 0))

        def load_w_dma(idx):
            blk, cb, _ = items[idx]
            c0, ncols, _, _ = COLBLOCKS[cb]
            s3 = idx % 3
            k.dma(wraw[s3][:, :, 0:ncols], D["w_in"][:, c0:c0 + ncols].rearrange("(c p) n -> p c n", p=128), [], ['wraw%d' % s3])

        def load_w_round(idx):
            blk, cb, _ = items[idx]
            c0, ncols, _, _ = COLBLOCKS[cb]
            s3 = idx % 3; s = idx % 2
            k.act(wb[s][:, :, 0:ncols].bitcast(F32R), wraw[s3][:, :, 0:ncols], AF.Copy, ['wraw%d' % s3], ['wb%d' % s])

        load_w_dma(0)
        load_w_round(0)
        load_w_dma(1)
        pzi = 0; rawi = 0; stgi = 0
        for idx, (blk, cb, first) in enumerate(items):
            ub = uT[blk % 2]; ubn = 'uT%d' % (blk % 2)
            own = blk >= 4; kv = blk >= 2
            if first:
                for j in range(8):
                    row0 = blk * 1024 + j * 128
                    x_ = xt[j % 2]; xn = 'xt%d' % (j % 2)
                    k.dma(x_[:], D["xloc"][row0:row0 + 128, :], [], [xn])
                    k.rms_rstd(x_[:], junk[:], ss[:], rstd[:], 1024, EPS, [xn], ('junk', 'ss', 'rstd'))
                    k.ts('dve', xs[:], x_[:], rstd[:, 0:1], None, ALU.mult, None, [xn, 'rstd'], ['xs'])
                    for c in range(8):
                        k.tr(pt[c // 4][:, (c % 4) * 128:(c % 4 + 1) * 128], xs[:, c * 128:(c + 1) * 128], ['xs'], ['pt%d' % (c // 4)])
                    for hf in range(2):
                        k.tt('dve', ub[:, hf * 4:(hf + 1) * 4, j * 128:(j + 1) * 128].bitcast(F32R),
                             pt[hf][:].rearrange("p (c t) -> p c t", c=4),
                             gpre[:, hf * 4:(hf + 1) * 4].unsqueeze(2).to_broadcast([128, 4, 128]), ALU.mult,
                             ['pt%d' % hf, 'gpre'], [ubn + '_%d' % j])
                if kv:
                    r0 = (blk - 2) * 1024
                    k.dma(cblk[:], D["cosT"][r0:r0 + 1024, :].rearrange("(j p) n -> p j n", p=128), [], ['cblk'])
                    k.dma(sblk[:], D["sinT"][r0:r0 + 1024, :].rearrange("(j p) n -> p j n", p=128), [], ['sblk'])
            if idx + 2 < len(items):
                load_w_dma(idx + 2)
            c0, ncols, kind, dcol = COLBLOCKS[cb]
            ws = idx % 2
            for j in range(8):
                if j == 4 and idx + 1 < len(items):
                    load_w_round(idx + 1)
                pzn = pzi % 3; pzi += 1
                for c in range(8):
                    k.mm(pz[pzn][:, 0:ncols], ub[:, c, j * 128:(j + 1) * 128], wb[ws][:, c, 0:ncols], c == 0, c == 7,
                         [ubn + '_%d' % j, 'wb%d' % ws], ['pz%d' % pzn], fast=True)
                si = stgi % 3; stgi += 1
                sg = stg[si]; sgn = 'stg%d' % si
                lrow = blk * 1024 + j * 128
                if kind in ('Q', 'K'):
                    ri = rawi % 2; rawi += 1
                    rw = raw[ri]; rwn = 'raw%d' % ri
                    k.act(rw[:, 0:ncols], pz[pzn][:, 0:ncols], AF.Copy, ['pz%d' % pzn], [rwn])
                    nh = ncols // 64
                    r3 = rw[:, 0:ncols].rearrange("p (h two f) -> p h two f", two=2, f=32)
                    s3 = sg[:, 0:ncols].rearrange("p (h two f) -> p h two f", two=2, f=32)
                    x1, x2 = r3[:, :, 0, :], r3[:, :, 1, :]
                    c3 = cblk[:, j, 0:nh * 32].rearrange("p (h f) -> p h f", f=32)
                    n3 = sblk[:, j, 0:nh * 32].rearrange("p (h f) -> p h f", f=32)
                    tv = [t[:, 0:nh * 32].rearrange("p (h f) -> p h f", f=32) for t in tmp]
                    k.tt('dve', tv[0], x1, c3, ALU.mult, [rwn, 'cblk'], ['tmp0'])
                    k.tt('pool', tv[1], x2, n3, ALU.mult, [rwn, 'sblk'], ['tmp1'])
                    k.tt('dve', s3[:, :, 0, :], tv[0], tv[1], ALU.subtract, ['tmp0', 'tmp1'], [sgn])
                    k.tt('pool', tv[2], x2, c3, ALU.mult, [rwn, 'cblk'], ['tmp2'])
                    k.tt('dve', tv[3], x1, n3, ALU.mult, [rwn, 'sblk'], ['tmp3'])
                    k.tt('pool', s3[:, :, 1, :], tv[2], tv[3], ALU.add, ['tmp2', 'tmp3'], [sgn])
                    if kind == 'Q':
                        o0 = lrow - NPRE
                        k.dma(D["S_q"][o0:o0 + 128, dcol:dcol + ncols], sg[:, 0:ncols], [sgn], ['S_q'])
                    else:
                        o0 = lrow - KV0
                        k.dma(D["S_k"][o0:o0 + 128, dcol:dcol + ncols], sg[:, 0:ncols], [sgn], ['S_k'])
                elif kind == 'V':
                    k.act(sg[:, 0:ncols], pz[pzn][:, 0:ncols], AF.Copy, ['pz%d' % pzn], [sgn])
                    o0 = lrow - KV0
                    k.dma(D["S_v"][o0:o0 + 128, dcol:dcol + ncols], sg[:, 0:ncols], [sgn], ['S_v'])
                elif kind == 'R':
                    k.act(sg[:, 0:ncols], pz[pzn][:, 0:ncols], AF.Copy, ['pz%d' % pzn], [sgn])
                    k.dma(D["S_r"][1 + lrow:1 + lrow + 128, dcol:dcol + ncols], sg[:, 0:ncols], [sgn], ['S_r'])
                else:
                    k.act(sg[:, 0:ncols], pz[pzn][:, 0:ncols], AF.Sigmoid, ['pz%d' % pzn], [sgn])
                    o0 = lrow - NPRE
                    k.dma(D["S_g"][o0:o0 + 128, dcol:dcol + ncols], sg[:, 0:ncols], [sgn], ['S_g'])


def phase_b(nc, k, D):
    S = k.S
    with ExitStack() as es:
        T = lambda name, shape: es.enter_context(nc.sbuf_tensor(name, shape, F32))
        P = lambda name: es.enter_context(nc.psum_tensor(name, [128, 512], F32))
        acc = T("b_acc", [128, 4, NO])
        kt = [T("b_kt%d" % i, [128, 256]) for i in range(3)]
        vpad = [T("b_vpad%d" % i, [128, 4, 128]) for i in range(3)]
        ktp = [T("b_ktp%d" % i, [128, 4, 128]) for i in range(3)]
        qt = [T("b_qt%d" % i, [128, 256]) for i in range(2)]
        qT = [T("b_qT%d" % i, [128, 2, 128]) for i in range(2)]
        pexp = [T("b_pexp%d" % i, [128, 4, 256]) for i in range(2)]
        pm = [T("b_pm%d" % i, [128, 4, 256]) for i in range(2)]
        maskA = T("b_maskA", [128, 4, 256]); onesPad = T("b_onesPad", [128, 2, 128]); onesPre = T("b_onesPre", [128, 2, 128])
        ptr = P("b_ptr"); ptrq = P("b_ptrq")
        pss = [[P("b_ps%d_%d" % (i, j)) for j in range(2)] for i in range(2)]
        pso = [P("b_pso%d" % i) for i in range(2)]
        k.dma(maskA[:], D["maskA4"].rearrange("p (h n) -> p h n", h=4), [], ['maskA'])
        k.dma(onesPad[:], D["onesPad"].rearrange("p (e n) -> p e n", e=2), [], ['onesPad'])
        k.dma(onesPre[:], D["onesPre"].rearrange("p (e n) -> p e n", e=2), [], ['onesPre'])
        for i in range(3):
            S.op('pool', lambda e, i=i: e.memset(vpad[i][:], 0.0), [], ['vpad%d' % i])
            S.op('pool', lambda e, i=i: e.memset(ktp[i][:], 0.0), [], ['ktp%d' % i])
        ptrv = ptr[:].rearrange("p (c t) -> p c t", c=4)
        ptrqv = ptrq[:].rearrange("p (c t) -> p c t", c=4)
        iters = []
        it = 0
        for g, d in enumerate((1, 4, 16)):
            nsub = 32 // d
            for r in range(d):
                for n in range(-1, nsub):
                    qs = None
                    if n >= 0:
                        qs = it % 2; it += 1
                    iters.append((g, d, r, n, qs))

        def front(i):
            g, d, r, n, qs = iters[i]
            slot = (n + 1) % 3
            start = KV0 + n * 128 * d + r
            rows = slice(start, start + 127 * d + 1, d)
            k.dma(kt[slot][:], D["S_k"][rows, g * 256:(g + 1) * 256], ['S_k'], ['kt%d' % slot])
            vsrc = D["S_v"][rows, g * 256:(g + 1) * 256].rearrange("k (p e d) -> k p e d", p=2, e=2)
            vdst = vpad[slot][:].rearrange("k (p e) (f d) -> k p e f d", e=2, f=2)
            for e_ in range(2):
                k.dma(vdst[:, :, e_, e_, :], vsrc[:, :, e_, :], ['S_v'], ['vpad%d' % slot])
            for p in range(2):
                k.tr(ptrv[:, p, :], kt[slot][:, p * 128:(p + 1) * 128], ['kt%d' % slot], ['pkb'])
            kdst = ktp[slot][:].rearrange("k (p e) s -> k p e s", e=2)
            k.act(kdst[0:64, :, 0, :], ptrv[0:64, 0:2, :], AF.Copy, ['pkb'], ['ktp%d' % slot])
            k.S.op('dve', lambda e, kdst=kdst: e.tensor_copy(out=kdst[64:128, :, 1, :], in_=ptrv[64:128, 0:2, :]), ['pkb'], ['ktp%d' % slot])
            if n < 0:
                return
            q0 = n * 128 * d + r
            toks = slice(q0, q0 + 127 * d + 1, d)
            k.dma(qt[qs][:], D["S_q"][toks, g * 256:(g + 1) * 256], ['S_q'], ['qt%d' % qs])
            for p in range(2):
                k.tr(ptrqv[:, p, :], qt[qs][:, p * 128:(p + 1) * 128], ['qt%d' % qs], ['pqb'])
            k.S.op('dve', lambda e, qs=qs: e.tensor_copy(out=qT[qs][:], in_=ptrqv[:, 0:2, :]), ['pqb'], ['qT%d' % qs])
            prev, cur = n % 3, slot
            for h in range(4):
                p = h // 2
                bank = pss[qs][h // 2]; bn = 'pss%d_%d' % (qs, h // 2)
                bv = bank[:].rearrange("k (h n) -> k h n", h=2)
                k.mm(bv[:, h % 2, 0:128], ktp[prev][:, h, :], qT[qs][:, p, :], True, True, ['ktp%d' % prev, 'qT%d' % qs], [bn])
                k.mm(bv[:, h % 2, 128:256], ktp[cur][:, h, :], qT[qs][:, p, :], True, True, ['ktp%d' % cur, 'qT%d' % qs], [bn])

        def back(i):
            g, d, r, n, qs = iters[i]
            if n < 0:
                return
            slot = (n + 1) % 3
            prev, cur = n % 3, slot
            q0 = n * 128 * d + r
            toks = slice(q0, q0 + 127 * d + 1, d)
            for hb in range(2):
                k.act(pexp[qs][:, 2 * hb:2 * hb + 2, :], pss[qs][hb][:].rearrange("k (h n) -> k h n", h=2), AF.Exp,
                      ['pss%d_%d' % (qs, hb)], ['pexp%d' % qs], scale=0.125)
            k.tt('pool', pm[qs][:], pexp[qs][:], maskA[:], ALU.mult, ['pexp%d' % qs, 'maskA'], ['pm%d' % qs])
            po = pso[qs]; pon = 'pso%d' % qs
            pov = po[:].rearrange("k (c t) -> k c t", c=4)
            first = True
            nmm = 0
            for p in range(2):
                for e_ in range(2):
                    h = 2 * p + e_
                    for which, sl in ((0, prev), (1, cur)):
                        ones = onesPre if (which == 0 and n == 0) else onesPad
                        rhs = pm[qs][:, h, which * 128:(which + 1) * 128]
                        nmm += 2
                        k.mm(pov[:, p, :], vpad[sl][:, h, :], rhs, first, False, ['vpad%d' % sl, 'pm%d' % qs], [pon])
                        first = False
                        k.mm(pov[:, 2 + p, :], ones[:, e_, :], rhs, False, nmm == 16, ['onesPad', 'onesPre', 'pm%d' % qs], [pon])
            if g == 0:
                k.act(acc[:, :, toks], pov, AF.Copy, [pon], ['acc'])
            else:
                k.tt('dve', acc[:, :, toks], acc[:, :, toks], pov, ALU.add, [pon, 'acc'], ['acc'])

        def capture(fn, i):
            S.begin_capture()
            fn(i)
            return S.end_capture()

        def merge(lb, la):
            out = []
            ia = 0
            nb, na = len(lb), len(la)
            for ib, item in enumerate(lb):
                out.append(item)
                tgt = ((ib + 1) * na) // max(nb, 1)
                while ia < tgt:
                    out.append(la[ia]); ia += 1
            out.extend(la[ia:])
            return out

        S.replay(capture(front, 0))
        for i in range(len(iters)):
            lb = capture(back, i)
            la = capture(front, i + 1) if i + 1 < len(iters) else []
            S.replay(merge(lb, la))
        for q4 in range(4):
            sl = slice(q4 * 1024, (q4 + 1) * 1024)
            k.S.op('dve', lambda e, sl=sl: e.reciprocal(out=acc[:, 2:4, sl], in_=acc[:, 2:4, sl]), ['acc'], ['acc'])
            k.tt('dve', acc[:, 0:2, sl], acc[:, 0:2, sl], acc[:, 2:4, sl], ALU.mult, ['acc'], ['acc'])
        k.dma(D["S_oaT"].rearrange("(p k) t -> k p t", p=2), acc[:, 0:2, :], ['acc'], ['S_oaT'])


def phase_c(nc, k, D):
    S = k.S
    NCH = int(os.environ.get('C_NCH', NL // 128))
    STG = int(os.environ.get('C_STAGE', 99))
    SUB = int(os.environ.get('C_SUB', 99))
    with ExitStack() as es:
        T = lambda name, shape: es.enter_context(nc.sbuf_tensor(name, shape, F32))
        P = lambda name: es.enter_context(nc.psum_tensor(name, [128, 512], F32))
        B = [P("c_b%d" % i) for i in range(8)]
        bn = ['cb%d' % i for i in range(8)]
        zc = [T("c_zc%d" % i, [128, 1696]) for i in range(2)]
        zp = T("c_zp", [128, 1696])
        bc = {}
        mu = T("c_mu", [128, 1696])
        k.dma(mu[:], D["mu"].partition_broadcast(128), [], ['cst'])
        for nm in ("w0", "a0", "k_k", "k_a", "ln_w", "ln_b", "r_k"):
            bc[nm] = T("c_" + nm, [128, 512])
            k.dma(bc[nm][:], D[nm].partition_broadcast(128), [], ['cst'])
        w2pad = T("c_w2pad", [64, 512]); a2pad = T("c_a2pad", [64, 512]); g2 = T("c_g2", [96, 512])
        for dst_, nm_, np_ in ((w2pad, "w2pad", 64), (a2pad, "a2pad", 64), (g2, "g2", 96)):
            k.dma(zp[0:np_, 0:512], D[nm_][:, :], [], ['zp'])
            k.act(dst_[:].bitcast(F32R), zp[0:np_, 0:512], AF.Copy, ['zp'], ['cst'])
        maskR = T("c_maskR", [128, 512]); tri = T("c_tri", [128, 128]); bm4 = T("c_bm4", [128, 4, 128]); onec = T("c_onec", [128, 1])
        k.dma(maskR[:], D["maskR"][:, :], [], ['cst']); k.dma(tri[:], D["tri"][:, :], [], ['cst'])
        k.dma(bm4[:], D["blockmask4"].rearrange("p (c n) -> p c n", c=4), [], ['cst'])
        S.op('pool', lambda e: e.memset(onec[:], 1.0), [], ['onec'])
        lor = T("c_lor", [128, 256]); lorT1 = T("c_lorT1", [64, 128]); lorT2 = T("c_lorT2", [96, 128])
        names = ["dws", "lw", "aa", "kk", "kkn", "k2", "bv", "rkr", "ecum", "einv", "eex", "rt", "ysb", "yc"]
        t = {nm: T("c_" + nm, [128, 512]) for nm in names}
        atd = [T("c_at%d" % i, [128, 512]) for i in range(2)]
        bonus2 = [T("c_bonus%d" % i, [128, 512]) for i in range(2)]
        gsb2 = [T("c_gsb%d" % i, [128, 512]) for i in range(2)]
        btd = [T("c_btd%d" % i, [128, 512]) for i in range(2)]
        ktd = [T("c_ktd%d" % i, [128, 512]) for i in range(2)]
        sm = {nm: T("c_" + nm, [128, 8]) for nm in ("ss8", "nrm", "rs8", "s1", "s2")}
        art = [T("c_art%d" % i, [128, 4, 2, 128]) for i in range(2)]
        btTp = T("c_btTp", [128, 8, 128]); ktTp = T("c_ktTp", [128, 8, 128])
        A1 = T("c_A1", [128, 8, 2, 128])
        A2 = [T("c_A2%d" % i, [128, 8, 2, 128]) for i in range(2)]
        Lb = [T("c_L%d" % i, [128, 8, 128]) for i in range(2)]
        LTb = [T("c_LT%d" % i, [128, 8, 128]) for i in range(2)]
        Mb = [T("c_M%d" % i, [128, 8, 128]) for i in range(2)]
        WT = [T("c_WT%d" % i, [128, 4, 128]) for i in range(2)]
        X0 = [T("c_X0%d" % i, [128, 512]) for i in range(2)]
        U0 = [T("c_U0%d" % i, [128, 512]) for i in range(2)]
        U = [T("c_U%d" % i, [128, 512]) for i in range(2)]
        Tb = [T("c_T%d" % i, [128, 4, 128]) for i in range(2)]
        maskP = [T("c_maskP%d" % i, [128, 4, 128]) for i in range(2)]
        pc = T("c_pc", [128, 4])
        orT = T("c_orT", [128, 4, 128])
        S.op('pool', lambda e: e.memset(Tb[0][:], 0.0), [], ['T0'])
        S.op('pool', lambda e: e.memset(lor[:], 0.0), [], ['lor'])
        S.op('pool', lambda e: e.memset(btTp[:], 0.0), [], ['btTp'])
        S.op('pool', lambda e: e.memset(ktTp[:], 0.0), [], ['ktTp'])
        S.op('dve', lambda e: e.tensor_copy(out=btTp[:].bitcast(F32R), in_=btTp[:]), ['btTp'], ['btTp'])
        S.op('dve', lambda e: e.tensor_copy(out=ktTp[:].bitcast(F32R), in_=ktTp[:]), ['ktTp'], ['ktTp'])
        h8 = lambda ap: ap.rearrange("p (h d) -> p h d", h=8)
        bc8 = lambda ap: ap.unsqueeze(2).to_broadcast([128, 8, 64])
        c4 = lambda ap: ap.rearrange("p (c t) -> p c t", c=4)

        def stage_a(c):
            s = c % 2
            own = c >= NCH // 2
            t0 = c * 128
            z = zc[s]; zn = 'zc%d' % s
            k.dma(z[:], D["S_r"][1 + t0:1 + t0 + 128, :], ['S_r', 'S_r0'], [zn])
            k.dma(zp[:], D["S_r"][t0:t0 + 128, :], ['S_r', 'S_r0'], ['zp'])
            k.tt('pool', zp[:], zp[:], z[:], ALU.subtract, ['zp', zn], ['zp'])
            k.tt('pool', zp[:], zp[:], mu[:], ALU.mult, ['zp', 'cst'], ['zp'])
            k.tt('dve', z[:], z[:], zp[:], ALU.add, ['zp', zn], [zn])
            r_, k_, v_ = z[:, 0:512], z[:, 512:1024], z[:, 1024:1536]
            k.act(lor[:, 0:32], z[:, 1536:1568], AF.Tanh, [zn], ['lor'])
            k.act(lor[:, 32:64], z[:, 1568:1600], AF.Copy, [zn], ['lor'])
            k.act(lor[:, 128:224], z[:, 1600:1696], AF.Sigmoid, [zn], ['lor'])
            k.tr(B[0][:, 0:128], lor[:, 0:128], ['lor'], [bn[0]])
            k.tr(B[0][:, 128:256], lor[:, 128:256], ['lor'], [bn[0]])
            k.act(lorT1[:].bitcast(F32R), B[0][0:64, 0:128], AF.Copy, [bn[0]], ['lorT1'])
            k.S.op('dve', lambda e: e.tensor_copy(out=lorT2[:].bitcast(F32R), in_=B[0][0:96, 128:256]), [bn[0]], ['lorT2'])
            k.mm(B[1][:], lorT1[:], w2pad[:], True, True, ['lorT1', 'cst'], [bn[1]], fast=True)
            k.mm(B[2][:], lorT1[:], a2pad[:], True, True, ['lorT1', 'cst'], [bn[2]], fast=True)
            gs = gsb2[s]; gsn = 'gsb%d' % s
            if own:
                k.mm(B[0][:], lorT2[:], g2[:], True, True, ['lorT2', 'cst'], [bn[0]], fast=True)
                k.act(gs[:], B[0][:], AF.Copy, [bn[0]], [gsn])
            k.tt('dve', t["dws"][:], B[1][:], bc["w0"][:], ALU.add, [bn[1], 'cst'], ['dws'])
            k.act(t["lw"][:], t["dws"][:], AF.Sigmoid, ['dws'], ['lw'])
            k.S.op('act', lambda e: e.mul(out=t["lw"][:], in_=t["lw"][:], mul=-float(np.exp(-0.5))), ['lw'], ['lw'])
            k.tt('dve', t["dws"][:], B[2][:], bc["a0"][:], ALU.add, [bn[2], 'cst', 'lw'], ['dws'])
            k.act(t["aa"][:], t["dws"][:], AF.Sigmoid, ['dws'], ['aa'])
            k.tt('pool', t["kk"][:], k_, bc["k_k"][:], ALU.mult, [zn, 'cst'], ['kk'])
            k.tt('pool', t["rkr"][:], t["kk"][:], t["kk"][:], ALU.mult, ['kk'], ['rkr'])
            k.S.op('dve', lambda e: e.tensor_reduce(out=sm["ss8"][:], in_=h8(t["rkr"][:]), axis=AX.X, op=ALU.add), ['rkr'], ['ss8'])
            k.act(sm["nrm"][:], sm["ss8"][:], AF.Sqrt, ['ss8'], ['nrm'])
            k.ts('dve', sm["nrm"][:], sm["nrm"][:], 1e-12, None, ALU.max, None, ['nrm'], ['nrm'])
            k.S.op('dve', lambda e: e.reciprocal(out=sm["nrm"][:], in_=sm["nrm"][:]), ['nrm'], ['nrm'])
            k.tt('dve', h8(t["kkn"][:]), h8(t["kk"][:]), bc8(sm["nrm"][:]), ALU.mult, ['kk', 'nrm'], ['kkn'])
            k.stt(t["k2"][:], t["aa"][:], -1.0, bc["k_a"][:], ALU.add, ALU.mult, ['aa', 'cst'], ['k2'])
            k.stt(t["k2"][:], t["k2"][:], 1.0, k_, ALU.add, ALU.mult, ['k2', zn], ['k2'])
            k.tt('pool', t["bv"][:], t["kkn"][:], t["aa"][:], ALU.mult, ['kkn', 'aa'], ['bv'])
            bo = bonus2[s]; bon = 'bonus%d' % s
            if own:
                k.tt('pool', t["rkr"][:], r_, t["k2"][:], ALU.mult, [zn, 'k2', 'ss8'], ['rkr'])
                k.tt('pool', t["rkr"][:], t["rkr"][:], bc["r_k"][:], ALU.mult, ['rkr', 'cst'], ['rkr'])
                k.S.op('dve', lambda e: e.tensor_reduce(out=sm["rs8"][:], in_=h8(t["rkr"][:]), axis=AX.X, op=ALU.add), ['rkr'], ['rs8'])
                k.tt('dve', h8(bo[:]), h8(v_), bc8(sm["rs8"][:]), ALU.mult, [zn, 'rs8'], [bon])
            k.mm(B[1][:], tri[:], t["lw"][:], True, True, ['cst', 'lw'], [bn[1]])
            k.act(t["ecum"][:], B[1][:], AF.Exp, [bn[1]], ['ecum'])
            k.act(t["einv"][:], B[1][:], AF.Exp, [bn[1]], ['einv'], scale=-1.0)
            k.tt('dve', t["dws"][:], B[1][:], t["lw"][:], ALU.subtract, [bn[1], 'lw', 'aa'], ['dws'])
            k.act(t["eex"][:], t["dws"][:], AF.Exp, ['dws'], ['eex'])
            for p in range(4):
                k.mm(B[0][:, 256 + p:257 + p], t["lw"][:, p * 128:(p + 1) * 128], onec[:], True, True, ['lw', 'onec'], [bn[0]])
            k.act(pc[:], B[0][:, 256:260], AF.Exp, [bn[0]], ['pc'])
            mP = maskP[s]; mPn = 'maskP%d' % s
            k.tt('pool', mP[:], bm4[:], pc[:].unsqueeze(2).to_broadcast([128, 4, 128]), ALU.mult, ['cst', 'pc'], [mPn])
            bt_, kt_, at_ = btd[s], ktd[s], atd[s]
            btn, ktn, atn = 'btd%d' % s, 'ktd%d' % s, 'atd%d' % s
            k.tt('dve', t["rt"][:], r_, t["ecum"][:], ALU.mult, [zn, 'ecum'], ['rt'])
            k.stt(at_[:], t["kkn"][:], -1.0, t["eex"][:], ALU.mult, ALU.mult, ['kkn', 'eex'], [atn])
            k.tt('pool', bt_[:], t["bv"][:], t["einv"][:], ALU.mult, ['bv', 'einv'], [btn])
            k.tt('pool', kt_[:], t["k2"][:], t["einv"][:], ALU.mult, ['k2', 'einv'], [ktn])
            ar = art[s]; arn = 'art%d' % s
            for p in range(4):
                k.tr(B[0][:, p * 128:(p + 1) * 128], at_[:, p * 128:(p + 1) * 128], [atn], [bn[0]])
            k.act(ar[:, :, 0, :].bitcast(F32R), c4(B[0][:]), AF.Copy, [bn[0]], [arn])
            for p in range(4):
                k.tr(B[1][:, p * 128:(p + 1) * 128], t["rt"][:, p * 128:(p + 1) * 128], ['rt'], [bn[1]])
            k.S.op('dve', lambda e: e.tensor_copy(out=ar[:, :, 1, :].bitcast(F32R), in_=c4(B[1][:])), [bn[1]], [arn])
            for (src, srcn, dst, dn, bi) in ((bt_, btn, btTp, 'btTp', 2), (kt_, ktn, ktTp, 'ktTp', 0)):
                for p in range(4):
                    k.tr(B[bi][:, p * 128:(p + 1) * 128], src[:, p * 128:(p + 1) * 128], [srcn], [bn[bi]])
                dv = dst[:].rearrange("k (p e) s -> k p e s", e=2)
                sv = c4(B[bi][:])
                k.act(dv[0:64, :, 0, :].bitcast(F32R), sv[0:64, :, :], AF.Copy, [bn[bi]], [dn])
                k.S.op('dve', lambda e, dv=dv, sv=sv: e.tensor_copy(out=dv[64:128, :, 1, :].bitcast(F32R), in_=sv[64:128, :, :]), [bn[bi]], [dn])
            a2 = A2[s]; a2n = 'A2%d' % s
            mstrict = maskR[:, 0:128].unsqueeze(1).to_broadcast([128, 2, 128])
            mincl = maskR[:, 128:256].unsqueeze(1).to_broadcast([128, 2, 128])
            for h in range(8):
                p = h // 2
                bi = h % 3
                rhs = ar[:, p, :, :].rearrange("k a t -> k (a t)")
                k.mm(B[bi][:, 0:256], btTp[:, h, :], rhs, True, True, ['btTp', arn], [bn[bi]], fast=True)
                k.mm(B[bi][:, 256:512], ktTp[:, h, :], rhs, True, True, ['ktTp', arn], [bn[bi]], fast=True)
                bv4 = B[bi][:].rearrange("k (x a t) -> k x a t", x=2, a=2)
                k.tt('dve', A1[:, h, :, :], bv4[:, :, 0, :], mstrict, ALU.mult, [bn[bi], 'cst'], ['A1'])
                if own:
                    k.tt('dve', a2[:, h, :, :], bv4[:, :, 1, :], mincl, ALU.mult, [bn[bi], 'cst'], [a2n])
            for h in range(8):
                k.mm(B[0][:, h * 64:(h + 1) * 64], A1[:, h, 1, :], z[:, 1024 + h * 64:1024 + (h + 1) * 64], True, True, ['A1', zn], [bn[0]])
            k.act(X0[s][:], B[0][:], AF.Copy, [bn[0]], ['X0%d' % s])

        def stage_b(c):
            s = c % 2
            own = c >= NCH // 2
            z = zc[s]; zn = 'zc%d' % s
            ar = art[s]; arn = 'art%d' % s
            a2 = A2[s]; a2n = 'A2%d' % s
            bt_, kt_, at_ = btd[s], ktd[s], atd[s]
            btn, ktn, atn = 'btd%d' % s, 'ktd%d' % s, 'atd%d' % s
            mP = maskP[s]; mPn = 'maskP%d' % s
            gs = gsb2[s]; gsn = 'gsb%d' % s
            bo = bonus2[s]; bon = 'bonus%d' % s
            L, LT, M = Lb[0], LTb[0], Mb[0]
            k.S.op('pool', lambda e, L=L: e.tensor_copy(out=L[:], in_=A1[:, :, 0, :]), ['A1'], ['L0'])
            for hh in range(2):
                bb = 3 + hh
                for h in range(4 * hh, 4 * hh + 4):
                    k.tr(B[bb][:, (h % 4) * 128:(h % 4 + 1) * 128], A1[:, h, 0, :], ['A1'], [bn[bb]])
                k.act(LT[:, 4 * hh:4 * hh + 4, :], c4(B[bb][:]), AF.Copy, [bn[bb]], ['LT0'])
            k.tt('pool', M[:], L[:], k.ident[:].unsqueeze(1).to_broadcast([128, 8, 128]), ALU.add, ['L0', 'ident'], ['M0'])
            cur = 0
            for rnd in range(6):
                nx = 1 - cur
                L, LT, M = Lb[cur], LTb[cur], Mb[cur]
                Ln, LTn, Mn = Lb[nx], LTb[nx], Mb[nx]
                last = rnd == 5
                for hh in range(2):
                    bL, bLT, bM = (3, 4, 5) if hh == 0 else (6, 7, 5)
                    hs = range(4 * hh, 4 * hh + 4)
                    if not last:
                        for h in hs:
                            k.mm(B[bL][:, (h % 4) * 128:(h % 4 + 1) * 128], LT[:, h, :], L[:, h, :], True, True, ['L%d' % cur, 'LT%d' % cur], [bn[bL]])
                    for h in hs:
                        k.mm(B[bLT][:, (h % 4) * 128:(h % 4 + 1) * 128], L[:, h, :], LT[:, h, :], True, True, ['L%d' % cur, 'LT%d' % cur], [bn[bLT]])
                    if not last:
                        k.act(Ln[:, 4 * hh:4 * hh + 4, :], c4(B[bL][:]), AF.Copy, [bn[bL]], ['L%d' % nx])
                    k.S.op('dve', lambda e, LTn=LTn, hh=hh, bLT=bLT: e.tensor_copy(out=LTn[:, 4 * hh:4 * hh + 4, :], in_=c4(B[bLT][:])),
                           [bn[bLT]], ['LT%d' % nx])
                    for h in hs:
                        k.mm(B[bM][:, (h % 4) * 128:(h % 4 + 1) * 128], LTn[:, h, :], M[:, h, :], True, True, ['LT%d' % nx, 'M%d' % cur], [bn[bM]])
                    k.tt('dve', Mn[:, 4 * hh:4 * hh + 4, :], M[:, 4 * hh:4 * hh + 4, :], c4(B[bM][:]), ALU.add,
                         [bn[bM], 'M%d' % cur], ['M%d' % nx])
                cur = nx
            Minv = Mb[cur]; Mn_ = 'M%d' % cur
            for h in range(8):
                k.mm(B[3][:, h * 64:(h + 1) * 64], Minv[:, h, :], X0[s][:, h * 64:(h + 1) * 64], True, True, [Mn_, 'X0%d' % s], [bn[3]])
            k.act(U0[s][:], B[3][:], AF.Copy, [bn[3]], ['U0%d' % s])
            for h in range(8):
                p = h // 2
                bi = 6 + h // 4
                k.mm(B[bi][:, (h % 4) * 128:(h % 4 + 1) * 128], at_[:, p * 128:(p + 1) * 128], Minv[:, h, :], True, True, [atn, Mn_], [bn[bi]])
            for hh in range(2):
                sv = B[6 + hh][:].rearrange("k (p e t) -> k p e t", p=2, e=2)
                k.act(WT[s][0:64, 2 * hh:2 * hh + 2, :], sv[0:64, :, 0, :], AF.Copy, [bn[6 + hh]], ['WT%d' % s])
                k.S.op('dve', lambda e, sv=sv, hh=hh, s=s: e.tensor_copy(out=WT[s][64:128, 2 * hh:2 * hh + 2, :], in_=sv[64:128, :, 1, :]), [bn[6 + hh]], ['WT%d' % s])
            Tc, Tn = Tb[c % 2], Tb[(c + 1) % 2]
            Tcn, Tnn = 'T%d' % (c % 2), 'T%d' % ((c + 1) % 2)
            for p in range(4):
                k.mm(B[3][:, p * 128:(p + 1) * 128], WT[s][:, p, :], Tc[:, p, :], True, True, ['WT%d' % s, Tcn], [bn[3]])
            k.tt('dve', U[s][:], B[3][:], U0[s][:], ALU.add, [bn[3], 'U0%d' % s], ['U%d' % s])
            if own:
                first = True
                for p in range(4):
                    k.mm(B[5][:, p * 128:(p + 1) * 128], ar[:, p, 1, :], Tc[:, p, :], first, False, [arn, Tcn], [bn[5]])
                    first = False
                for h in range(8):
                    k.mm(B[5][:, h * 64:(h + 1) * 64], a2[:, h, 0, :], U[s][:, h * 64:(h + 1) * 64], False, False, [a2n, 'U%d' % s], [bn[5]])
                    k.mm(B[5][:, h * 64:(h + 1) * 64], a2[:, h, 1, :], z[:, 1024 + h * 64:1024 + (h + 1) * 64], False, h == 7, [a2n, zn], [bn[5]])
            k.mm(B[4][:], k.ident[:], Tc[:].rearrange("k p n -> k (p n)"), True, False, ['ident', Tcn], [bn[4]])
            for p in range(4):
                k.mm(B[4][:, p * 128:(p + 1) * 128], kt_[:, p * 128:(p + 1) * 128], z[:, 1024 + p * 128:1024 + (p + 1) * 128], False, False, [ktn, zn], [bn[4]])
            for p in range(4):
                k.mm(B[4][:, p * 128:(p + 1) * 128], bt_[:, p * 128:(p + 1) * 128], U[s][:, p * 128:(p + 1) * 128], False, p == 3, [btn, 'U%d' % s], [bn[4]])
            k.tt('dve', Tn[:].rearrange("k p n -> k (p n)"), B[4][:], mP[:].rearrange("k p n -> k (p n)"), ALU.mult, [bn[4], mPn], [Tnn])
            if own:
                y, yc = t["ysb"], t["yc"]
                k.act(y[:], B[5][:], AF.Copy, [bn[5]], ['ysb'])
                k.S.op('dve', lambda e: e.tensor_reduce(out=sm["s1"][:], in_=h8(y[:]), axis=AX.X, op=ALU.add), ['ysb'], ['s1'])
                k.ts('dve', sm["s1"][:], sm["s1"][:], -1.0 / 64, None, ALU.mult, None, ['s1'], ['s1'])
                k.tt('dve', h8(yc[:]), h8(y[:]), bc8(sm["s1"][:]), ALU.add, ['ysb', 's1'], ['yc'])
                k.tt('pool', y[:], yc[:], yc[:], ALU.mult, ['yc'], ['ysb'])
                k.S.op('dve', lambda e: e.tensor_reduce(out=sm["s2"][:], in_=h8(y[:]), axis=AX.X, op=ALU.add), ['ysb'], ['s2'])
                k.ts('dve', sm["s2"][:], sm["s2"][:], 1.0 / 64, GN_EPS, ALU.mult, ALU.add, ['s2'], ['s2'])
                k.act(sm["s2"][:], sm["s2"][:], AF.Sqrt, ['s2'], ['s2'])
                k.S.op('dve', lambda e: e.reciprocal(out=sm["s2"][:], in_=sm["s2"][:]), ['s2'], ['s2'])
                k.tt('dve', h8(yc[:]), h8(yc[:]), bc8(sm["s2"][:]), ALU.mult, ['yc', 's2'], ['yc'])
                k.tt('pool', yc[:], yc[:], bc["ln_w"][:], ALU.mult, ['yc', 'cst'], ['yc'])
                k.tt('pool', yc[:], yc[:], bc["ln_b"][:], ALU.add, ['yc', 'cst'], ['yc'])
                k.tt('pool', yc[:], yc[:], bo[:], ALU.add, ['yc', bon], ['yc'])
                k.tt('dve', yc[:], yc[:], gs[:], ALU.mult, ['yc', gsn], ['yc'])
                for p in range(4):
                    k.tr(B[5][:, p * 128:(p + 1) * 128], yc[:, p * 128:(p + 1) * 128], ['yc'], [bn[5]])
                k.act(orT[:], c4(B[5][:]), AF.Copy, [bn[5]], ['orT'])
                o0 = (c - NCH // 2) * 128
                k.dma(D["S_orT"][:, o0:o0 + 128].rearrange("(c p) t -> p c t", p=128), orT[:], ['orT'], ['S_orT'])

        def capture(fn, c):
            S.begin_capture()
            fn(c)
            return S.end_capture()

        def merge(lb, la):
            out = []
            ia = 0
            nb, na = len(lb), len(la)
            for ib, it in enumerate(lb):
                out.append(it)
                tgt = ((ib + 1) * na) // max(nb, 1)
                while ia < tgt:
                    out.append(la[ia]); ia += 1
            out.extend(la[ia:])
            return out

        S.replay(capture(stage_a, 0))
        for c in range(NCH):
            lb = capture(stage_b, c)
            la = capture(stage_a, c + 1) if c + 1 < NCH else []
            S.replay(merge(lb, la))


def phase_d(nc, k, D):
    S = k.S
    TB = 256
    NB = NO // TB
    with ExitStack() as es:
        T = lambda name, shape: es.enter_context(nc.sbuf_tensor(name, shape, F32))
        P = lambda name: es.enter_context(nc.psum_tensor(name, [128, 512], F32))
        B = [P("d_b%d" % i) for i in range(8)]
        bn = ['db%d' % i for i in range(8)]
        NW = 2
        wst = [T("d_wst%d" % i, [128, 4096]) for i in range(NW)]
        wraw = [T("d_wraw%d" % i, [128, 4096]) for i in range(3)]
        oaTr = T("d_oaTr", [128, 2, 128]); orTr = T("d_orTr", [128, 4, 128])
        wau = T("d_wau", [128, 2, 1024]); wru = T("d_wru", [128, 4, 1024]); wple = T("d_wple", [128, 2, 1024])
        k.dma(wraw[0][:, 0:2048].rearrange("p (c n) -> p c n", c=2), D["w_au"].rearrange("(c p) n -> p c n", p=128), [], ['wraw0'])
        k.act(wau[:].bitcast(F32R), wraw[0][:, 0:2048].rearrange("p (c n) -> p c n", c=2), AF.Copy, ['wraw0'], ['cst'])
        k.dma(wraw[0][:, 0:4096].rearrange("p (c n) -> p c n", c=4), D["w_ru"].rearrange("(c p) n -> p c n", p=128), [], ['wraw0'])
        k.act(wru[:].bitcast(F32R), wraw[0][:, 0:4096].rearrange("p (c n) -> p c n", c=4), AF.Copy, ['wraw0'], ['cst'])
        k.dma(wraw[0][:, 0:2048].rearrange("p (c n) -> p c n", c=2), D["w_ple"].rearrange("(c p) n -> p c n", p=128), [], ['wraw0'])
        k.act(wple[:].bitcast(F32R), wraw[0][:, 0:2048].rearrange("p (c n) -> p c n", c=2), AF.Copy, ['wraw0'], ['cst'])
        gffn = T("d_gffn", [128, 8]); k.dma(gffn[:], D["gffn"][:, :], [], ['cst'])
        gm = T("d_gm", [128, 1024]); gf = T("d_gf", [128, 1024]); gp = T("d_gp", [128, 1024])
        k.dma(gm[:], D["g_mpost"].partition_broadcast(128), [], ['cst'])
        k.dma(gf[:], D["g_fpost"].partition_broadcast(128), [], ['cst'])
        k.dma(gp[:], D["g_ppost"].partition_broadcast(128), [], ['cst'])
        hT = T("d_hT", [128, 32, TB]); fT = T("d_fT", [128, 8, TB])
        hres = [T("d_h%d" % i, [128, 1024]) for i in range(2)]
        gates = T("d_gates", [128, 2048])
        oaT = T("d_oaT", [128, 2, 128]); orT = T("d_orT", [128, 4, 128])
        mer = T("d_mer", [128, 1024]); mer2 = T("d_mer2", [128, 1024]); junk = T("d_junk", [128, 1024])
        xT = T("d_xT", [128, 8, 128])
        pt_ = T("d_pt", [128, 256]); pT = T("d_pT", [128, 2, 128])
        ss = T("d_ss", [128, 1]); rstd = T("d_rstd", [128, 1])
        rel = [T("d_rel%d" % i, [128, TB]) for i in range(2)]
        wi = [0]

        def wload(src_ap, shape_kind):
            s = wi[0] % NW; s2 = wi[0] % 3; wi[0] += 1
            if shape_kind == 'col':
                k.dma(wraw[s2][:].rearrange("p (c n) -> p c n", c=8), src_ap.rearrange("(c p) n -> p c n", p=128), [], ['wraw%d' % s2])
            else:
                k.dma(wraw[s2][:].rearrange("p (c n) -> p c n", c=4), src_ap.rearrange("(c p) n -> p c n", p=128), [], ['wraw%d' % s2])
            if (wi[0] % 2) == 0:
                k.act(wst[s][:].bitcast(F32R), wraw[s2][:], AF.Copy, ['wraw%d' % s2], ['wst%d' % s])
            else:
                k.S.op('dve', lambda e: e.tensor_copy(out=wst[s][:].bitcast(F32R), in_=wraw[s2][:]), ['wraw%d' % s2], ['wst%d' % s])
            return s

        def transpose8(src, srcn, dst3, dstn, scale_ap=None):
            for c in range(8):
                k.tr(B[4 + c // 4][:, (c % 4) * 128:(c % 4 + 1) * 128], src[:, c * 128:(c + 1) * 128], [srcn], [bn[4 + c // 4]])
            for hf in range(2):
                pv = B[4 + hf][:].rearrange("p (c t) -> p c t", c=4)
                if scale_ap is None:
                    if hf == 0:
                        k.act(dst3[:, 0:4, :].bitcast(F32R), pv, AF.Copy, [bn[4]], [dstn])
                    else:
                        k.S.op('dve', lambda e, pv=pv: e.tensor_copy(out=dst3[:, 4:8, :].bitcast(F32R), in_=pv), [bn[5]], [dstn])
                else:
                    k.tt('dve', dst3[:, hf * 4:(hf + 1) * 4, :].bitcast(F32R), pv, scale_ap[:, hf * 4:(hf + 1) * 4].unsqueeze(2).to_broadcast([128, 4, 128]),
                         ALU.mult, [bn[4 + hf], 'cst'], [dstn])

        def dense1024(lhs3, lhsn, nk, wslots_or_res, banks):
            for hf in range(2):
                for c in range(nk):
                    wap, wn = wslots_or_res(c, hf)
                    k.mm(B[banks[hf]][:], lhs3[:, c, :], wap, c == 0, c == nk - 1, [lhsn, wn], [bn[banks[hf]]], fast=True)

        for blk in range(NB):
            ws_out = [wload(D["w_out"][:, hf * 512:(hf + 1) * 512], 'col') for hf in range(2)]
            for j in range(TB // 128):
                o0 = blk * TB + j * 128
                h_ = hres[j]; hn = 'h%d' % j
                k.dma(h_[:], D["xloc"][NPRE + o0:NPRE + o0 + 128, :], [], [hn])
                k.dma(oaTr[:], D["S_oaT"][:, o0:o0 + 128].rearrange("(c p) t -> p c t", p=128), ['S_oaT'], ['oaTr'])
                k.dma(orTr[:], D["S_orT"][:, o0:o0 + 128].rearrange("(c p) t -> p c t", p=128), ['S_orT'], ['orTr'])
                k.S.op('dve', lambda e: e.tensor_copy(out=oaT[:].bitcast(F32R), in_=oaTr[:]), ['oaTr'], ['oaT'])
                k.S.op('dve', lambda e: e.tensor_copy(out=orT[:].bitcast(F32R), in_=orTr[:]), ['orTr'], ['orT'])
                k.dma(gates[:], D["S_g"][o0:o0 + 128, :], ['S_g'], ['gates'])
                dense1024(oaT, 'oaT', 2, lambda c, hf: (wau[:, c, hf * 512:(hf + 1) * 512], 'cst'), (0, 1))
                dense1024(orT, 'orT', 4, lambda c, hf: (wru[:, c, hf * 512:(hf + 1) * 512], 'cst'), (2, 3))
                for hf in range(2):
                    sl = slice(hf * 512, (hf + 1) * 512)
                    k.tt('dve', mer[:, sl], B[hf][:], gates[:, sl], ALU.mult, [bn[hf], 'gates'], ['mer'])
                    k.tt('dve', mer2[:, sl], B[2 + hf][:], gates[:, 1024 + hf * 512:1024 + (hf + 1) * 512], ALU.mult, [bn[2 + hf], 'gates'], ['mer2'])
                k.tt('pool', mer[:], mer[:], mer2[:], ALU.add, ['mer', 'mer2'], ['mer'])
                transpose8(mer, 'mer', xT, 'xT')
                dense1024(xT, 'xT', 8, lambda c, hf: (wst[ws_out[hf]][:].rearrange("p (c n) -> p c n", c=8)[:, c, :], 'wst%d' % ws_out[hf]), (6, 7))
                pso = B[6][:]
                k.act(junk[:, 0:512], B[6][:], AF.Square, [bn[6]], ['junk', 'ss'], accum_out=ss[:])
                k.act(junk[:, 512:1024], B[7][:], AF.Square, [bn[7]], ['junk', 'rstd'], accum_out=rstd[:])
                k.tt('dve', ss[:], ss[:], rstd[:], ALU.add, ['ss', 'rstd'], ['ss'])
                k.ts('dve', rstd[:], ss[:], 1.0 / 1024, EPS, ALU.mult, ALU.add, ['ss'], ['rstd'])
                k.act(rstd[:], rstd[:], AF.Sqrt, ['rstd'], ['rstd'])
                k.S.op('dve', lambda e: e.reciprocal(out=rstd[:], in_=rstd[:]), ['rstd'], ['rstd'])
                for hf in range(2):
                    sl = slice(hf * 512, (hf + 1) * 512)
                    k.stt(mer[:, sl], B[6 + hf][:], rstd[:, 0:1], gm[:, sl], ALU.mult, ALU.mult, [bn[6 + hf], 'rstd', 'cst'], ['mer'])
                k.tt('pool', h_[:], h_[:], mer[:], ALU.add, [hn, 'mer'], [hn])
                k.rms_rstd(h_[:], junk[:], ss[:], rstd[:], 1024, EPS, [hn], ('junk', 'ss', 'rstd'))
                k.ts('dve', mer2[:], h_[:], rstd[:, 0:1], None, ALU.mult, None, [hn, 'rstd'], ['mer2'])
                transpose8(mer2, 'mer2', fT[:, :, j * 128:(j + 1) * 128], 'fT', scale_ap=gffn)
            for cb in range(8):
                s = wload(D["w_ffi"][:, cb * 512:(cb + 1) * 512], 'col')
                wv = wst[s][:].rearrange("p (c n) -> p c n", c=8)
                for q in range(4):
                    ffc = cb * 4 + q
                    bi = 4 + ffc % 2
                    for c in range(8):
                        k.mm(B[bi][:, 0:TB], wv[:, c, q * 128:(q + 1) * 128], fT[:, c, :], c == 0, c == 7, ['wst%d' % s, 'fT'], [bn[bi]], fast=True)
                    rl = rel[ffc % 2]; rln = 'rel%d' % (ffc % 2)
                    k.act(rl[:], B[bi][:, 0:TB], AF.Relu, [bn[bi]], [rln])
                    k.tt('dve', hT[:, ffc, :].bitcast(F32R), rl[:], rl[:], ALU.mult, [rln], ['hT'])
            for rb in range(8):
                s = wload(D["w_ffo"][rb * 512:(rb + 1) * 512, :], 'row')
                wv = wst[s][:].rearrange("p (c n) -> p c n", c=4)
                for j in range(TB // 128):
                    for hf in range(2):
                        bi = 2 * j + hf
                        for q in range(4):
                            ffc = rb * 4 + q
                            k.mm(B[bi][:], hT[:, ffc, j * 128:(j + 1) * 128], wv[:, q, hf * 512:(hf + 1) * 512], ffc == 0, ffc == 31,
                                 ['hT', 'wst%d' % s], [bn[bi]], fast=True)
            ws_pg = [wload(D["w_pg"][:, hf * 512:(hf + 1) * 512], 'col') for hf in range(2)]
            for j in range(TB // 128):
                o0 = blk * TB + j * 128
                h_ = hres[j]; hn = 'h%d' % j
                b0, b1 = 2 * j, 2 * j + 1
                k.act(junk[:, 0:512], B[b0][:], AF.Square, [bn[b0]], ['junk', 'ss'], accum_out=ss[:])
                k.act(junk[:, 512:1024], B[b1][:], AF.Square, [bn[b1]], ['junk', 'rstd'], accum_out=rstd[:])
                k.tt('dve', ss[:], ss[:], rstd[:], ALU.add, ['ss', 'rstd'], ['ss'])
                k.ts('dve', rstd[:], ss[:], 1.0 / 1024, EPS, ALU.mult, ALU.add, ['ss'], ['rstd'])
                k.act(rstd[:], rstd[:], AF.Sqrt, ['rstd'], ['rstd'])
                k.S.op('dve', lambda e: e.reciprocal(out=rstd[:], in_=rstd[:]), ['rstd'], ['rstd'])
                for hf, bb in enumerate((b0, b1)):
                    sl = slice(hf * 512, (hf + 1) * 512)
                    k.stt(mer[:, sl], B[bb][:], rstd[:, 0:1], gf[:, sl], ALU.mult, ALU.mult, [bn[bb], 'rstd', 'cst'], ['mer'])
                k.tt('pool', h_[:], h_[:], mer[:], ALU.add, [hn, 'mer'], [hn])
                transpose8(h_, hn, xT, 'xT')
                dense1024(xT, 'xT', 8, lambda c, hf: (wst[ws_pg[hf]][:].rearrange("p (c n) -> p c n", c=8)[:, c, :], 'wst%d' % ws_pg[hf]), (6, 7))
                for hf in range(2):
                    k.act(mer2[:, hf * 512:(hf + 1) * 512], B[6 + hf][:], AF.Sigmoid, [bn[6 + hf]], ['mer2'])
                k.dma(pt_[:], D["pown"][o0:o0 + 128, :], [], ['pt'])
                for c in range(2):
                    k.tr(B[4][:, c * 128:(c + 1) * 128], pt_[:, c * 128:(c + 1) * 128], ['pt'], [bn[4]])
                k.act(pT[:].bitcast(F32R), B[4][:, 0:256].rearrange("p (c t) -> p c t", c=2), AF.Copy, [bn[4]], ['pT'])
                dense1024(pT, 'pT', 2, lambda c, hf: (wple[:, c, hf * 512:(hf + 1) * 512], 'cst'), (6, 7))
                for hf in range(2):
                    sl = slice(hf * 512, (hf + 1) * 512)
                    k.tt('dve', mer[:, sl], B[6 + hf][:], mer2[:, sl], ALU.mult, [bn[6 + hf], 'mer2'], ['mer'])
                k.rms_rstd(mer[:], junk[:], ss[:], rstd[:], 1024, EPS, ['mer'], ('junk', 'ss', 'rstd'))
                k.stt(mer[:], mer[:], rstd[:, 0:1], gp[:], ALU.mult, ALU.mult, ['mer', 'rstd', 'cst'], ['mer'])
                k.tt('pool', mer[:], mer[:], h_[:], ALU.add, ['mer', hn], ['mer'])
                k.dma(D["out"][o0:o0 + 128, :], mer[:], ['mer'], ['out'])


def _consts():
    i = np.arange(128)
    kq = i[:, None]; qq = i[None, :]
    mprev = (kq >= qq).astype(np.float32); mcur = (kq <= qq).astype(np.float32)
    maskA = np.concatenate([mprev, mcur], 1)
    maskA4 = np.tile(maskA, (1, 4))
    strict = (kq < qq).astype(np.float32); incl = (kq <= qq).astype(np.float32)
    maskR = np.concatenate([strict, incl, strict, incl], 1)
    tri = (kq <= qq).astype(np.float32)
    bm = ((kq // 64) == (qq // 64)).astype(np.float32)
    bm4 = np.tile(bm, (1, 4))
    onesPad = np.zeros((128, 2, 128), np.float32)
    onesPad[:, 0, 0:64] = 1.0; onesPad[:, 1, 64:128] = 1.0
    return dict(ident=np.eye(128, dtype=np.float32), maskA4=maskA4, maskR=maskR, tri=tri, blockmask4=bm4,
                onesPad=onesPad.reshape(128, 256))


def make_in_maps(inp):
    f = lambda a: np.ascontiguousarray(np.asarray(a, dtype=np.float32))
    x = f(inp['x']); p = f(inp['p'])
    cst = _consts()
    shared = dict(
        w_in=f(inp['w_in'][0]), w_au=f(inp['w_attn_up'][0]), w_ru=f(inp['w_rwkv_up'][0]), w_out=f(inp['w_out'][0]),
        w_ffi=f(inp['w_ff_in'][0]), w_ffo=f(inp['w_ff_out'][0]), w_ple=f(inp['w_ple'][0]), w_pg=f(inp['w_ple_gate'][0]),
        gpre=f(np.asarray(inp['mix_pre_norm'][0]).reshape(8, 128).T), gffn=f(np.asarray(inp['ffn_pre_norm'][0]).reshape(8, 128).T),
        g_mpost=f(np.asarray(inp['mix_post_norm'][0]).reshape(1, 1024)), g_fpost=f(np.asarray(inp['ffn_post_norm'][0]).reshape(1, 1024)),
        g_ppost=f(np.asarray(inp['ple_post_norm'][0]).reshape(1, 1024)),
        mu=f(np.asarray(inp['rwkv_mu'][0]).reshape(1, 1696)),
        w0=f(np.asarray(inp['rwkv_w0'][0]).reshape(1, 512)), a0=f(np.asarray(inp['rwkv_a0'][0]).reshape(1, 512)),
        k_k=f(np.asarray(inp['rwkv_k_k'][0]).reshape(1, 512)), k_a=f(np.asarray(inp['rwkv_k_a'][0]).reshape(1, 512)),
        ln_w=f(np.asarray(inp['rwkv_ln_w'][0]).reshape(1, 512)), ln_b=f(np.asarray(inp['rwkv_ln_b'][0]).reshape(1, 512)),
        r_k=f(np.asarray(inp['rwkv_r_k'][0]).reshape(1, 512)),
        g2=f(inp['rwkv_g2'][0]),
    )
    w2pad = np.zeros((64, 512), np.float32); w2pad[0:32] = np.asarray(inp['rwkv_w2'][0])
    a2pad = np.zeros((64, 512), np.float32); a2pad[32:64] = np.asarray(inp['rwkv_a2'][0])
    shared.update(w2pad=w2pad, a2pad=a2pad)
    shared.update({kk: vv for kk, vv in cst.items()})
    inv_freq = (10000.0 ** (-np.arange(0, 64, 2, dtype=np.float32) / np.float32(64))).astype(np.float32)
    maps = []
    for c in range(8):
        b, half = c // 2, c % 2
        xloc = np.zeros((NL, 1024), np.float32)
        if half == 0:
            xloc[NPRE:] = x[b, 0:4096]
        else:
            xloc[:] = x[b]
        pos = (half * 4096 - 2048 + np.arange(NK)).astype(np.float32)
        ang = (pos[:, None] * inv_freq[None, :]).astype(np.float32)
        m = dict(shared)
        m.update(xloc=xloc, pown=f(p[0, b, half * 4096:(half + 1) * 4096]),
                 cosT=f(np.tile(np.cos(ang), (1, 12))), sinT=f(np.tile(np.sin(ang), (1, 12))),
                 onesPre=f(cst['onesPad'] * float(half)))
        maps.append(m)
    return maps


def kernel(**inputs):
    nc = build_program()
    maps = make_in_maps(inputs)
    res = run_bass_kernel_spmd(nc, maps, core_ids=list(range(8)))
    out = np.zeros((4, 8192, 1024), np.float32)
    for c in range(8):
        b, half = c // 2, c % 2
        out[b, half * 4096:(half + 1) * 4096] = res.results[c]["out"]
    return out
```

```python
import os
import numpy as np
import concourse.bass as bass
import concourse.mybir as mybir
from concourse.bass_utils import run_bass_kernel_spmd
from contextlib import ExitStack

F32 = mybir.dt.float32
F32R = mybir.dt.float32r
AF = mybir.ActivationFunctionType
ALU = mybir.AluOpType
AX = mybir.AxisListType
ENGS = ('sp', 'pe', 'act', 'dve', 'pool')
DEBUG = None


class Sched:
    def __init__(self, nc, es, n_dma=8):
        self.nc = nc
        self.sem = {e: es.enter_context(nc.semaphore('s_' + e)) for e in ('pe', 'act', 'dve', 'pool')}
        self.dsem = [es.enter_context(nc.semaphore('sd%d' % i)) for i in range(n_dma)]
        self.cnt = {e: 0 for e in self.sem}
        self.dcnt = [0] * n_dma
        self.dnext = 0
        self.streams = {e: [] for e in ENGS}
        self.waited = {e: {} for e in ENGS}
        self.lastw = {}
        self.readers = {}
        self.nops = 0

    def _semh(self, k):
        return self.dsem[k[1]] if isinstance(k, tuple) else self.sem[k]

    def _deps(self, reads, writes):
        deps = {}

        def add(k, v):
            if deps.get(k, 0) < v:
                deps[k] = v
        for r in reads:
            t = self.lastw.get(r)
            if t:
                add(*t)
        for w in writes:
            t = self.lastw.get(w)
            if t:
                add(*t)
            for k, v in self.readers.get(w, {}).items():
                add(k, v)
        return deps

    def _record(self, tok, reads, writes):
        for r in reads:
            d = self.readers.setdefault(r, {})
            if d.get(tok[0], 0) < tok[1]:
                d[tok[0]] = tok[1]
        for w in writes:
            self.lastw[w] = tok
            self.readers[w] = {}

    def _waits(self, eng, deps):
        out = []
        wd = self.waited[eng]
        for k, v in deps.items():
            if eng == 'pe' and k == 'pe':
                continue
            if wd.get(k, 0) >= v:
                continue
            wd[k] = v
            out.append((k, v))
        return out

    def begin_capture(self):
        self.cap = []

    def end_capture(self):
        c = self.cap
        self.cap = None
        return c

    def replay(self, items):
        for it in items:
            if it[0] == 'op':
                self.op(it[1], it[2], it[3], it[4])
            else:
                self.dma(it[1], it[2], it[3], it[4], it[5], **it[6])

    def op(self, eng, fn, reads=(), writes=()):
        if getattr(self, 'cap', None) is not None:
            self.cap.append(('op', eng, fn, tuple(reads), tuple(writes)))
            return None
        ex = [r for r in reads if isinstance(r, str) and r[:2] in ('cb', 'db', 'pt', 'pz', 'ps', 'pk', 'pq') and r not in writes]
        if ex:
            writes = list(writes) + ex
        deps = self._deps(reads, writes)
        waits = self._waits(eng, deps)
        self.cnt[eng] += 1
        tok = (eng, self.cnt[eng])
        self.streams[eng].append((waits, fn, (eng, 1)))
        self._record(tok, reads, writes)
        self.nops += 1
        return tok

    def dma(self, out, in_, reads=(), writes=(), q='sp', **kw):
        if getattr(self, 'cap', None) is not None:
            self.cap.append(('dma', out, in_, tuple(reads), tuple(writes), q, kw))
            return None
        slot = self.dnext
        self.dnext = (self.dnext + 1) % len(self.dsem)
        deps = self._deps(reads, writes)
        k = ('d', slot)
        if self.dcnt[slot] > 0 and deps.get(k, 0) < self.dcnt[slot]:
            deps[k] = self.dcnt[slot]
        waits = self._waits(q, deps)
        self.dcnt[slot] += 16
        tok = (k, self.dcnt[slot])
        self.streams[q].append((waits, lambda e: e.dma_start(out=out, in_=in_, **kw), (k, 16)))
        self._record(tok, reads, writes)
        self.nops += 1
        return tok

    def barrier(self):
        toks = {e: c for e, c in self.cnt.items() if c > 0}
        for i, c in enumerate(self.dcnt):
            if c > 0:
                toks[('d', i)] = c
        for e in ENGS:
            waits = self._waits(e, {k: v for k, v in toks.items() if k != e})
            if waits:
                self.streams[e].append((waits, None, None))
        self.lastw = {}
        self.readers = {}

    def finish(self):
        toks = {}
        for i, c in enumerate(self.dcnt):
            if c > 0:
                toks[('d', i)] = c
        waits = self._waits('sp', toks)
        self.streams['sp'].append((waits, None, None))

    def flush(self):
        nc = self.nc
        with nc.Block() as block:
            for ename, deco in (('sp', block.sync), ('pe', block.tensor), ('act', block.scalar),
                                ('dve', block.vector), ('pool', block.gpsimd)):
                stream = self.streams[ename]

                def body(eng, stream=stream):
                    for waits, fn, inc in stream:
                        for (k, v) in waits:
                            eng.wait_ge(self._semh(k), v)
                        if fn is not None:
                            ins = fn(eng)
                            if inc is not None:
                                ins.then_inc(self._semh(inc[0]), inc[1])
                deco(body)
        self.streams = {e: [] for e in ENGS}


NL, NPRE, NO, KV0, NK = 8192, 4096, 4096, 2048, 6144
EPS = 1e-6
GN_EPS = 64e-5
WCOLS = 6048


class K:
    def __init__(self, nc, es):
        self.nc, self.es = nc, es
        self.S = Sched(nc, es)
        self.uid = 0

    def act(self, out, in_, func, r, w, **kw):
        self.S.op('act', lambda e: e.activation(out=out, in_=in_, func=func, **kw), r, w)

    def tt(self, eng, out, in0, in1, op, r, w):
        self.S.op(eng, lambda e: e.tensor_tensor(out=out, in0=in0, in1=in1, op=op), r, w)

    def ts(self, eng, out, in0, s1, s2, op0, op1, r, w):
        if s2 is None:
            self.S.op(eng, lambda e: e.tensor_scalar(out=out, in0=in0, scalar1=s1, scalar2=None, op0=op0), r, w)
        else:
            self.S.op(eng, lambda e: e.tensor_scalar(out=out, in0=in0, scalar1=s1, scalar2=s2, op0=op0, op1=op1), r, w)

    def stt(self, out, in0, scalar, in1, op0, op1, r, w):
        self.S.op('dve', lambda e: e.scalar_tensor_tensor(out=out, in0=in0, scalar=scalar, in1=in1, op0=op0, op1=op1), r, w)

    def mm(self, out, lhsT, rhs, start, stop, r, w, fast=False):
        if fast:
            lhsT = lhsT.bitcast(F32R)
            rhs = rhs.bitcast(F32R)
        self.S.op('pe', lambda e: e.matmul(out, lhsT=lhsT, rhs=rhs, start=start, stop=stop, skip_group_check=True), r, w)

    def tr(self, out, in_, r, w):
        ident = self.ident
        self.S.op('pe', lambda e: e.transpose(out=out, in_=in_, identity=ident[:]), list(r) + ['ident'], w)

    def dma(self, out, in_, r, w):
        self.S.dma(out, in_, r, w)

    def rms_rstd(self, src, junk, ss, rstd, n, eps, r, names):
        jn, sn, rn = names
        self.act(junk, src, AF.Square, r, [jn, sn], accum_out=ss)
        self.ts('dve', rstd, ss, 1.0 / n, eps, ALU.mult, ALU.add, [sn], [rn])
        self.act(rstd, rstd, AF.Sqrt, [rn], [rn])
        self.S.op('dve', lambda e: e.reciprocal(out=rstd, in_=rstd), [rn], [rn])


def build_program():
    nc = bass.Bass("TRN2", target_bir_lowering=False)
    D = {}

    def din(name, shape):
        D[name] = nc.dram_tensor(name, list(shape), F32, kind="ExternalInput").ap()

    def dscr(name, shape):
        kind = "ExternalOutput" if DEBUG else "Internal"
        D[name] = nc.dram_tensor(name, list(shape), F32, kind=kind).ap()

    din("xloc", [NL, 1024]); din("pown", [NO, 256])
    din("w_in", [1024, WCOLS]); din("w_au", [256, 1024]); din("w_ru", [512, 1024]); din("w_out", [1024, 1024])
    din("w_ffi", [1024, 4096]); din("w_ffo", [4096, 1024]); din("w_ple", [256, 1024]); din("w_pg", [1024, 1024])
    din("gpre", [128, 8]); din("gffn", [128, 8])
    din("g_mpost", [1, 1024]); din("g_fpost", [1, 1024]); din("g_ppost", [1, 1024])
    din("mu", [1, 1696])
    for nm in ("w0", "a0", "k_k", "k_a", "ln_w", "ln_b", "r_k"):
        din(nm, [1, 512])
    din("w2pad", [64, 512]); din("a2pad", [64, 512]); din("g2", [96, 512])
    din("cosT", [NK, 384]); din("sinT", [NK, 384])
    din("ident", [128, 128]); din("maskA4", [128, 1024]); din("maskR", [128, 512]); din("tri", [128, 128])
    din("blockmask4", [128, 512]); din("onesPad", [128, 256]); din("onesPre", [128, 256])
    D["out"] = nc.dram_tensor("out", [NO, 1024], F32, kind="ExternalOutput").ap()
    dscr("S_q", [NO, 768]); dscr("S_k", [NK, 768]); dscr("S_v", [NK, 768])
    dscr("S_r", [NL + 1, 1696]); dscr("S_g", [NO, 2048])
    dscr("S_oaT", [256, NO]); dscr("S_orT", [512, NO])
    if DEBUG:
        dscr("dbgC", [24, 128, 512])

    with ExitStack() as es0:
        k = K(nc, es0)
        S = k.S
        ident = es0.enter_context(nc.sbuf_tensor("ident_sb", [128, 128], F32))
        k.ident = ident
        S.dma(ident[:], D["ident"][:, :], writes=['ident'])
        phases = DEBUG or "ABCD"
        if "A" in phases:
            phase_a(nc, k, D)
            S.barrier(); S.flush()
        if "B" in phases:
            phase_b(nc, k, D)
            S.barrier(); S.flush()
        if "C" in phases:
            phase_c(nc, k, D)
            S.barrier(); S.flush()
        if "D" in phases:
            phase_d(nc, k, D)
        S.barrier()
        S.finish()
        S.flush()
    return nc


COLBLOCKS = {
    'Q0': (0, 512, 'Q', 0), 'Q1': (512, 256, 'Q', 512),
    'K0': (768, 512, 'K', 0), 'K1': (1280, 256, 'K', 512),
    'V0': (1536, 512, 'V', 0), 'V1': (2048, 256, 'V', 512),
    'R0': (2304, 512, 'R', 0), 'R1': (2816, 512, 'R', 512), 'R2': (3328, 512, 'R', 1024), 'R3': (3840, 160, 'R', 1536),
    'G0': (4000, 512, 'G', 0), 'G1': (4512, 512, 'G', 512), 'G2': (5024, 512, 'G', 1024), 'G3': (5536, 512, 'G', 1536),
}


def phase_a(nc, k, D):
    S = k.S
    with ExitStack() as es:
        T = lambda name, shape: es.enter_context(nc.sbuf_tensor(name, shape, F32))
        P = lambda name: es.enter_context(nc.psum_tensor(name, [128, 512], F32))
        xt = [T("a_xt%d" % i, [128, 1024]) for i in range(2)]
        junk = T("a_junk", [128, 1024]); xs = T("a_xs", [128, 1024])
        ss = T("a_ss", [128, 1]); rstd = T("a_rstd", [128, 1])
        uT = [T("a_uT%d" % i, [128, 8, 1024]) for i in range(2)]
        wb = [T("a_wb%d" % i, [128, 8, 512]) for i in range(2)]
        wraw = [T("a_wraw%d" % i, [128, 8, 512]) for i in range(3)]
        raw = [T("a_raw%d" % i, [128, 512]) for i in range(2)]
        stg = [T("a_stg%d" % i, [128, 512]) for i in range(3)]
        cblk = T("a_cblk", [128, 8, 384]); sblk = T("a_sblk", [128, 8, 384])
        tmp = [T("a_tmp%d" % i, [128, 256]) for i in range(4)]
        gpre = T("a_gpre", [128, 8]); zrow = T("a_zrow", [1, 1696])
        pt = [P("a_pt%d" % i) for i in range(2)]
        pz = [P("a_pz%d" % i) for i in range(3)]
        k.dma(gpre[:], D["gpre"][:, :], [], ['gpre'])
        S.op('pool', lambda e: e.memset(zrow[:], 0.0), [], ['zrow'])
        k.dma(D["S_r"][0:1, :], zrow[:], ['zrow'], ['S_r0'])

        items = []
        for blk in range(8):
            own = blk >= 4
            kv = blk >= 2
            cbs = []
            if own:
                cbs += ['Q0', 'Q1']
            if kv:
                cbs += ['K0', 'K1', 'V0', 'V1']
            cbs += ['R0', 'R1', 'R2', 'R3']
            if own:
                cbs += ['G0', 'G1', 'G2', 'G3']
            for i, cb in enumerate(cbs):
                items.append((blk, cb, i == 0))

        def load_w_dma(idx):
            blk, cb, _ = items[idx]
            c0, ncols, _, _ = COLBLOCKS[cb]
            s3 = idx % 3
            k.dma(wraw[s3][:, :, 0:ncols], D["w_in"][:, c0:c0 + ncols].rearrange("(c p) n -> p c n", p=128), [], ['wraw%d' % s3])

        def load_w_round(idx):
            blk, cb, _ = items[idx]
            c0, ncols, _, _ = COLBLOCKS[cb]
            s3 = idx % 3; s = idx % 2
            k.act(wb[s][:, :, 0:ncols].bitcast(F32R), wraw[s3][:, :, 0:ncols], AF.Copy, ['wraw%d' % s3], ['wb%d' % s])

        load_w_dma(0)
        load_w_round(0)
        load_w_dma(1)
        pzi = 0; rawi = 0; stgi = 0
        for idx, (blk, cb, first) in enumerate(items):
            ub = uT[blk % 2]; ubn = 'uT%d' % (blk % 2)
            own = blk >= 4; kv = blk >= 2
            if first:
                for j in range(8):
                    row0 = blk * 1024 + j * 128
                    x_ = xt[j % 2]; xn = 'xt%d' % (j % 2)
                    k.dma(x_[:], D["xloc"][row0:row0 + 128, :], [], [xn])
                    k.rms_rstd(x_[:], junk[:], ss[:], rstd[:], 1024, EPS, [xn], ('junk', 'ss', 'rstd'))
                    k.ts('dve', xs[:], x_[:], rstd[:, 0:1], None, ALU.mult, None, [xn, 'rstd'], ['xs'])
                    for c in range(8):
                        k.tr(pt[c // 4][:, (c % 4) * 128:(c % 4 + 1) * 128], xs[:, c * 128:(c + 1) * 128], ['xs'], ['pt%d' % (c // 4)])
                    for hf in range(2):
                        k.tt('dve', ub[:, hf * 4:(hf + 1) * 4, j * 128:(j + 1) * 128].bitcast(F32R),
                             pt[hf][:].rearrange("p (c t) -> p c t", c=4),
                             gpre[:, hf * 4:(hf + 1) * 4].unsqueeze(2).to_broadcast([128, 4, 128]), ALU.mult,
                             ['pt%d' % hf, 'gpre'], [ubn + '_%d' % j])
                if kv:
                    r0 = (blk - 2) * 1024
                    k.dma(cblk[:], D["cosT"][r0:r0 + 1024, :].rearrange("(j p) n -> p j n", p=128), [], ['cblk'])
                    k.dma(sblk[:], D["sinT"][r0:r0 + 1024, :].rearrange("(j p) n -> p j n", p=128), [], ['sblk'])
            if idx + 2 < len(items):
                load_w_dma(idx + 2)
            c0, ncols, kind, dcol = COLBLOCKS[cb]
            ws = idx % 2
            for j in range(8):
                if j == 4 and idx + 1 < len(items):
                    load_w_round(idx + 1)
                pzn = pzi % 3; pzi += 1
                for c in range(8):
                    k.mm(pz[pzn][:, 0:ncols], ub[:, c, j * 128:(j + 1) * 128], wb[ws][:, c, 0:ncols], c == 0, c == 7,
                         [ubn + '_%d' % j, 'wb%d' % ws], ['pz%d' % pzn], fast=True)
                si = stgi % 3; stgi += 1
                sg = stg[si]; sgn = 'stg%d' % si
                lrow = blk * 1024 + j * 128
                if kind in ('Q', 'K'):
                    ri = rawi % 2; rawi += 1
                    rw = raw[ri]; rwn = 'raw%d' % ri
                    k.act(rw[:, 0:ncols], pz[pzn][:, 0:ncols], AF.Copy, ['pz%d' % pzn], [rwn])
                    nh = ncols // 64
                    r3 = rw[:, 0:ncols].rearrange("p (h two f) -> p h two f", two=2, f=32)
                    s3 = sg[:, 0:ncols].rearrange("p (h two f) -> p h two f", two=2, f=32)
                    x1, x2 = r3[:, :, 0, :], r3[:, :, 1, :]
                    c3 = cblk[:, j, 0:nh * 32].rearrange("p (h f) -> p h f", f=32)
                    n3 = sblk[:, j, 0:nh * 32].rearrange("p (h f) -> p h f", f=32)
                    tv = [t[:, 0:nh * 32].rearrange("p (h f) -> p h f", f=32) for t in tmp]
                    k.tt('dve', tv[0], x1, c3, ALU.mult, [rwn, 'cblk'], ['tmp0'])
                    k.tt('pool', tv[1], x2, n3, ALU.mult, [rwn, 'sblk'], ['tmp1'])
                    k.tt('dve', s3[:, :, 0, :], tv[0], tv[1], ALU.subtract, ['tmp0', 'tmp1'], [sgn])
                    k.tt('pool', tv[2], x2, c3, ALU.mult, [rwn, 'cblk'], ['tmp2'])
                    k.tt('dve', tv[3], x1, n3, ALU.mult, [rwn, 'sblk'], ['tmp3'])
                    k.tt('pool', s3[:, :, 1, :], tv[2], tv[3], ALU.add, ['tmp2', 'tmp3'], [sgn])
                    if kind == 'Q':
                        o0 = lrow - NPRE
                        k.dma(D["S_q"][o0:o0 + 128, dcol:dcol + ncols], sg[:, 0:ncols], [sgn], ['S_q'])
                    else:
                        o0 = lrow - KV0
                        k.dma(D["S_k"][o0:o0 + 128, dcol:dcol + ncols], sg[:, 0:ncols], [sgn], ['S_k'])
                elif kind == 'V':
                    k.act(sg[:, 0:ncols], pz[pzn][:, 0:ncols], AF.Copy, ['pz%d' % pzn], [sgn])
                    o0 = lrow - KV0
                    k.dma(D["S_v"][o0:o0 + 128, dcol:dcol + ncols], sg[:, 0:ncols], [sgn], ['S_v'])
                elif kind == 'R':
                    k.act(sg[:, 0:ncols], pz[pzn][:, 0:ncols], AF.Copy, ['pz%d' % pzn], [sgn])
                    k.dma(D["S_r"][1 + lrow:1 + lrow + 128, dcol:dcol + ncols], sg[:, 0:ncols], [sgn], ['S_r'])
                else:
                    k.act(sg[:, 0:ncols], pz[pzn][:, 0:ncols], AF.Sigmoid, ['pz%d' % pzn], [sgn])
                    o0 = lrow - NPRE
                    k.dma(D["S_g"][o0:o0 + 128, dcol:dcol + ncols], sg[:, 0:ncols], [sgn], ['S_g'])


def phase_b(nc, k, D):
    S = k.S
    with ExitStack() as es:
        T = lambda name, shape: es.enter_context(nc.sbuf_tensor(name, shape, F32))
        P = lambda name: es.enter_context(nc.psum_tensor(name, [128, 512], F32))
        acc = T("b_acc", [128, 4, NO])
        kt = [T("b_kt%d" % i, [128, 256]) for i in range(3)]
        vpad = [T("b_vpad%d" % i, [128, 4, 128]) for i in range(3)]
        ktp = [T("b_ktp%d" % i, [128, 4, 128]) for i in range(3)]
        qt = [T("b_qt%d" % i, [128, 256]) for i in range(2)]
        qT = [T("b_qT%d" % i, [128, 2, 128]) for i in range(2)]
        pexp = [T("b_pexp%d" % i, [128, 4, 256]) for i in range(2)]
        pm = [T("b_pm%d" % i, [128, 4, 256]) for i in range(2)]
        maskA = T("b_maskA", [128, 4, 256]); onesPad = T("b_onesPad", [128, 2, 128]); onesPre = T("b_onesPre", [128, 2, 128])
        ptr = P("b_ptr"); ptrq = P("b_ptrq")
        pss = [[P("b_ps%d_%d" % (i, j)) for j in range(2)] for i in range(2)]
        pso = [P("b_pso%d" % i) for i in range(2)]
        k.dma(maskA[:], D["maskA4"].rearrange("p (h n) -> p h n", h=4), [], ['maskA'])
        k.dma(onesPad[:], D["onesPad"].rearrange("p (e n) -> p e n", e=2), [], ['onesPad'])
        k.dma(onesPre[:], D["onesPre"].rearrange("p (e n) -> p e n", e=2), [], ['onesPre'])
        for i in range(3):
            S.op('pool', lambda e, i=i: e.memset(vpad[i][:], 0.0), [], ['vpad%d' % i])
            S.op('pool', lambda e, i=i: e.memset(ktp[i][:], 0.0), [], ['ktp%d' % i])
        ptrv = ptr[:].rearrange("p (c t) -> p c t", c=4)
        ptrqv = ptrq[:].rearrange("p (c t) -> p c t", c=4)
        iters = []
        it = 0
        for g, d in enumerate((1, 4, 16)):
            nsub = 32 // d
            for r in range(d):
                for n in range(-1, nsub):
                    qs = None
                    if n >= 0:
                        qs = it % 2; it += 1
                    iters.append((g, d, r, n, qs))

        def front(i):
            g, d, r, n, qs = iters[i]
            slot = (n + 1) % 3
            start = KV0 + n * 128 * d + r
            rows = slice(start, start + 127 * d + 1, d)
            k.dma(kt[slot][:], D["S_k"][rows, g * 256:(g + 1) * 256], ['S_k'], ['kt%d' % slot])
            vsrc = D["S_v"][rows, g * 256:(g + 1) * 256].rearrange("k (p e d) -> k p e d", p=2, e=2)
            vdst = vpad[slot][:].rearrange("k (p e) (f d) -> k p e f d", e=2, f=2)
            for e_ in range(2):
                k.dma(vdst[:, :, e_, e_, :], vsrc[:, :, e_, :], ['S_v'], ['vpad%d' % slot])
            for p in range(2):
                k.tr(ptrv[:, p, :], kt[slot][:, p * 128:(p + 1) * 128], ['kt%d' % slot], ['pkb'])
            kdst = ktp[slot][:].rearrange("k (p e) s -> k p e s", e=2)
            k.act(kdst[0:64, :, 0, :], ptrv[0:64, 0:2, :], AF.Copy, ['pkb'], ['ktp%d' % slot])
            k.S.op('dve', lambda e, kdst=kdst: e.tensor_copy(out=kdst[64:128, :, 1, :], in_=ptrv[64:128, 0:2, :]), ['pkb'], ['ktp%d' % slot])
            if n < 0:
                return
            q0 = n * 128 * d + r
            toks = slice(q0, q0 + 127 * d + 1, d)
            k.dma(qt[qs][:], D["S_q"][toks, g * 256:(g + 1) * 256], ['S_q'], ['qt%d' % qs])
            for p in range(2):
                k.tr(ptrqv[:, p, :], qt[qs][:, p * 128:(p + 1) * 128], ['qt%d' % qs], ['pqb'])
            k.S.op('dve', lambda e, qs=qs: e.tensor_copy(out=qT[qs][:], in_=ptrqv[:, 0:2, :]), ['pqb'], ['qT%d' % qs])
            prev, cur = n % 3, slot
            for h in range(4):
                p = h // 2
                bank = pss[qs][h // 2]; bn = 'pss%d_%d' % (qs, h // 2)
                bv = bank[:].rearrange("k (h n) -> k h n", h=2)
                k.mm(bv[:, h % 2, 0:128], ktp[prev][:, h, :], qT[qs][:, p, :], True, True, ['ktp%d' % prev, 'qT%d' % qs], [bn])
                k.mm(bv[:, h % 2, 128:256], ktp[cur][:, h, :], qT[qs][:, p, :], True, True, ['ktp%d' % cur, 'qT%d' % qs], [bn])

        def back(i):
            g, d, r, n, qs = iters[i]
            if n < 0:
                return
            slot = (n + 1) % 3
            prev, cur = n % 3, slot
            q0 = n * 128 * d + r
            toks = slice(q0, q0 + 127 * d + 1, d)
            for hb in range(2):
                k.act(pexp[qs][:, 2 * hb:2 * hb + 2, :], pss[qs][hb][:].rearrange("k (h n) -> k h n", h=2), AF.Exp,
                      ['pss%d_%d' % (qs, hb)], ['pexp%d' % qs], scale=0.125)
            k.tt('pool', pm[qs][:], pexp[qs][:], maskA[:], ALU.mult, ['pexp%d' % qs, 'maskA'], ['pm%d' % qs])
            po = pso[qs]; pon = 'pso%d' % qs
            pov = po[:].rearrange("k (c t) -> k c t", c=4)
            first = True
            nmm = 0
            for p in range(2):
                for e_ in range(2):
                    h = 2 * p + e_
                    for which, sl in ((0, prev), (1, cur)):
                        ones = onesPre if (which == 0 and n == 0) else onesPad
                        rhs = pm[qs][:, h, which * 128:(which + 1) * 128]
                        nmm += 2
                        k.mm(pov[:, p, :], vpad[sl][:, h, :], rhs, first, False, ['vpad%d' % sl, 'pm%d' % qs], [pon])
                        first = False
                        k.mm(pov[:, 2 + p, :], ones[:, e_, :], rhs, False, nmm == 16, ['onesPad', 'onesPre', 'pm%d' % qs], [pon])
            if g == 0:
                k.act(acc[:, :, toks], pov, AF.Copy, [pon], ['acc'])
            else:
                k.tt('dve', acc[:, :, toks], acc[:, :, toks], pov, ALU.add, [pon, 'acc'], ['acc'])

        def capture(fn, i):
            S.begin_capture()
            fn(i)
            return S.end_capture()

        def merge(lb, la):
            out = []
            ia = 0
            nb, na = len(lb), len(la)
            for ib, item in enumerate(lb):
                out.append(item)
                tgt = ((ib + 1) * na) // max(nb, 1)
                while ia < tgt:
                    out.append(la[ia]); ia += 1
            out.extend(la[ia:])
            return out

        S.replay(capture(front, 0))
        for i in range(len(iters)):
            lb = capture(back, i)
            la = capture(front, i + 1) if i + 1 < len(iters) else []
            S.replay(merge(lb, la))
        for q4 in range(4):
            sl = slice(q4 * 1024, (q4 + 1) * 1024)
            k.S.op('dve', lambda e, sl=sl: e.reciprocal(out=acc[:, 2:4, sl], in_=acc[:, 2:4, sl]), ['acc'], ['acc'])
            k.tt('dve', acc[:, 0:2, sl], acc[:, 0:2, sl], acc[:, 2:4, sl], ALU.mult, ['acc'], ['acc'])
        k.dma(D["S_oaT"].rearrange("(p k) t -> k p t", p=2), acc[:, 0:2, :], ['acc'], ['S_oaT'])


def phase_c(nc, k, D):
    S = k.S
    NCH = int(os.environ.get('C_NCH', NL // 128))
    STG = int(os.environ.get('C_STAGE', 99))
    SUB = int(os.environ.get('C_SUB', 99))
    with ExitStack() as es:
        T = lambda name, shape: es.enter_context(nc.sbuf_tensor(name, shape, F32))
        P = lambda name: es.enter_context(nc.psum_tensor(name, [128, 512], F32))
        B = [P("c_b%d" % i) for i in range(8)]
        bn = ['cb%d' % i for i in range(8)]
        zc = [T("c_zc%d" % i, [128, 1696]) for i in range(2)]
        zp = T("c_zp", [128, 1696])
        bc = {}
        mu = T("c_mu", [128, 1696])
        k.dma(mu[:], D["mu"].partition_broadcast(128), [], ['cst'])
        for nm in ("w0", "a0", "k_k", "k_a", "ln_w", "ln_b", "r_k"):
            bc[nm] = T("c_" + nm, [128, 512])
            k.dma(bc[nm][:], D[nm].partition_broadcast(128), [], ['cst'])
        w2pad = T("c_w2pad", [64, 512]); a2pad = T("c_a2pad", [64, 512]); g2 = T("c_g2", [96, 512])
        for dst_, nm_, np_ in ((w2pad, "w2pad", 64), (a2pad, "a2pad", 64), (g2, "g2", 96)):
            k.dma(zp[0:np_, 0:512], D[nm_][:, :], [], ['zp'])
            k.act(dst_[:].bitcast(F32R), zp[0:np_, 0:512], AF.Copy, ['zp'], ['cst'])
        maskR = T("c_maskR", [128, 512]); tri = T("c_tri", [128, 128]); bm4 = T("c_bm4", [128, 4, 128]); onec = T("c_onec", [128, 1])
        k.dma(maskR[:], D["maskR"][:, :], [], ['cst']); k.dma(tri[:], D["tri"][:, :], [], ['cst'])
        k.dma(bm4[:], D["blockmask4"].rearrange("p (c n) -> p c n", c=4), [], ['cst'])
        S.op('pool', lambda e: e.memset(onec[:], 1.0), [], ['onec'])
        lor = T("c_lor", [128, 256]); lorT1 = T("c_lorT1", [64, 128]); lorT2 = T("c_lorT2", [96, 128])
        names = ["dws", "lw", "aa", "kk", "kkn", "k2", "bv", "rkr", "ecum", "einv", "eex", "rt", "ysb", "yc"]
        t = {nm: T("c_" + nm, [128, 512]) for nm in names}
        atd = [T("c_at%d" % i, [128, 512]) for i in range(2)]
        bonus2 = [T("c_bonus%d" % i, [128, 512]) for i in range(2)]
        gsb2 = [T("c_gsb%d" % i, [128, 512]) for i in range(2)]
        btd = [T("c_btd%d" % i, [128, 512]) for i in range(2)]
        ktd = [T("c_ktd%d" % i, [128, 512]) for i in range(2)]
        sm = {nm: T("c_" + nm, [128, 8]) for nm in ("ss8", "nrm", "rs8", "s1", "s2")}
        art = [T("c_art%d" % i, [128, 4, 2, 128]) for i in range(2)]
        btTp = T("c_btTp", [128, 8, 128]); ktTp = T("c_ktTp", [128, 8, 128])
        A1 = T("c_A1", [128, 8, 2, 128])
        A2 = [T("c_A2%d" % i, [128, 8, 2, 128]) for i in range(2)]
        Lb = [T("c_L%d" % i, [128, 8, 128]) for i in range(2)]
        LTb = [T("c_LT%d" % i, [128, 8, 128]) for i in range(2)]
        Mb = [T("c_M%d" % i, [128, 8, 128]) for i in range(2)]
        WT = [T("c_WT%d" % i, [128, 4, 128]) for i in range(2)]
        X0 = [T("c_X0%d" % i, [128, 512]) for i in range(2)]
        U0 = [T("c_U0%d" % i, [128, 512]) for i in range(2)]
        U = [T("c_U%d" % i, [128, 512]) for i in range(2)]
        Tb = [T("c_T%d" % i, [128, 4, 128]) for i in range(2)]
        maskP = [T("c_maskP%d" % i, [128, 4, 128]) for i in range(2)]
        pc = T("c_pc", [128, 4])
        orT = T("c_orT", [128, 4, 128])
        S.op('pool', lambda e: e.memset(Tb[0][:], 0.0), [], ['T0'])
        S.op('pool', lambda e: e.memset(lor[:], 0.0), [], ['lor'])
        S.op('pool', lambda e: e.memset(btTp[:], 0.0), [], ['btTp'])
        S.op('pool', lambda e: e.memset(ktTp[:], 0.0), [], ['ktTp'])
        S.op('dve', lambda e: e.tensor_copy(out=btTp[:].bitcast(F32R), in_=btTp[:]), ['btTp'], ['btTp'])
        S.op('dve', lambda e: e.tensor_copy(out=ktTp[:].bitcast(F32R), in_=ktTp[:]), ['ktTp'], ['ktTp'])
        h8 = lambda ap: ap.rearrange("p (h d) -> p h d", h=8)
        bc8 = lambda ap: ap.unsqueeze(2).to_broadcast([128, 8, 64])
        c4 = lambda ap: ap.rearrange("p (c t) -> p c t", c=4)

        def stage_a(c):
            s = c % 2
            own = c >= NCH // 2
            t0 = c * 128
            z = zc[s]; zn = 'zc%d' % s
            k.dma(z[:], D["S_r"][1 + t0:1 + t0 + 128, :], ['S_r', 'S_r0'], [zn])
            k.dma(zp[:], D["S_r"][t0:t0 + 128, :], ['S_r', 'S_r0'], ['zp'])
            k.tt('dve', zp[:], zp[:], z[:], ALU.subtract, ['zp', zn], ['zp'])
            k.tt('dve', zp[:], zp[:], mu[:], ALU.mult, ['zp', 'cst'], ['zp'])
            k.tt('dve', z[:], z[:], zp[:], ALU.add, ['zp', zn], [zn])
            r_, k_, v_ = z[:, 0:512], z[:, 512:1024], z[:, 1024:1536]
            k.act(lor[:, 0:32], z[:, 1536:1568], AF.Tanh, [zn], ['lor'])
            k.act(lor[:, 32:64], z[:, 1568:1600], AF.Copy, [zn], ['lor'])
            k.act(lor[:, 128:224], z[:, 1600:1696], AF.Sigmoid, [zn], ['lor'])
            k.tr(B[0][:, 0:128], lor[:, 0:128], ['lor'], [bn[0]])
            k.tr(B[0][:, 128:256], lor[:, 128:256], ['lor'], [bn[0]])
            k.act(lorT1[:].bitcast(F32R), B[0][0:64, 0:128], AF.Copy, [bn[0]], ['lorT1'])
            k.S.op('dve', lambda e: e.tensor_copy(out=lorT2[:].bitcast(F32R), in_=B[0][0:96, 128:256]), [bn[0]], ['lorT2'])
            k.mm(B[1][:], lorT1[:], w2pad[:], True, True, ['lorT1', 'cst'], [bn[1]], fast=True)
            k.mm(B[2][:], lorT1[:], a2pad[:], True, True, ['lorT1', 'cst'], [bn[2]], fast=True)
            gs = gsb2[s]; gsn = 'gsb%d' % s
            if own:
                k.mm(B[0][:], lorT2[:], g2[:], True, True, ['lorT2', 'cst'], [bn[0]], fast=True)
                k.act(gs[:], B[0][:], AF.Copy, [bn[0]], [gsn])
            k.tt('dve', t["dws"][:], B[1][:], bc["w0"][:], ALU.add, [bn[1], 'cst'], ['dws'])
            k.act(t["lw"][:], t["dws"][:], AF.Sigmoid, ['dws'], ['lw'])
            k.S.op('act', lambda e: e.mul(out=t["lw"][:], in_=t["lw"][:], mul=-float(np.exp(-0.5))), ['lw'], ['lw'])
            k.tt('dve', t["dws"][:], B[2][:], bc["a0"][:], ALU.add, [bn[2], 'cst', 'lw'], ['dws'])
            k.act(t["aa"][:], t["dws"][:], AF.Sigmoid, ['dws'], ['aa'])
            k.tt('pool', t["kk"][:], k_, bc["k_k"][:], ALU.mult, [zn, 'cst'], ['kk'])
            k.tt('pool', t["rkr"][:], t["kk"][:], t["kk"][:], ALU.mult, ['kk'], ['rkr'])
            k.S.op('dve', lambda e: e.tensor_reduce(out=sm["ss8"][:], in_=h8(t["rkr"][:]), axis=AX.X, op=ALU.add), ['rkr'], ['ss8'])
            k.act(sm["nrm"][:], sm["ss8"][:], AF.Sqrt, ['ss8'], ['nrm'])
            k.ts('dve', sm["nrm"][:], sm["nrm"][:], 1e-12, None, ALU.max, None, ['nrm'], ['nrm'])
            k.S.op('dve', lambda e: e.reciprocal(out=sm["nrm"][:], in_=sm["nrm"][:]), ['nrm'], ['nrm'])
            k.tt('dve', h8(t["kkn"][:]), h8(t["kk"][:]), bc8(sm["nrm"][:]), ALU.mult, ['kk', 'nrm'], ['kkn'])
            k.stt(t["k2"][:], t["aa"][:], -1.0, bc["k_a"][:], ALU.add, ALU.mult, ['aa', 'cst'], ['k2'])
            k.stt(t["k2"][:], t["k2"][:], 1.0, k_, ALU.add, ALU.mult, ['k2', zn], ['k2'])
            k.tt('pool', t["bv"][:], t["kkn"][:], t["aa"][:], ALU.mult, ['kkn', 'aa'], ['bv'])
            bo = bonus2[s]; bon = 'bonus%d' % s
            if own:
                k.tt('pool', t["rkr"][:], r_, t["k2"][:], ALU.mult, [zn, 'k2', 'ss8'], ['rkr'])
                k.tt('pool', t["rkr"][:], t["rkr"][:], bc["r_k"][:], ALU.mult, ['rkr', 'cst'], ['rkr'])
                k.S.op('dve', lambda e: e.tensor_reduce(out=sm["rs8"][:], in_=h8(t["rkr"][:]), axis=AX.X, op=ALU.add), ['rkr'], ['rs8'])
                k.tt('dve', h8(bo[:]), h8(v_), bc8(sm["rs8"][:]), ALU.mult, [zn, 'rs8'], [bon])
            k.mm(B[1][:], tri[:], t["lw"][:], True, True, ['cst', 'lw'], [bn[1]])
            k.act(t["ecum"][:], B[1][:], AF.Exp, [bn[1]], ['ecum'])
            k.act(t["einv"][:], B[1][:], AF.Exp, [bn[1]], ['einv'], scale=-1.0)
            k.tt('dve', t["dws"][:], B[1][:], t["lw"][:], ALU.subtract, [bn[1], 'lw', 'aa'], ['dws'])
            k.act(t["eex"][:], t["dws"][:], AF.Exp, ['dws'], ['eex'])
            for p in range(4):
                k.mm(B[0][:, 256 + p:257 + p], t["lw"][:, p * 128:(p + 1) * 128], onec[:], True, True, ['lw', 'onec'], [bn[0]])
            k.act(pc[:], B[0][:, 256:260], AF.Exp, [bn[0]], ['pc'])
            mP = maskP[s]; mPn = 'maskP%d' % s
            k.tt('pool', mP[:], bm4[:], pc[:].unsqueeze(2).to_broadcast([128, 4, 128]), ALU.mult, ['cst', 'pc'], [mPn])
            bt_, kt_, at_ = btd[s], ktd[s], atd[s]
            btn, ktn, atn = 'btd%d' % s, 'ktd%d' % s, 'atd%d' % s
            k.tt('dve', t["rt"][:], r_, t["ecum"][:], ALU.mult, [zn, 'ecum'], ['rt'])
            k.stt(at_[:], t["kkn"][:], -1.0, t["eex"][:], ALU.mult, ALU.mult, ['kkn', 'eex'], [atn])
            k.tt('pool', bt_[:], t["bv"][:], t["einv"][:], ALU.mult, ['bv', 'einv'], [btn])
            k.tt('pool', kt_[:], t["k2"][:], t["einv"][:], ALU.mult, ['k2', 'einv'], [ktn])
            ar = art[s]; arn = 'art%d' % s
            for p in range(4):
                k.tr(B[0][:, p * 128:(p + 1) * 128], at_[:, p * 128:(p + 1) * 128], [atn], [bn[0]])
            k.act(ar[:, :, 0, :].bitcast(F32R), c4(B[0][:]), AF.Copy, [bn[0]], [arn])
            for p in range(4):
                k.tr(B[1][:, p * 128:(p + 1) * 128], t["rt"][:, p * 128:(p + 1) * 128], ['rt'], [bn[1]])
            k.S.op('dve', lambda e: e.tensor_copy(out=ar[:, :, 1, :].bitcast(F32R), in_=c4(B[1][:])), [bn[1]], [arn])
            for (src, srcn, dst, dn, bi) in ((bt_, btn, btTp, 'btTp', 2), (kt_, ktn, ktTp, 'ktTp', 0)):
                for p in range(4):
                    k.tr(B[bi][:, p * 128:(p + 1) * 128], src[:, p * 128:(p + 1) * 128], [srcn], [bn[bi]])
                dv = dst[:].rearrange("k (p e) s -> k p e s", e=2)
                sv = c4(B[bi][:])
                k.act(dv[0:64, :, 0, :].bitcast(F32R), sv[0:64, :, :], AF.Copy, [bn[bi]], [dn])
                k.S.op('dve', lambda e, dv=dv, sv=sv: e.tensor_copy(out=dv[64:128, :, 1, :].bitcast(F32R), in_=sv[64:128, :, :]), [bn[bi]], [dn])
            a2 = A2[s]; a2n = 'A2%d' % s
            mstrict = maskR[:, 0:128].unsqueeze(1).to_broadcast([128, 2, 128])
            mincl = maskR[:, 128:256].unsqueeze(1).to_broadcast([128, 2, 128])
            for h in range(8):
                p = h // 2
                bi = h % 3
                rhs = ar[:, p, :, :].rearrange("k a t -> k (a t)")
                k.mm(B[bi][:, 0:256], btTp[:, h, :], rhs, True, True, ['btTp', arn], [bn[bi]], fast=True)
                k.mm(B[bi][:, 256:512], ktTp[:, h, :], rhs, True, True, ['ktTp', arn], [bn[bi]], fast=True)
                bv4 = B[bi][:].rearrange("k (x a t) -> k x a t", x=2, a=2)
                k.tt('dve', A1[:, h, :, :], bv4[:, :, 0, :], mstrict, ALU.mult, [bn[bi], 'cst'], ['A1'])
                if own:
                    k.tt('dve', a2[:, h, :, :], bv4[:, :, 1, :], mincl, ALU.mult, [bn[bi], 'cst'], [a2n])
            for h in range(8):
                k.mm(B[0][:, h * 64:(h + 1) * 64], A1[:, h, 1, :], z[:, 1024 + h * 64:1024 + (h + 1) * 64], True, True, ['A1', zn], [bn[0]])
            k.act(X0[s][:], B[0][:], AF.Copy, [bn[0]], ['X0%d' % s])

        def stage_b(c):
            s = c % 2
            own = c >= NCH // 2
            z = zc[s]; zn = 'zc%d' % s
            ar = art[s]; arn = 'art%d' % s
            a2 = A2[s]; a2n = 'A2%d' % s
            bt_, kt_, at_ = btd[s], ktd[s], atd[s]
            btn, ktn, atn = 'btd%d' % s, 'ktd%d' % s, 'atd%d' % s
            mP = maskP[s]; mPn = 'maskP%d' % s
            gs = gsb2[s]; gsn = 'gsb%d' % s
            bo = bonus2[s]; bon = 'bonus%d' % s
            L, LT, M = Lb[0], LTb[0], Mb[0]
            k.S.op('pool', lambda e, L=L: e.tensor_copy(out=L[:], in_=A1[:, :, 0, :]), ['A1'], ['L0'])
            for hh in range(2):
                bb = 3 + hh
                for h in range(4 * hh, 4 * hh + 4):
                    k.tr(B[bb][:, (h % 4) * 128:(h % 4 + 1) * 128], A1[:, h, 0, :], ['A1'], [bn[bb]])
                k.act(LT[:, 4 * hh:4 * hh + 4, :], c4(B[bb][:]), AF.Copy, [bn[bb]], ['LT0'])
            k.tt('pool', M[:], L[:], k.ident[:].unsqueeze(1).to_broadcast([128, 8, 128]), ALU.add, ['L0', 'ident'], ['M0'])
            cur = 0
            for rnd in range(6):
                nx = 1 - cur
                L, LT, M = Lb[cur], LTb[cur], Mb[cur]
                Ln, LTn, Mn = Lb[nx], LTb[nx], Mb[nx]
                last = rnd == 5
                for hh in range(2):
                    bL, bLT, bM = (3, 4, 5) if hh == 0 else (6, 7, 5)
                    hs = range(4 * hh, 4 * hh + 4)
                    if not last:
                        for h in hs:
                            k.mm(B[bL][:, (h % 4) * 128:(h % 4 + 1) * 128], LT[:, h, :], L[:, h, :], True, True, ['L%d' % cur, 'LT%d' % cur], [bn[bL]])
                    for h in hs:
                        k.mm(B[bLT][:, (h % 4) * 128:(h % 4 + 1) * 128], L[:, h, :], LT[:, h, :], True, True, ['L%d' % cur, 'LT%d' % cur], [bn[bLT]])
                    if not last:
                        k.act(Ln[:, 4 * hh:4 * hh + 4, :], c4(B[bL][:]), AF.Copy, [bn[bL]], ['L%d' % nx])
                    k.S.op('dve', lambda e, LTn=LTn, hh=hh, bLT=bLT: e.tensor_copy(out=LTn[:, 4 * hh:4 * hh + 4, :], in_=c4(B[bLT][:])),
                           [bn[bLT]], ['LT%d' % nx])
                    for h in hs:
                        k.mm(B[bM][:, (h % 4) * 128:(h % 4 + 1) * 128], LTn[:, h, :], M[:, h, :], True, True, ['LT%d' % nx, 'M%d' % cur], [bn[bM]])
                    k.tt('dve', Mn[:, 4 * hh:4 * hh + 4, :], M[:, 4 * hh:4 * hh + 4, :], c4(B[bM][:]), ALU.add,
                         [bn[bM], 'M%d' % cur], ['M%d' % nx])
                cur = nx
            Minv = Mb[cur]; Mn_ = 'M%d' % cur
            for h in range(8):
                k.mm(B[3][:, h * 64:(h + 1) * 64], Minv[:, h, :], X0[s][:, h * 64:(h + 1) * 64], True, True, [Mn_, 'X0%d' % s], [bn[3]])
            k.act(U0[s][:], B[3][:], AF.Copy, [bn[3]], ['U0%d' % s])
            for h in range(8):
                p = h // 2
                bi = 6 + h // 4
                k.mm(B[bi][:, (h % 4) * 128:(h % 4 + 1) * 128], at_[:, p * 128:(p + 1) * 128], Minv[:, h, :], True, True, [atn, Mn_], [bn[bi]])
            for hh in range(2):
                sv = B[6 + hh][:].rearrange("k (p e t) -> k p e t", p=2, e=2)
                k.act(WT[s][0:64, 2 * hh:2 * hh + 2, :], sv[0:64, :, 0, :], AF.Copy, [bn[6 + hh]], ['WT%d' % s])
                k.S.op('dve', lambda e, sv=sv, hh=hh, s=s: e.tensor_copy(out=WT[s][64:128, 2 * hh:2 * hh + 2, :], in_=sv[64:128, :, 1, :]), [bn[6 + hh]], ['WT%d' % s])
            Tc, Tn = Tb[c % 2], Tb[(c + 1) % 2]
            Tcn, Tnn = 'T%d' % (c % 2), 'T%d' % ((c + 1) % 2)
            for p in range(4):
                k.mm(B[3][:, p * 128:(p + 1) * 128], WT[s][:, p, :], Tc[:, p, :], True, True, ['WT%d' % s, Tcn], [bn[3]])
            k.tt('dve', U[s][:], B[3][:], U0[s][:], ALU.add, [bn[3], 'U0%d' % s], ['U%d' % s])
            if own:
                first = True
                for p in range(4):
                    k.mm(B[5][:, p * 128:(p + 1) * 128], ar[:, p, 1, :], Tc[:, p, :], first, False, [arn, Tcn], [bn[5]])
                    first = False
                for h in range(8):
                    k.mm(B[5][:, h * 64:(h + 1) * 64], a2[:, h, 0, :], U[s][:, h * 64:(h + 1) * 64], False, False, [a2n, 'U%d' % s], [bn[5]])
                    k.mm(B[5][:, h * 64:(h + 1) * 64], a2[:, h, 1, :], z[:, 1024 + h * 64:1024 + (h + 1) * 64], False, h == 7, [a2n, zn], [bn[5]])
            k.mm(B[4][:], k.ident[:], Tc[:].rearrange("k p n -> k (p n)"), True, False, ['ident', Tcn], [bn[4]])
            for p in range(4):
                k.mm(B[4][:, p * 128:(p + 1) * 128], kt_[:, p * 128:(p + 1) * 128], z[:, 1024 + p * 128:1024 + (p + 1) * 128], False, False, [ktn, zn], [bn[4]])
            for p in range(4):
                k.mm(B[4][:, p * 128:(p + 1) * 128], bt_[:, p * 128:(p + 1) * 128], U[s][:, p * 128:(p + 1) * 128], False, p == 3, [btn, 'U%d' % s], [bn[4]])
            k.tt('dve', Tn[:].rearrange("k p n -> k (p n)"), B[4][:], mP[:].rearrange("k p n -> k (p n)"), ALU.mult, [bn[4], mPn], [Tnn])
            if own:
                y, yc = t["ysb"], t["yc"]
                k.act(y[:], B[5][:], AF.Copy, [bn[5]], ['ysb'])
                k.S.op('dve', lambda e: e.tensor_reduce(out=sm["s1"][:], in_=h8(y[:]), axis=AX.X, op=ALU.add), ['ysb'], ['s1'])
                k.ts('dve', sm["s1"][:], sm["s1"][:], -1.0 / 64, None, ALU.mult, None, ['s1'], ['s1'])
                k.tt('dve', h8(yc[:]), h8(y[:]), bc8(sm["s1"][:]), ALU.add, ['ysb', 's1'], ['yc'])
                k.tt('pool', y[:], yc[:], yc[:], ALU.mult, ['yc'], ['ysb'])
                k.S.op('dve', lambda e: e.tensor_reduce(out=sm["s2"][:], in_=h8(y[:]), axis=AX.X, op=ALU.add), ['ysb'], ['s2'])
                k.ts('dve', sm["s2"][:], sm["s2"][:], 1.0 / 64, GN_EPS, ALU.mult, ALU.add, ['s2'], ['s2'])
                k.act(sm["s2"][:], sm["s2"][:], AF.Sqrt, ['s2'], ['s2'])
                k.S.op('dve', lambda e: e.reciprocal(out=sm["s2"][:], in_=sm["s2"][:]), ['s2'], ['s2'])
                k.tt('dve', h8(yc[:]), h8(yc[:]), bc8(sm["s2"][:]), ALU.mult, ['yc', 's2'], ['yc'])
                k.tt('pool', yc[:], yc[:], bc["ln_w"][:], ALU.mult, ['yc', 'cst'], ['yc'])
                k.tt('pool', yc[:], yc[:], bc["ln_b"][:], ALU.add, ['yc', 'cst'], ['yc'])
                k.tt('pool', yc[:], yc[:], bo[:], ALU.add, ['yc', bon], ['yc'])
                k.tt('dve', yc[:], yc[:], gs[:], ALU.mult, ['yc', gsn], ['yc'])
                for p in range(4):
                    k.tr(B[5][:, p * 128:(p + 1) * 128], yc[:, p * 128:(p + 1) * 128], ['yc'], [bn[5]])
                k.act(orT[:], c4(B[5][:]), AF.Copy, [bn[5]], ['orT'])
                o0 = (c - NCH // 2) * 128
                k.dma(D["S_orT"][:, o0:o0 + 128].rearrange("(c p) t -> p c t", p=128), orT[:], ['orT'], ['S_orT'])

        def capture(fn, c):
            S.begin_capture()
            fn(c)
            return S.end_capture()

        def merge(lb, la):
            out = []
            ia = 0
            nb, na = len(lb), len(la)
            for ib, it in enumerate(lb):
                out.append(it)
                tgt = ((ib + 1) * na) // max(nb, 1)
                while ia < tgt:
                    out.append(la[ia]); ia += 1
            out.extend(la[ia:])
            return out

        S.replay(capture(stage_a, 0))
        for c in range(NCH):
            lb = capture(stage_b, c)
            la = capture(stage_a, c + 1) if c + 1 < NCH else []
            S.replay(merge(lb, la))


def phase_d(nc, k, D):
    S = k.S
    TB = 256
    NB = NO // TB
    with ExitStack() as es:
        T = lambda name, shape: es.enter_context(nc.sbuf_tensor(name, shape, F32))
        P = lambda name: es.enter_context(nc.psum_tensor(name, [128, 512], F32))
        B = [P("d_b%d" % i) for i in range(8)]
        bn = ['db%d' % i for i in range(8)]
        NW = 2
        wst = [T("d_wst%d" % i, [128, 4096]) for i in range(NW)]
        wraw = [T("d_wraw%d" % i, [128, 4096]) for i in range(3)]
        oaTr = T("d_oaTr", [128, 2, 128]); orTr = T("d_orTr", [128, 4, 128])
        wau = T("d_wau", [128, 2, 1024]); wru = T("d_wru", [128, 4, 1024]); wple = T("d_wple", [128, 2, 1024])
        k.dma(wraw[0][:, 0:2048].rearrange("p (c n) -> p c n", c=2), D["w_au"].rearrange("(c p) n -> p c n", p=128), [], ['wraw0'])
        k.act(wau[:].bitcast(F32R), wraw[0][:, 0:2048].rearrange("p (c n) -> p c n", c=2), AF.Copy, ['wraw0'], ['cst'])
        k.dma(wraw[0][:, 0:4096].rearrange("p (c n) -> p c n", c=4), D["w_ru"].rearrange("(c p) n -> p c n", p=128), [], ['wraw0'])
        k.act(wru[:].bitcast(F32R), wraw[0][:, 0:4096].rearrange("p (c n) -> p c n", c=4), AF.Copy, ['wraw0'], ['cst'])
        k.dma(wraw[0][:, 0:2048].rearrange("p (c n) -> p c n", c=2), D["w_ple"].rearrange("(c p) n -> p c n", p=128), [], ['wraw0'])
        k.act(wple[:].bitcast(F32R), wraw[0][:, 0:2048].rearrange("p (c n) -> p c n", c=2), AF.Copy, ['wraw0'], ['cst'])
        gffn = T("d_gffn", [128, 8]); k.dma(gffn[:], D["gffn"][:, :], [], ['cst'])
        gm = T("d_gm", [128, 1024]); gf = T("d_gf", [128, 1024]); gp = T("d_gp", [128, 1024])
        k.dma(gm[:], D["g_mpost"].partition_broadcast(128), [], ['cst'])
        k.dma(gf[:], D["g_fpost"].partition_broadcast(128), [], ['cst'])
        k.dma(gp[:], D["g_ppost"].partition_broadcast(128), [], ['cst'])
        hT = T("d_hT", [128, 32, TB]); fT = T("d_fT", [128, 8, TB])
        hres = [T("d_h%d" % i, [128, 1024]) for i in range(2)]
        gates = T("d_gates", [128, 2048])
        oaT = T("d_oaT", [128, 2, 128]); orT = T("d_orT", [128, 4, 128])
        mer = T("d_mer", [128, 1024]); mer2 = T("d_mer2", [128, 1024]); junk = T("d_junk", [128, 1024])
        xT = T("d_xT", [128, 8, 128])
        pt_ = T("d_pt", [128, 256]); pT = T("d_pT", [128, 2, 128])
        ss = T("d_ss", [128, 1]); rstd = T("d_rstd", [128, 1])
        rel = [T("d_rel%d" % i, [128, TB]) for i in range(2)]
        wi = [0]

        def wload(src_ap, shape_kind):
            s = wi[0] % NW; s2 = wi[0] % 3; wi[0] += 1
            if shape_kind == 'col':
                k.dma(wraw[s2][:].rearrange("p (c n) -> p c n", c=8), src_ap.rearrange("(c p) n -> p c n", p=128), [], ['wraw%d' % s2])
            else:
                k.dma(wraw[s2][:].rearrange("p (c n) -> p c n", c=4), src_ap.rearrange("(c p) n -> p c n", p=128), [], ['wraw%d' % s2])
            if (wi[0] % 2) == 0:
                k.act(wst[s][:].bitcast(F32R), wraw[s2][:], AF.Copy, ['wraw%d' % s2], ['wst%d' % s])
            else:
                k.S.op('dve', lambda e: e.tensor_copy(out=wst[s][:].bitcast(F32R), in_=wraw[s2][:]), ['wraw%d' % s2], ['wst%d' % s])
            return s

        def transpose8(src, srcn, dst3, dstn, scale_ap=None):
            for c in range(8):
                k.tr(B[4 + c // 4][:, (c % 4) * 128:(c % 4 + 1) * 128], src[:, c * 128:(c + 1) * 128], [srcn], [bn[4 + c // 4]])
            for hf in range(2):
                pv = B[4 + hf][:].rearrange("p (c t) -> p c t", c=4)
                if scale_ap is None:
                    if hf == 0:
                        k.act(dst3[:, 0:4, :].bitcast(F32R), pv, AF.Copy, [bn[4]], [dstn])
                    else:
                        k.S.op('dve', lambda e, pv=pv: e.tensor_copy(out=dst3[:, 4:8, :].bitcast(F32R), in_=pv), [bn[5]], [dstn])
                else:
                    k.tt('dve', dst3[:, hf * 4:(hf + 1) * 4, :].bitcast(F32R), pv, scale_ap[:, hf * 4:(hf + 1) * 4].unsqueeze(2).to_broadcast([128, 4, 128]),
                         ALU.mult, [bn[4 + hf], 'cst'], [dstn])

        def dense1024(lhs3, lhsn, nk, wslots_or_res, banks):
            for hf in range(2):
                for c in range(nk):
                    wap, wn = wslots_or_res(c, hf)
                    k.mm(B[banks[hf]][:], lhs3[:, c, :], wap, c == 0, c == nk - 1, [lhsn, wn], [bn[banks[hf]]], fast=True)

        for blk in range(NB):
            ws_out = [wload(D["w_out"][:, hf * 512:(hf + 1) * 512], 'col') for hf in range(2)]
            for j in range(TB // 128):
                o0 = blk * TB + j * 128
                h_ = hres[j]; hn = 'h%d' % j
                k.dma(h_[:], D["xloc"][NPRE + o0:NPRE + o0 + 128, :], [], [hn])
                k.dma(oaTr[:], D["S_oaT"][:, o0:o0 + 128].rearrange("(c p) t -> p c t", p=128), ['S_oaT'], ['oaTr'])
                k.dma(orTr[:], D["S_orT"][:, o0:o0 + 128].rearrange("(c p) t -> p c t", p=128), ['S_orT'], ['orTr'])
                k.S.op('dve', lambda e: e.tensor_copy(out=oaT[:].bitcast(F32R), in_=oaTr[:]), ['oaTr'], ['oaT'])
                k.S.op('dve', lambda e: e.tensor_copy(out=orT[:].bitcast(F32R), in_=orTr[:]), ['orTr'], ['orT'])
                k.dma(gates[:], D["S_g"][o0:o0 + 128, :], ['S_g'], ['gates'])
                dense1024(oaT, 'oaT', 2, lambda c, hf: (wau[:, c, hf * 512:(hf + 1) * 512], 'cst'), (0, 1))
                dense1024(orT, 'orT', 4, lambda c, hf: (wru[:, c, hf * 512:(hf + 1) * 512], 'cst'), (2, 3))
                for hf in range(2):
                    sl = slice(hf * 512, (hf + 1) * 512)
                    k.tt('dve', mer[:, sl], B[hf][:], gates[:, sl], ALU.mult, [bn[hf], 'gates'], ['mer'])
                    k.tt('dve', mer2[:, sl], B[2 + hf][:], gates[:, 1024 + hf * 512:1024 + (hf + 1) * 512], ALU.mult, [bn[2 + hf], 'gates'], ['mer2'])
                k.tt('dve', mer[:], mer[:], mer2[:], ALU.add, ['mer', 'mer2'], ['mer'])
                transpose8(mer, 'mer', xT, 'xT')
                dense1024(xT, 'xT', 8, lambda c, hf: (wst[ws_out[hf]][:].rearrange("p (c n) -> p c n", c=8)[:, c, :], 'wst%d' % ws_out[hf]), (6, 7))
                pso = B[6][:]
                k.act(junk[:, 0:512], B[6][:], AF.Square, [bn[6]], ['junk', 'ss'], accum_out=ss[:])
                k.act(junk[:, 512:1024], B[7][:], AF.Square, [bn[7]], ['junk', 'rstd'], accum_out=rstd[:])
                k.tt('dve', ss[:], ss[:], rstd[:], ALU.add, ['ss', 'rstd'], ['ss'])
                k.ts('dve', rstd[:], ss[:], 1.0 / 1024, EPS, ALU.mult, ALU.add, ['ss'], ['rstd'])
                k.act(rstd[:], rstd[:], AF.Sqrt, ['rstd'], ['rstd'])
                k.S.op('dve', lambda e: e.reciprocal(out=rstd[:], in_=rstd[:]), ['rstd'], ['rstd'])
                for hf in range(2):
                    sl = slice(hf * 512, (hf + 1) * 512)
                    k.stt(mer[:, sl], B[6 + hf][:], rstd[:, 0:1], gm[:, sl], ALU.mult, ALU.mult, [bn[6 + hf], 'rstd', 'cst'], ['mer'])
                k.tt('dve', h_[:], h_[:], mer[:], ALU.add, [hn, 'mer'], [hn])
                k.rms_rstd(h_[:], junk[:], ss[:], rstd[:], 1024, EPS, [hn], ('junk', 'ss', 'rstd'))
                k.ts('dve', mer2[:], h_[:], rstd[:, 0:1], None, ALU.mult, None, [hn, 'rstd'], ['mer2'])
                transpose8(mer2, 'mer2', fT[:, :, j * 128:(j + 1) * 128], 'fT', scale_ap=gffn)
            for cb in range(8):
                s = wload(D["w_ffi"][:, cb * 512:(cb + 1) * 512], 'col')
                wv = wst[s][:].rearrange("p (c n) -> p c n", c=8)
                for q in range(4):
                    ffc = cb * 4 + q
                    bi = 4 + ffc % 2
                    for c in range(8):
                        k.mm(B[bi][:, 0:TB], wv[:, c, q * 128:(q + 1) * 128], fT[:, c, :], c == 0, c == 7, ['wst%d' % s, 'fT'], [bn[bi]], fast=True)
                    rl = rel[ffc % 2]; rln = 'rel%d' % (ffc % 2)
                    k.act(rl[:], B[bi][:, 0:TB], AF.Relu, [bn[bi]], [rln])
                    k.tt('dve', hT[:, ffc, :].bitcast(F32R), rl[:], rl[:], ALU.mult, [rln], ['hT'])
            for rb in range(8):
                s = wload(D["w_ffo"][rb * 512:(rb + 1) * 512, :], 'row')
                wv = wst[s][:].rearrange("p (c n) -> p c n", c=4)
                for j in range(TB // 128):
                    for hf in range(2):
                        bi = 2 * j + hf
                        for q in range(4):
                            ffc = rb * 4 + q
                            k.mm(B[bi][:], hT[:, ffc, j * 128:(j + 1) * 128], wv[:, q, hf * 512:(hf + 1) * 512], ffc == 0, ffc == 31,
                                 ['hT', 'wst%d' % s], [bn[bi]], fast=True)
            ws_pg = [wload(D["w_pg"][:, hf * 512:(hf + 1) * 512], 'col') for hf in range(2)]
            for j in range(TB // 128):
                o0 = blk * TB + j * 128
                h_ = hres[j]; hn = 'h%d' % j
                b0, b1 = 2 * j, 2 * j + 1
                k.act(junk[:, 0:512], B[b0][:], AF.Square, [bn[b0]], ['junk', 'ss'], accum_out=ss[:])
                k.act(junk[:, 512:1024], B[b1][:], AF.Square, [bn[b1]], ['junk', 'rstd'], accum_out=rstd[:])
                k.tt('dve', ss[:], ss[:], rstd[:], ALU.add, ['ss', 'rstd'], ['ss'])
                k.ts('dve', rstd[:], ss[:], 1.0 / 1024, EPS, ALU.mult, ALU.add, ['ss'], ['rstd'])
                k.act(rstd[:], rstd[:], AF.Sqrt, ['rstd'], ['rstd'])
                k.S.op('dve', lambda e: e.reciprocal(out=rstd[:], in_=rstd[:]), ['rstd'], ['rstd'])
                for hf, bb in enumerate((b0, b1)):
                    sl = slice(hf * 512, (hf + 1) * 512)
                    k.stt(mer[:, sl], B[bb][:], rstd[:, 0:1], gf[:, sl], ALU.mult, ALU.mult, [bn[bb], 'rstd', 'cst'], ['mer'])
                k.tt('dve', h_[:], h_[:], mer[:], ALU.add, [hn, 'mer'], [hn])
                transpose8(h_, hn, xT, 'xT')
                dense1024(xT, 'xT', 8, lambda c, hf: (wst[ws_pg[hf]][:].rearrange("p (c n) -> p c n", c=8)[:, c, :], 'wst%d' % ws_pg[hf]), (6, 7))
                for hf in range(2):
                    k.act(mer2[:, hf * 512:(hf + 1) * 512], B[6 + hf][:], AF.Sigmoid, [bn[6 + hf]], ['mer2'])
                k.dma(pt_[:], D["pown"][o0:o0 + 128, :], [], ['pt'])
                for c in range(2):
                    k.tr(B[4][:, c * 128:(c + 1) * 128], pt_[:, c * 128:(c + 1) * 128], ['pt'], [bn[4]])
                k.act(pT[:].bitcast(F32R), B[4][:, 0:256].rearrange("p (c t) -> p c t", c=2), AF.Copy, [bn[4]], ['pT'])
                dense1024(pT, 'pT', 2, lambda c, hf: (wple[:, c, hf * 512:(hf + 1) * 512], 'cst'), (6, 7))
                for hf in range(2):
                    sl = slice(hf * 512, (hf + 1) * 512)
                    k.tt('dve', mer[:, sl], B[6 + hf][:], mer2[:, sl], ALU.mult, [bn[6 + hf], 'mer2'], ['mer'])
                k.rms_rstd(mer[:], junk[:], ss[:], rstd[:], 1024, EPS, ['mer'], ('junk', 'ss', 'rstd'))
                k.stt(mer[:], mer[:], rstd[:, 0:1], gp[:], ALU.mult, ALU.mult, ['mer', 'rstd', 'cst'], ['mer'])
                k.tt('dve', mer[:], mer[:], h_[:], ALU.add, ['mer', hn], ['mer'])
                k.dma(D["out"][o0:o0 + 128, :], mer[:], ['mer'], ['out'])


def _consts():
    i = np.arange(128)
    kq = i[:, None]; qq = i[None, :]
    mprev = (kq >= qq).astype(np.float32); mcur = (kq <= qq).astype(np.float32)
    maskA = np.concatenate([mprev, mcur], 1)
    maskA4 = np.tile(maskA, (1, 4))
    strict = (kq < qq).astype(np.float32); incl = (kq <= qq).astype(np.float32)
    maskR = np.concatenate([strict, incl, strict, incl], 1)
    tri = (kq <= qq).astype(np.float32)
    bm = ((kq // 64) == (qq // 64)).astype(np.float32)
    bm4 = np.tile(bm, (1, 4))
    onesPad = np.zeros((128, 2, 128), np.float32)
    onesPad[:, 0, 0:64] = 1.0; onesPad[:, 1, 64:128] = 1.0
    return dict(ident=np.eye(128, dtype=np.float32), maskA4=maskA4, maskR=maskR, tri=tri, blockmask4=bm4,
                onesPad=onesPad.reshape(128, 256))


def make_in_maps(inp):
    f = lambda a: np.ascontiguousarray(np.asarray(a, dtype=np.float32))
    x = f(inp['x']); p = f(inp['p'])
    cst = _consts()
    shared = dict(
        w_in=f(inp['w_in'][0]), w_au=f(inp['w_attn_up'][0]), w_ru=f(inp['w_rwkv_up'][0]), w_out=f(inp['w_out'][0]),
        w_ffi=f(inp['w_ff_in'][0]), w_ffo=f(inp['w_ff_out'][0]), w_ple=f(inp['w_ple'][0]), w_pg=f(inp['w_ple_gate'][0]),
        gpre=f(np.asarray(inp['mix_pre_norm'][0]).reshape(8, 128).T), gffn=f(np.asarray(inp['ffn_pre_norm'][0]).reshape(8, 128).T),
        g_mpost=f(np.asarray(inp['mix_post_norm'][0]).reshape(1, 1024)), g_fpost=f(np.asarray(inp['ffn_post_norm'][0]).reshape(1, 1024)),
        g_ppost=f(np.asarray(inp['ple_post_norm'][0]).reshape(1, 1024)),
        mu=f(np.asarray(inp['rwkv_mu'][0]).reshape(1, 1696)),
        w0=f(np.asarray(inp['rwkv_w0'][0]).reshape(1, 512)), a0=f(np.asarray(inp['rwkv_a0'][0]).reshape(1, 512)),
        k_k=f(np.asarray(inp['rwkv_k_k'][0]).reshape(1, 512)), k_a=f(np.asarray(inp['rwkv_k_a'][0]).reshape(1, 512)),
        ln_w=f(np.asarray(inp['rwkv_ln_w'][0]).reshape(1, 512)), ln_b=f(np.asarray(inp['rwkv_ln_b'][0]).reshape(1, 512)),
        r_k=f(np.asarray(inp['rwkv_r_k'][0]).reshape(1, 512)),
        g2=f(inp['rwkv_g2'][0]),
    )
    w2pad = np.zeros((64, 512), np.float32); w2pad[0:32] = np.asarray(inp['rwkv_w2'][0])
    a2pad = np.zeros((64, 512), np.float32); a2pad[32:64] = np.asarray(inp['rwkv_a2'][0])
    shared.update(w2pad=w2pad, a2pad=a2pad)
    shared.update({kk: vv for kk, vv in cst.items()})
    inv_freq = (10000.0 ** (-np.arange(0, 64, 2, dtype=np.float32) / np.float32(64))).astype(np.float32)
    maps = []
    for c in range(8):
        b, half = c // 2, c % 2
        xloc = np.zeros((NL, 1024), np.float32)
        if half == 0:
            xloc[NPRE:] = x[b, 0:4096]
        else:
            xloc[:] = x[b]
        pos = (half * 4096 - 2048 + np.arange(NK)).astype(np.float32)
        ang = (pos[:, None] * inv_freq[None, :]).astype(np.float32)
        m = dict(shared)
        m.update(xloc=xloc, pown=f(p[0, b, half * 4096:(half + 1) * 4096]),
                 cosT=f(np.tile(np.cos(ang), (1, 12))), sinT=f(np.tile(np.sin(ang), (1, 12))),
                 onesPre=f(cst['onesPad'] * float(half)))
        maps.append(m)
    return maps


def kernel(**inputs):
    nc = build_program()
    maps = make_in_maps(inputs)
    res = run_bass_kernel_spmd(nc, maps, core_ids=list(range(8)))
    out = np.zeros((4, 8192, 1024), np.float32)
    for c in range(8):
        b, half = c // 2, c % 2
        out[b, half * 4096:(half + 1) * 4096] = res.results[c]["out"]
    return out
```

```python
import os
import numpy as np
import concourse.bass as bass
import concourse.mybir as mybir
from concourse.bass_utils import run_bass_kernel_spmd
from contextlib import ExitStack

F32 = mybir.dt.float32
F32R = mybir.dt.float32r
AF = mybir.ActivationFunctionType
ALU = mybir.AluOpType
AX = mybir.AxisListType
ENGS = ('sp', 'pe', 'act', 'dve', 'pool')
DEBUG = None


class Sched:
    def __init__(self, nc, es, n_dma=8):
        self.nc = nc
        self.sem = {e: es.enter_context(nc.semaphore('s_' + e)) for e in ('pe', 'act', 'dve', 'pool')}
        self.dsem = [es.enter_context(nc.semaphore('sd%d' % i)) for i in range(n_dma)]
        self.cnt = {e: 0 for e in self.sem}
        self.dcnt = [0] * n_dma
        self.dnext = 0
        self.streams = {e: [] for e in ENGS}
        self.waited = {e: {} for e in ENGS}
        self.lastw = {}
        self.readers = {}
        self.nops = 0

    def _semh(self, k):
        return self.dsem[k[1]] if isinstance(k, tuple) else self.sem[k]

    def _deps(self, reads, writes):
        deps = {}

        def add(k, v):
            if deps.get(k, 0) < v:
                deps[k] = v
        for r in reads:
            t = self.lastw.get(r)
            if t:
                add(*t)
        for w in writes:
            t = self.lastw.get(w)
            if t:
                add(*t)
            for k, v in self.readers.get(w, {}).items():
                add(k, v)
        return deps

    def _record(self, tok, reads, writes):
        for r in reads:
            d = self.readers.setdefault(r, {})
            if d.get(tok[0], 0) < tok[1]:
                d[tok[0]] = tok[1]
        for w in writes:
            self.lastw[w] = tok
            self.readers[w] = {}

    def _waits(self, eng, deps):
        out = []
        wd = self.waited[eng]
        for k, v in deps.items():
            if eng == 'pe' and k == 'pe':
                continue
            if wd.get(k, 0) >= v:
                continue
            wd[k] = v
            out.append((k, v))
        return out

    def begin_capture(self):
        self.cap = []

    def end_capture(self):
        c = self.cap
        self.cap = None
        return c

    def replay(self, items):
        for it in items:
            if it[0] == 'op':
                self.op(it[1], it[2], it[3], it[4])
            else:
                self.dma(it[1], it[2], it[3], it[4], it[5], **it[6])

    def op(self, eng, fn, reads=(), writes=()):
        if getattr(self, 'cap', None) is not None:
            self.cap.append(('op', eng, fn, tuple(reads), tuple(writes)))
            return None
        ex = [r for r in reads if isinstance(r, str) and r[:2] in ('cb', 'db', 'pt', 'pz', 'ps', 'pk', 'pq') and r not in writes]
        if ex:
            writes = list(writes) + ex
        deps = self._deps(reads, writes)
        waits = self._waits(eng, deps)
        self.cnt[eng] += 1
        tok = (eng, self.cnt[eng])
        self.streams[eng].append((waits, fn, (eng, 1)))
        self._record(tok, reads, writes)
        self.nops += 1
        return tok

    def dma(self, out, in_, reads=(), writes=(), q='sp', **kw):
        if getattr(self, 'cap', None) is not None:
            self.cap.append(('dma', out, in_, tuple(reads), tuple(writes), q, kw))
            return None
        slot = self.dnext
        self.dnext = (self.dnext + 1) % len(self.dsem)
        deps = self._deps(reads, writes)
        k = ('d', slot)
        if self.dcnt[slot] > 0 and deps.get(k, 0) < self.dcnt[slot]:
            deps[k] = self.dcnt[slot]
        waits = self._waits(q, deps)
        self.dcnt[slot] += 16
        tok = (k, self.dcnt[slot])
        self.streams[q].append((waits, lambda e: e.dma_start(out=out, in_=in_, **kw), (k, 16)))
        self._record(tok, reads, writes)
        self.nops += 1
        return tok

    def barrier(self):
        toks = {e: c for e, c in self.cnt.items() if c > 0}
        for i, c in enumerate(self.dcnt):
            if c > 0:
                toks[('d', i)] = c
        for e in ENGS:
            waits = self._waits(e, {k: v for k, v in toks.items() if k != e})
            if waits:
                self.streams[e].append((waits, None, None))
        self.lastw = {}
        self.readers = {}

    def finish(self):
        toks = {}
        for i, c in enumerate(self.dcnt):
            if c > 0:
                toks[('d', i)] = c
        waits = self._waits('sp', toks)
        self.streams['sp'].append((waits, None, None))

    def flush(self):
        nc = self.nc
        with nc.Block() as block:
            for ename, deco in (('sp', block.sync), ('pe', block.tensor), ('act', block.scalar),
                                ('dve', block.vector), ('pool', block.gpsimd)):
                stream = self.streams[ename]

                def body(eng, stream=stream):
                    for waits, fn, inc in stream:
                        for (k, v) in waits:
                            eng.wait_ge(self._semh(k), v)
                        if fn is not None:
                            ins = fn(eng)
                            if inc is not None:
                                ins.then_inc(self._semh(inc[0]), inc[1])
                deco(body)
        self.streams = {e: [] for e in ENGS}


NL, NPRE, NO, KV0, NK = 8192, 4096, 4096, 2048, 6144
EPS = 1e-6
GN_EPS = 64e-5
WCOLS = 6048


class K:
    def __init__(self, nc, es):
        self.nc, self.es = nc, es
        self.S = Sched(nc, es)
        self.uid = 0

    def act(self, out, in_, func, r, w, **kw):
        self.S.op('act', lambda e: e.activation(out=out, in_=in_, func=func, **kw), r, w)

    def tt(self, eng, out, in0, in1, op, r, w):
        self.S.op(eng, lambda e: e.tensor_tensor(out=out, in0=in0, in1=in1, op=op), r, w)

    def ts(self, eng, out, in0, s1, s2, op0, op1, r, w):
        if s2 is None:
            self.S.op(eng, lambda e: e.tensor_scalar(out=out, in0=in0, scalar1=s1, scalar2=None, op0=op0), r, w)
        else:
            self.S.op(eng, lambda e: e.tensor_scalar(out=out, in0=in0, scalar1=s1, scalar2=s2, op0=op0, op1=op1), r, w)

    def stt(self, out, in0, scalar, in1, op0, op1, r, w):
        self.S.op('dve', lambda e: e.scalar_tensor_tensor(out=out, in0=in0, scalar=scalar, in1=in1, op0=op0, op1=op1), r, w)

    def mm(self, out, lhsT, rhs, start, stop, r, w, fast=False):
        if fast:
            lhsT = lhsT.bitcast(F32R)
            rhs = rhs.bitcast(F32R)
        self.S.op('pe', lambda e: e.matmul(out, lhsT=lhsT, rhs=rhs, start=start, stop=stop, skip_group_check=True), r, w)

    def tr(self, out, in_, r, w):
        ident = self.ident
        self.S.op('pe', lambda e: e.transpose(out=out, in_=in_, identity=ident[:]), list(r) + ['ident'], w)

    def dma(self, out, in_, r, w):
        self.S.dma(out, in_, r, w)

    def rms_rstd(self, src, junk, ss, rstd, n, eps, r, names):
        jn, sn, rn = names
        self.act(junk, src, AF.Square, r, [jn, sn], accum_out=ss)
        self.ts('dve', rstd, ss, 1.0 / n, eps, ALU.mult, ALU.add, [sn], [rn])
        self.act(rstd, rstd, AF.Sqrt, [rn], [rn])
        self.S.op('dve', lambda e: e.reciprocal(out=rstd, in_=rstd), [rn], [rn])


def build_program():
    nc = bass.Bass("TRN2", target_bir_lowering=False)
    D = {}

    def din(name, shape):
        D[name] = nc.dram_tensor(name, list(shape), F32, kind="ExternalInput").ap()

    def dscr(name, shape):
        kind = "ExternalOutput" if DEBUG else "Internal"
        D[name] = nc.dram_tensor(name, list(shape), F32, kind=kind).ap()

    din("xloc", [NL, 1024]); din("pown", [NO, 256])
    din("w_in", [1024, WCOLS]); din("w_au", [256, 1024]); din("w_ru", [512, 1024]); din("w_out", [1024, 1024])
    din("w_ffi", [1024, 4096]); din("w_ffo", [4096, 1024]); din("w_ple", [256, 1024]); din("w_pg", [1024, 1024])
    din("gpre", [128, 8]); din("gffn", [128, 8])
    din("g_mpost", [1, 1024]); din("g_fpost", [1, 1024]); din("g_ppost", [1, 1024])
    din("mu", [1, 1696])
    for nm in ("w0", "a0", "k_k", "k_a", "ln_w", "ln_b", "r_k"):
        din(nm, [1, 512])
    din("w2pad", [64, 512]); din("a2pad", [64, 512]); din("g2", [96, 512])
    din("cosT", [NK, 384]); din("sinT", [NK, 384])
    din("ident", [128, 128]); din("maskA4", [128, 1024]); din("maskR", [128, 512]); din("tri", [128, 128])
    din("blockmask4", [128, 512]); din("onesPad", [128, 256]); din("onesPre", [128, 256])
    D["out"] = nc.dram_tensor("out", [NO, 1024], F32, kind="ExternalOutput").ap()
    dscr("S_q", [NO, 768]); dscr("S_k", [NK, 768]); dscr("S_v", [NK, 768])
    dscr("S_r", [NL + 1, 1696]); dscr("S_g", [NO, 2048])
    dscr("S_oaT", [256, NO]); dscr("S_orT", [512, NO])
    if DEBUG:
        dscr("dbgC", [24, 128, 512])

    with ExitStack() as es0:
        k = K(nc, es0)
        S = k.S
        ident = es0.enter_context(nc.sbuf_tensor("ident_sb", [128, 128], F32))
        k.ident = ident
        S.dma(ident[:], D["ident"][:, :], writes=['ident'])
        phases = DEBUG or "ABCD"
        if "A" in phases:
            phase_a(nc, k, D)
            S.barrier(); S.flush()
        if "B" in phases:
            phase_b(nc, k, D)
            S.barrier(); S.flush()
        if "C" in phases:
            phase_c(nc, k, D)
            S.barrier(); S.flush()
        if "D" in phases:
            phase_d(nc, k, D)
        S.barrier()
        S.finish()
        S.flush()
    return nc


COLBLOCKS = {
    'Q0': (0, 512, 'Q', 0), 'Q1': (512, 256, 'Q', 512),
    'K0': (768, 512, 'K', 0), 'K1': (1280, 256, 'K', 512),
    'V0': (1536, 512, 'V', 0), 'V1': (2048, 256, 'V', 512),
    'R0': (2304, 512, 'R', 0), 'R1': (2816, 512, 'R', 512), 'R2': (3328, 512, 'R', 1024), 'R3': (3840, 160, 'R', 1536),
    'G0': (4000, 512, 'G', 0), 'G1': (4512, 512, 'G', 512), 'G2': (5024, 512, 'G', 1024), 'G3': (5536, 512, 'G', 1536),
}


def phase_a(nc, k, D):
    S = k.S
    with ExitStack() as es:
        T = lambda name, shape: es.enter_context(nc.sbuf_tensor(name, shape, F32))
        P = lambda name: es.enter_context(nc.psum_tensor(name, [128, 512], F32))
        xt = [T("a_xt%d" % i, [128, 1024]) for i in range(2)]
        junk = T("a_junk", [128, 1024]); xs = T("a_xs", [128, 1024])
        ss = T("a_ss", [128, 1]); rstd = T("a_rstd", [128, 1])
        uT = [T("a_uT%d" % i, [128, 8, 1024]) for i in range(2)]
        wb = [T("a_wb%d" % i, [128, 8, 512]) for i in range(2)]
        wraw = [T("a_wraw%d" % i, [128, 8, 512]) for i in range(3)]
        raw = [T("a_raw%d" % i, [128, 512]) for i in range(2)]
        stg = [T("a_stg%d" % i, [128, 512]) for i in range(3)]
        cblk = T("a_cblk", [128, 8, 384]); sblk = T("a_sblk", [128, 8, 384])
        tmp = [T("a_tmp%d" % i, [128, 256]) for i in range(4)]
        gpre = T("a_gpre", [128, 8]); zrow = T("a_zrow", [1, 1696])
        pt = [P("a_pt%d" % i) for i in range(2)]
        pz = [P("a_pz%d" % i) for i in range(3)]
        k.dma(gpre[:], D["gpre"][:, :], [], ['gpre'])
        S.op('pool', lambda e: e.memset(zrow[:], 0.0), [], ['zrow'])
        k.dma(D["S_r"][0:1, :], zrow[:], ['zrow'], ['S_r0'])

        items = []
        for blk in range(8):
            own = blk >= 4
            kv = blk >= 2
            cbs = []
            if own:
                cbs += ['Q0', 'Q1']
            if kv:
                cbs += ['K0', 'K1', 'V0', 'V1']
            cbs += ['R0', 'R1', 'R2', 'R3']
            if own:
                cbs += ['G0', 'G1', 'G2', 'G3']
            for i, cb in enumerate(cbs):
                items.append((blk, cb, i == 0))

        def load_w_dma(idx):
            blk, cb, _ = items[idx]
            c0, ncols, _, _ = COLBLOCKS[cb]
            s3 = idx % 3
            k.dma(wraw[s3][:, :, 0:ncols], D["w_in"][:, c0:c0 + ncols].rearrange("(c p) n -> p c n", p=128), [], ['wraw%d' % s3])

        def load_w_round(idx):
            blk, cb, _ = items[idx]
            c0, ncols, _, _ = COLBLOCKS[cb]
            s3 = idx % 3; s = idx % 2
            k.act(wb[s][:, :, 0:ncols].bitcast(F32R), wraw[s3][:, :, 0:ncols], AF.Copy, ['wraw%d' % s3], ['wb%d' % s])

        load_w_dma(0)
        load_w_round(0)
        load_w_dma(1)
        pzi = 0; rawi = 0; stgi = 0
        for idx, (blk, cb, first) in enumerate(items):
            ub = uT[blk % 2]; ubn = 'uT%d' % (blk % 2)
            own = blk >= 4; kv = blk >= 2
            if first:
                for j in range(8):
                    row0 = blk * 1024 + j * 128
                    x_ = xt[j % 2]; xn = 'xt%d' % (j % 2)
                    k.dma(x_[:], D["xloc"][row0:row0 + 128, :], [], [xn])
                    k.rms_rstd(x_[:], junk[:], ss[:], rstd[:], 1024, EPS, [xn], ('junk', 'ss', 'rstd'))
                    k.ts('dve', xs[:], x_[:], rstd[:, 0:1], None, ALU.mult, None, [xn, 'rstd'], ['xs'])
                    for c in range(8):
                        k.tr(pt[c // 4][:, (c % 4) * 128:(c % 4 + 1) * 128], xs[:, c * 128:(c + 1) * 128], ['xs'], ['pt%d' % (c // 4)])
                    for hf in range(2):
                        k.tt('dve', ub[:, hf * 4:(hf + 1) * 4, j * 128:(j + 1) * 128].bitcast(F32R),
                             pt[hf][:].rearrange("p (c t) -> p c t", c=4),
                             gpre[:, hf * 4:(hf + 1) * 4].unsqueeze(2).to_broadcast([128, 4, 128]), ALU.mult,
                             ['pt%d' % hf, 'gpre'], [ubn + '_%d' % j])
                if kv:
                    r0 = (blk - 2) * 1024
                    k.dma(cblk[:], D["cosT"][r0:r0 + 1024, :].rearrange("(j p) n -> p j n", p=128), [], ['cblk'])
                    k.dma(sblk[:], D["sinT"][r0:r0 + 1024, :].rearrange("(j p) n -> p j n", p=128), [], ['sblk'])
            if idx + 2 < len(items):
                load_w_dma(idx + 2)
            c0, ncols, kind, dcol = COLBLOCKS[cb]
            ws = idx % 2
            for j in range(8):
                if j == 4 and idx + 1 < len(items):
                    load_w_round(idx + 1)
                pzn = pzi % 3; pzi += 1
                for c in range(8):
                    k.mm(pz[pzn][:, 0:ncols], ub[:, c, j * 128:(j + 1) * 128], wb[ws][:, c, 0:ncols], c == 0, c == 7,
                         [ubn + '_%d' % j, 'wb%d' % ws], ['pz%d' % pzn], fast=True)
                si = stgi % 3; stgi += 1
                sg = stg[si]; sgn = 'stg%d' % si
                lrow = blk * 1024 + j * 128
                if kind in ('Q', 'K'):
                    ri = rawi % 2; rawi += 1
                    rw = raw[ri]; rwn = 'raw%d' % ri
                    k.act(rw[:, 0:ncols], pz[pzn][:, 0:ncols], AF.Copy, ['pz%d' % pzn], [rwn])
                    nh = ncols // 64
                    r3 = rw[:, 0:ncols].rearrange("p (h two f) -> p h two f", two=2, f=32)
                    s3 = sg[:, 0:ncols].rearrange("p (h two f) -> p h two f", two=2, f=32)
                    x1, x2 = r3[:, :, 0, :], r3[:, :, 1, :]
                    c3 = cblk[:, j, 0:nh * 32].rearrange("p (h f) -> p h f", f=32)
                    n3 = sblk[:, j, 0:nh * 32].rearrange("p (h f) -> p h f", f=32)
                    tv = [t[:, 0:nh * 32].rearrange("p (h f) -> p h f", f=32) for t in tmp]
                    k.tt('dve', tv[0], x1, c3, ALU.mult, [rwn, 'cblk'], ['tmp0'])
                    k.tt('dve', tv[1], x2, n3, ALU.mult, [rwn, 'sblk'], ['tmp1'])
                    k.tt('dve', s3[:, :, 0, :], tv[0], tv[1], ALU.subtract, ['tmp0', 'tmp1'], [sgn])
                    k.tt('dve', tv[2], x2, c3, ALU.mult, [rwn, 'cblk'], ['tmp2'])
                    k.tt('dve', tv[3], x1, n3, ALU.mult, [rwn, 'sblk'], ['tmp3'])
                    k.tt('dve', s3[:, :, 1, :], tv[2], tv[3], ALU.add, ['tmp2', 'tmp3'], [sgn])
                    if kind == 'Q':
                        o0 = lrow - NPRE
                        k.dma(D["S_q"][o0:o0 + 128, dcol:dcol + ncols], sg[:, 0:ncols], [sgn], ['S_q'])
                    else:
                        o0 = lrow - KV0
                        k.dma(D["S_k"][o0:o0 + 128, dcol:dcol + ncols], sg[:, 0:ncols], [sgn], ['S_k'])
                elif kind == 'V':
                    k.act(sg[:, 0:ncols], pz[pzn][:, 0:ncols], AF.Copy, ['pz%d' % pzn], [sgn])
                    o0 = lrow - KV0
                    k.dma(D["S_v"][o0:o0 + 128, dcol:dcol + ncols], sg[:, 0:ncols], [sgn], ['S_v'])
                elif kind == 'R':
                    k.act(sg[:, 0:ncols], pz[pzn][:, 0:ncols], AF.Copy, ['pz%d' % pzn], [sgn])
                    k.dma(D["S_r"][1 + lrow:1 + lrow + 128, dcol:dcol + ncols], sg[:, 0:ncols], [sgn], ['S_r'])
                else:
                    k.act(sg[:, 0:ncols], pz[pzn][:, 0:ncols], AF.Sigmoid, ['pz%d' % pzn], [sgn])
                    o0 = lrow - NPRE
                    k.dma(D["S_g"][o0:o0 + 128, dcol:dcol + ncols], sg[:, 0:ncols], [sgn], ['S_g'])


def phase_b(nc, k, D):
    S = k.S
    with ExitStack() as es:
        T = lambda name, shape: es.enter_context(nc.sbuf_tensor(name, shape, F32))
        P = lambda name: es.enter_context(nc.psum_tensor(name, [128, 512], F32))
        acc = T("b_acc", [128, 4, NO])
        kt = [T("b_kt%d" % i, [128, 256]) for i in range(3)]
        vpad = [T("b_vpad%d" % i, [128, 4, 128]) for i in range(3)]
        ktp = [T("b_ktp%d" % i, [128, 4, 128]) for i in range(3)]
        qt = [T("b_qt%d" % i, [128, 256]) for i in range(2)]
        qT = [T("b_qT%d" % i, [128, 2, 128]) for i in range(2)]
        pexp = [T("b_pexp%d" % i, [128, 4, 256]) for i in range(2)]
        pm = [T("b_pm%d" % i, [128, 4, 256]) for i in range(2)]
        maskA = T("b_maskA", [128, 4, 256]); onesPad = T("b_onesPad", [128, 2, 128]); onesPre = T("b_onesPre", [128, 2, 128])
        ptr = P("b_ptr"); ptrq = P("b_ptrq")
        pss = [[P("b_ps%d_%d" % (i, j)) for j in range(2)] for i in range(2)]
        pso = [P("b_pso%d" % i) for i in range(2)]
        k.dma(maskA[:], D["maskA4"].rearrange("p (h n) -> p h n", h=4), [], ['maskA'])
        k.dma(onesPad[:], D["onesPad"].rearrange("p (e n) -> p e n", e=2), [], ['onesPad'])
        k.dma(onesPre[:], D["onesPre"].rearrange("p (e n) -> p e n", e=2), [], ['onesPre'])
        for i in range(3):
            S.op('pool', lambda e, i=i: e.memset(vpad[i][:], 0.0), [], ['vpad%d' % i])
            S.op('pool', lambda e, i=i: e.memset(ktp[i][:], 0.0), [], ['ktp%d' % i])
        ptrv = ptr[:].rearrange("p (c t) -> p c t", c=4)
        ptrqv = ptrq[:].rearrange("p (c t) -> p c t", c=4)
        iters = []
        it = 0
        for g, d in enumerate((1, 4, 16)):
            nsub = 32 // d
            for r in range(d):
                for n in range(-1, nsub):
                    qs = None
                    if n >= 0:
                        qs = it % 2; it += 1
                    iters.append((g, d, r, n, qs))

        def front(i):
            g, d, r, n, qs = iters[i]
            slot = (n + 1) % 3
            start = KV0 + n * 128 * d + r
            rows = slice(start, start + 127 * d + 1, d)
            k.dma(kt[slot][:], D["S_k"][rows, g * 256:(g + 1) * 256], ['S_k'], ['kt%d' % slot])
            vsrc = D["S_v"][rows, g * 256:(g + 1) * 256].rearrange("k (p e d) -> k p e d", p=2, e=2)
            vdst = vpad[slot][:].rearrange("k (p e) (f d) -> k p e f d", e=2, f=2)
            for e_ in range(2):
                k.dma(vdst[:, :, e_, e_, :], vsrc[:, :, e_, :], ['S_v'], ['vpad%d' % slot])
            for p in range(2):
                k.tr(ptrv[:, p, :], kt[slot][:, p * 128:(p + 1) * 128], ['kt%d' % slot], ['pkb'])
            kdst = ktp[slot][:].rearrange("k (p e) s -> k p e s", e=2)
            k.act(kdst[0:64, :, 0, :], ptrv[0:64, 0:2, :], AF.Copy, ['pkb'], ['ktp%d' % slot])
            k.S.op('dve', lambda e, kdst=kdst: e.tensor_copy(out=kdst[64:128, :, 1, :], in_=ptrv[64:128, 0:2, :]), ['pkb'], ['ktp%d' % slot])
            if n < 0:
                return
            q0 = n * 128 * d + r
            toks = slice(q0, q0 + 127 * d + 1, d)
            k.dma(qt[qs][:], D["S_q"][toks, g * 256:(g + 1) * 256], ['S_q'], ['qt%d' % qs])
            for p in range(2):
                k.tr(ptrqv[:, p, :], qt[qs][:, p * 128:(p + 1) * 128], ['qt%d' % qs], ['pqb'])
            k.S.op('dve', lambda e, qs=qs: e.tensor_copy(out=qT[qs][:], in_=ptrqv[:, 0:2, :]), ['pqb'], ['qT%d' % qs])
            prev, cur = n % 3, slot
            for h in range(4):
                p = h // 2
                bank = pss[qs][h // 2]; bn = 'pss%d_%d' % (qs, h // 2)
                bv = bank[:].rearrange("k (h n) -> k h n", h=2)
                k.mm(bv[:, h % 2, 0:128], ktp[prev][:, h, :], qT[qs][:, p, :], True, True, ['ktp%d' % prev, 'qT%d' % qs], [bn])
                k.mm(bv[:, h % 2, 128:256], ktp[cur][:, h, :], qT[qs][:, p, :], True, True, ['ktp%d' % cur, 'qT%d' % qs], [bn])

        def back(i):
            g, d, r, n, qs = iters[i]
            if n < 0:
                return
            slot = (n + 1) % 3
            prev, cur = n % 3, slot
            q0 = n * 128 * d + r
            toks = slice(q0, q0 + 127 * d + 1, d)
            for hb in range(2):
                k.act(pexp[qs][:, 2 * hb:2 * hb + 2, :], pss[qs][hb][:].rearrange("k (h n) -> k h n", h=2), AF.Exp,
                      ['pss%d_%d' % (qs, hb)], ['pexp%d' % qs], scale=0.125)
            k.tt('dve', pm[qs][:], pexp[qs][:], maskA[:], ALU.mult, ['pexp%d' % qs, 'maskA'], ['pm%d' % qs])
            po = pso[qs]; pon = 'pso%d' % qs
            pov = po[:].rearrange("k (c t) -> k c t", c=4)
            first = True
            nmm = 0
            for p in range(2):
                for e_ in range(2):
                    h = 2 * p + e_
                    for which, sl in ((0, prev), (1, cur)):
                        ones = onesPre if (which == 0 and n == 0) else onesPad
                        rhs = pm[qs][:, h, which * 128:(which + 1) * 128]
                        nmm += 2
                        k.mm(pov[:, p, :], vpad[sl][:, h, :], rhs, first, False, ['vpad%d' % sl, 'pm%d' % qs], [pon])
                        first = False
                        k.mm(pov[:, 2 + p, :], ones[:, e_, :], rhs, False, nmm == 16, ['onesPad', 'onesPre', 'pm%d' % qs], [pon])
            if g == 0:
                k.act(acc[:, :, toks], pov, AF.Copy, [pon], ['acc'])
            else:
                k.tt('dve', acc[:, :, toks], acc[:, :, toks], pov, ALU.add, [pon, 'acc'], ['acc'])

        def capture(fn, i):
            S.begin_capture()
            fn(i)
            return S.end_capture()

        def merge(lb, la):
            out = []
            ia = 0
            nb, na = len(lb), len(la)
            for ib, item in enumerate(lb):
                out.append(item)
                tgt = ((ib + 1) * na) // max(nb, 1)
                while ia < tgt:
                    out.append(la[ia]); ia += 1
            out.extend(la[ia:])
            return out

        S.replay(capture(front, 0))
        for i in range(len(iters)):
            lb = capture(back, i)
            la = capture(front, i + 1) if i + 1 < len(iters) else []
            S.replay(merge(lb, la))
        for q4 in range(4):
            sl = slice(q4 * 1024, (q4 + 1) * 1024)
            k.S.op('dve', lambda e, sl=sl: e.reciprocal(out=acc[:, 2:4, sl], in_=acc[:, 2:4, sl]), ['acc'], ['acc'])
            k.tt('dve', acc[:, 0:2, sl], acc[:, 0:2, sl], acc[:, 2:4, sl], ALU.mult, ['acc'], ['acc'])
        k.dma(D["S_oaT"].rearrange("(p k) t -> k p t", p=2), acc[:, 0:2, :], ['acc'], ['S_oaT'])


def phase_c(nc, k, D):
    S = k.S
    NCH = int(os.environ.get('C_NCH', NL // 128))
    STG = int(os.environ.get('C_STAGE', 99))
    SUB = int(os.environ.get('C_SUB', 99))
    with ExitStack() as es:
        T = lambda name, shape: es.enter_context(nc.sbuf_tensor(name, shape, F32))
        P = lambda name: es.enter_context(nc.psum_tensor(name, [128, 512], F32))
        B = [P("c_b%d" % i) for i in range(8)]
        bn = ['cb%d' % i for i in range(8)]
        zc = [T("c_zc%d" % i, [128, 1696]) for i in range(2)]
        zp = T("c_zp", [128, 1696])
        bc = {}
        mu = T("c_mu", [128, 1696])
        k.dma(mu[:], D["mu"].partition_broadcast(128), [], ['cst'])
        for nm in ("w0", "a0", "k_k", "k_a", "ln_w", "ln_b", "r_k"):
            bc[nm] = T("c_" + nm, [128, 512])
            k.dma(bc[nm][:], D[nm].partition_broadcast(128), [], ['cst'])
        w2pad = T("c_w2pad", [64, 512]); a2pad = T("c_a2pad", [64, 512]); g2 = T("c_g2", [96, 512])
        for dst_, nm_, np_ in ((w2pad, "w2pad", 64), (a2pad, "a2pad", 64), (g2, "g2", 96)):
            k.dma(zp[0:np_, 0:512], D[nm_][:, :], [], ['zp'])
            k.act(dst_[:].bitcast(F32R), zp[0:np_, 0:512], AF.Copy, ['zp'], ['cst'])
        maskR = T("c_maskR", [128, 512]); tri = T("c_tri", [128, 128]); bm4 = T("c_bm4", [128, 4, 128]); onec = T("c_onec", [128, 1])
        k.dma(maskR[:], D["maskR"][:, :], [], ['cst']); k.dma(tri[:], D["tri"][:, :], [], ['cst'])
        k.dma(bm4[:], D["blockmask4"].rearrange("p (c n) -> p c n", c=4), [], ['cst'])
        S.op('pool', lambda e: e.memset(onec[:], 1.0), [], ['onec'])
        lor = T("c_lor", [128, 256]); lorT1 = T("c_lorT1", [64, 128]); lorT2 = T("c_lorT2", [96, 128])
        names = ["dws", "lw", "aa", "kk", "kkn", "k2", "bv", "rkr", "ecum", "einv", "eex", "rt", "ysb", "yc"]
        t = {nm: T("c_" + nm, [128, 512]) for nm in names}
        atd = [T("c_at%d" % i, [128, 512]) for i in range(2)]
        bonus2 = [T("c_bonus%d" % i, [128, 512]) for i in range(2)]
        gsb2 = [T("c_gsb%d" % i, [128, 512]) for i in range(2)]
        btd = [T("c_btd%d" % i, [128, 512]) for i in range(2)]
        ktd = [T("c_ktd%d" % i, [128, 512]) for i in range(2)]
        sm = {nm: T("c_" + nm, [128, 8]) for nm in ("ss8", "nrm", "rs8", "s1", "s2")}
        art = [T("c_art%d" % i, [128, 4, 2, 128]) for i in range(2)]
        btTp = T("c_btTp", [128, 8, 128]); ktTp = T("c_ktTp", [128, 8, 128])
        A1 = T("c_A1", [128, 8, 2, 128])
        A2 = [T("c_A2%d" % i, [128, 8, 2, 128]) for i in range(2)]
        Lb = [T("c_L%d" % i, [128, 8, 128]) for i in range(2)]
        LTb = [T("c_LT%d" % i, [128, 8, 128]) for i in range(2)]
        Mb = [T("c_M%d" % i, [128, 8, 128]) for i in range(2)]
        WT = [T("c_WT%d" % i, [128, 4, 128]) for i in range(2)]
        X0 = [T("c_X0%d" % i, [128, 512]) for i in range(2)]
        U0 = [T("c_U0%d" % i, [128, 512]) for i in range(2)]
        U = [T("c_U%d" % i, [128, 512]) for i in range(2)]
        Tb = [T("c_T%d" % i, [128, 4, 128]) for i in range(2)]
        maskP = [T("c_maskP%d" % i, [128, 4, 128]) for i in range(2)]
        pc = T("c_pc", [128, 4])
        orT = T("c_orT", [128, 4, 128])
        S.op('pool', lambda e: e.memset(Tb[0][:], 0.0), [], ['T0'])
        S.op('pool', lambda e: e.memset(lor[:], 0.0), [], ['lor'])
        S.op('pool', lambda e: e.memset(btTp[:], 0.0), [], ['btTp'])
        S.op('pool', lambda e: e.memset(ktTp[:], 0.0), [], ['ktTp'])
        S.op('dve', lambda e: e.tensor_copy(out=btTp[:].bitcast(F32R), in_=btTp[:]), ['btTp'], ['btTp'])
        S.op('dve', lambda e: e.tensor_copy(out=ktTp[:].bitcast(F32R), in_=ktTp[:]), ['ktTp'], ['ktTp'])
        h8 = lambda ap: ap.rearrange("p (h d) -> p h d", h=8)
        bc8 = lambda ap: ap.unsqueeze(2).to_broadcast([128, 8, 64])
        c4 = lambda ap: ap.rearrange("p (c t) -> p c t", c=4)

        def stage_a(c):
            s = c % 2
            own = c >= NCH // 2
            t0 = c * 128
            z = zc[s]; zn = 'zc%d' % s
            k.dma(z[:], D["S_r"][1 + t0:1 + t0 + 128, :], ['S_r', 'S_r0'], [zn])
            k.dma(zp[:], D["S_r"][t0:t0 + 128, :], ['S_r', 'S_r0'], ['zp'])
            k.tt('dve', zp[:], zp[:], z[:], ALU.subtract, ['zp', zn], ['zp'])
            k.tt('dve', zp[:], zp[:], mu[:], ALU.mult, ['zp', 'cst'], ['zp'])
            k.tt('dve', z[:], z[:], zp[:], ALU.add, ['zp', zn], [zn])
            r_, k_, v_ = z[:, 0:512], z[:, 512:1024], z[:, 1024:1536]
            k.act(lor[:, 0:32], z[:, 1536:1568], AF.Tanh, [zn], ['lor'])
            k.act(lor[:, 32:64], z[:, 1568:1600], AF.Copy, [zn], ['lor'])
            k.act(lor[:, 128:224], z[:, 1600:1696], AF.Sigmoid, [zn], ['lor'])
            k.tr(B[0][:, 0:128], lor[:, 0:128], ['lor'], [bn[0]])
            k.tr(B[0][:, 128:256], lor[:, 128:256], ['lor'], [bn[0]])
            k.act(lorT1[:].bitcast(F32R), B[0][0:64, 0:128], AF.Copy, [bn[0]], ['lorT1'])
            k.S.op('dve', lambda e: e.tensor_copy(out=lorT2[:].bitcast(F32R), in_=B[0][0:96, 128:256]), [bn[0]], ['lorT2'])
            k.mm(B[1][:], lorT1[:], w2pad[:], True, True, ['lorT1', 'cst'], [bn[1]], fast=True)
            k.mm(B[2][:], lorT1[:], a2pad[:], True, True, ['lorT1', 'cst'], [bn[2]], fast=True)
            gs = gsb2[s]; gsn = 'gsb%d' % s
            if own:
                k.mm(B[0][:], lorT2[:], g2[:], True, True, ['lorT2', 'cst'], [bn[0]], fast=True)
                k.act(gs[:], B[0][:], AF.Copy, [bn[0]], [gsn])
            k.tt('dve', t["dws"][:], B[1][:], bc["w0"][:], ALU.add, [bn[1], 'cst'], ['dws'])
            k.act(t["lw"][:], t["dws"][:], AF.Sigmoid, ['dws'], ['lw'])
            k.S.op('act', lambda e: e.mul(out=t["lw"][:], in_=t["lw"][:], mul=-float(np.exp(-0.5))), ['lw'], ['lw'])
            k.tt('dve', t["dws"][:], B[2][:], bc["a0"][:], ALU.add, [bn[2], 'cst', 'lw'], ['dws'])
            k.act(t["aa"][:], t["dws"][:], AF.Sigmoid, ['dws'], ['aa'])
            k.tt('dve', t["kk"][:], k_, bc["k_k"][:], ALU.mult, [zn, 'cst'], ['kk'])
            k.tt('dve', t["rkr"][:], t["kk"][:], t["kk"][:], ALU.mult, ['kk'], ['rkr'])
            k.S.op('dve', lambda e: e.tensor_reduce(out=sm["ss8"][:], in_=h8(t["rkr"][:]), axis=AX.X, op=ALU.add), ['rkr'], ['ss8'])
            k.act(sm["nrm"][:], sm["ss8"][:], AF.Sqrt, ['ss8'], ['nrm'])
            k.ts('dve', sm["nrm"][:], sm["nrm"][:], 1e-12, None, ALU.max, None, ['nrm'], ['nrm'])
            k.S.op('dve', lambda e: e.reciprocal(out=sm["nrm"][:], in_=sm["nrm"][:]), ['nrm'], ['nrm'])
            k.tt('dve', h8(t["kkn"][:]), h8(t["kk"][:]), bc8(sm["nrm"][:]), ALU.mult, ['kk', 'nrm'], ['kkn'])
            k.stt(t["k2"][:], t["aa"][:], -1.0, bc["k_a"][:], ALU.add, ALU.mult, ['aa', 'cst'], ['k2'])
            k.stt(t["k2"][:], t["k2"][:], 1.0, k_, ALU.add, ALU.mult, ['k2', zn], ['k2'])
            k.tt('dve', t["bv"][:], t["kkn"][:], t["aa"][:], ALU.mult, ['kkn', 'aa'], ['bv'])
            bo = bonus2[s]; bon = 'bonus%d' % s
            if own:
                k.tt('dve', t["rkr"][:], r_, t["k2"][:], ALU.mult, [zn, 'k2', 'ss8'], ['rkr'])
                k.tt('dve', t["rkr"][:], t["rkr"][:], bc["r_k"][:], ALU.mult, ['rkr', 'cst'], ['rkr'])
                k.S.op('dve', lambda e: e.tensor_reduce(out=sm["rs8"][:], in_=h8(t["rkr"][:]), axis=AX.X, op=ALU.add), ['rkr'], ['rs8'])
                k.tt('dve', h8(bo[:]), h8(v_), bc8(sm["rs8"][:]), ALU.mult, [zn, 'rs8'], [bon])
            k.mm(B[1][:], tri[:], t["lw"][:], True, True, ['cst', 'lw'], [bn[1]])
            k.act(t["ecum"][:], B[1][:], AF.Exp, [bn[1]], ['ecum'])
            k.act(t["einv"][:], B[1][:], AF.Exp, [bn[1]], ['einv'], scale=-1.0)
            k.tt('dve', t["dws"][:], B[1][:], t["lw"][:], ALU.subtract, [bn[1], 'lw', 'aa'], ['dws'])
            k.act(t["eex"][:], t["dws"][:], AF.Exp, ['dws'], ['eex'])
            for p in range(4):
                k.mm(B[0][:, 256 + p:257 + p], t["lw"][:, p * 128:(p + 1) * 128], onec[:], True, True, ['lw', 'onec'], [bn[0]])
            k.act(pc[:], B[0][:, 256:260], AF.Exp, [bn[0]], ['pc'])
            mP = maskP[s]; mPn = 'maskP%d' % s
            k.tt('dve', mP[:], bm4[:], pc[:].unsqueeze(2).to_broadcast([128, 4, 128]), ALU.mult, ['cst', 'pc'], [mPn])
            bt_, kt_, at_ = btd[s], ktd[s], atd[s]
            btn, ktn, atn = 'btd%d' % s, 'ktd%d' % s, 'atd%d' % s
            k.tt('dve', t["rt"][:], r_, t["ecum"][:], ALU.mult, [zn, 'ecum'], ['rt'])
            k.stt(at_[:], t["kkn"][:], -1.0, t["eex"][:], ALU.mult, ALU.mult, ['kkn', 'eex'], [atn])
            k.tt('dve', bt_[:], t["bv"][:], t["einv"][:], ALU.mult, ['bv', 'einv'], [btn])
            k.tt('dve', kt_[:], t["k2"][:], t["einv"][:], ALU.mult, ['k2', 'einv'], [ktn])
            ar = art[s]; arn = 'art%d' % s
            for p in range(4):
                k.tr(B[0][:, p * 128:(p + 1) * 128], at_[:, p * 128:(p + 1) * 128], [atn], [bn[0]])
            k.act(ar[:, :, 0, :].bitcast(F32R), c4(B[0][:]), AF.Copy, [bn[0]], [arn])
            for p in range(4):
                k.tr(B[1][:, p * 128:(p + 1) * 128], t["rt"][:, p * 128:(p + 1) * 128], ['rt'], [bn[1]])
            k.S.op('dve', lambda e: e.tensor_copy(out=ar[:, :, 1, :].bitcast(F32R), in_=c4(B[1][:])), [bn[1]], [arn])
            for (src, srcn, dst, dn, bi) in ((bt_, btn, btTp, 'btTp', 2), (kt_, ktn, ktTp, 'ktTp', 0)):
                for p in range(4):
                    k.tr(B[bi][:, p * 128:(p + 1) * 128], src[:, p * 128:(p + 1) * 128], [srcn], [bn[bi]])
                dv = dst[:].rearrange("k (p e) s -> k p e s", e=2)
                sv = c4(B[bi][:])
                k.act(dv[0:64, :, 0, :].bitcast(F32R), sv[0:64, :, :], AF.Copy, [bn[bi]], [dn])
                k.S.op('dve', lambda e, dv=dv, sv=sv: e.tensor_copy(out=dv[64:128, :, 1, :].bitcast(F32R), in_=sv[64:128, :, :]), [bn[bi]], [dn])
            a2 = A2[s]; a2n = 'A2%d' % s
            mstrict = maskR[:, 0:128].unsqueeze(1).to_broadcast([128, 2, 128])
            mincl = maskR[:, 128:256].unsqueeze(1).to_broadcast([128, 2, 128])
            for h in range(8):
                p = h // 2
                bi = h % 3
                rhs = ar[:, p, :, :].rearrange("k a t -> k (a t)")
                k.mm(B[bi][:, 0:256], btTp[:, h, :], rhs, True, True, ['btTp', arn], [bn[bi]], fast=True)
                k.mm(B[bi][:, 256:512], ktTp[:, h, :], rhs, True, True, ['ktTp', arn], [bn[bi]], fast=True)
                bv4 = B[bi][:].rearrange("k (x a t) -> k x a t", x=2, a=2)
                k.tt('dve', A1[:, h, :, :], bv4[:, :, 0, :], mstrict, ALU.mult, [bn[bi], 'cst'], ['A1'])
                if own:
                    k.tt('dve', a2[:, h, :, :], bv4[:, :, 1, :], mincl, ALU.mult, [bn[bi], 'cst'], [a2n])
            for h in range(8):
                k.mm(B[0][:, h * 64:(h + 1) * 64], A1[:, h, 1, :], z[:, 1024 + h * 64:1024 + (h + 1) * 64], True, True, ['A1', zn], [bn[0]])
            k.act(X0[s][:], B[0][:], AF.Copy, [bn[0]], ['X0%d' % s])

        def stage_b(c):
            s = c % 2
            own = c >= NCH // 2
            z = zc[s]; zn = 'zc%d' % s
            ar = art[s]; arn = 'art%d' % s
            a2 = A2[s]; a2n = 'A2%d' % s
            bt_, kt_, at_ = btd[s], ktd[s], atd[s]
            btn, ktn, atn = 'btd%d' % s, 'ktd%d' % s, 'atd%d' % s
            mP = maskP[s]; mPn = 'maskP%d' % s
            gs = gsb2[s]; gsn = 'gsb%d' % s
            bo = bonus2[s]; bon = 'bonus%d' % s
            L, LT, M = Lb[0], LTb[0], Mb[0]
            k.act(L[:], A1[:, :, 0, :], AF.Copy, ['A1'], ['L0'])
            for hh in range(2):
                bb = 3 + hh
                for h in range(4 * hh, 4 * hh + 4):
                    k.tr(B[bb][:, (h % 4) * 128:(h % 4 + 1) * 128], A1[:, h, 0, :], ['A1'], [bn[bb]])
                k.act(LT[:, 4 * hh:4 * hh + 4, :], c4(B[bb][:]), AF.Copy, [bn[bb]], ['LT0'])
            k.tt('dve', M[:], L[:], k.ident[:].unsqueeze(1).to_broadcast([128, 8, 128]), ALU.add, ['L0', 'ident'], ['M0'])
            cur = 0
            for rnd in range(6):
                nx = 1 - cur
                L, LT, M = Lb[cur], LTb[cur], Mb[cur]
                Ln, LTn, Mn = Lb[nx], LTb[nx], Mb[nx]
                last = rnd == 5
                for hh in range(2):
                    bL, bLT, bM = (3, 4, 5) if hh == 0 else (6, 7, 5)
                    hs = range(4 * hh, 4 * hh + 4)
                    if not last:
                        for h in hs:
                            k.mm(B[bL][:, (h % 4) * 128:(h % 4 + 1) * 128], LT[:, h, :], L[:, h, :], True, True, ['L%d' % cur, 'LT%d' % cur], [bn[bL]])
                    for h in hs:
                        k.mm(B[bLT][:, (h % 4) * 128:(h % 4 + 1) * 128], L[:, h, :], LT[:, h, :], True, True, ['L%d' % cur, 'LT%d' % cur], [bn[bLT]])
                    if not last:
                        k.act(Ln[:, 4 * hh:4 * hh + 4, :], c4(B[bL][:]), AF.Copy, [bn[bL]], ['L%d' % nx])
                    k.S.op('dve', lambda e, LTn=LTn, hh=hh, bLT=bLT: e.tensor_copy(out=LTn[:, 4 * hh:4 * hh + 4, :], in_=c4(B[bLT][:])),
                           [bn[bLT]], ['LT%d' % nx])
                    for h in hs:
                        k.mm(B[bM][:, (h % 4) * 128:(h % 4 + 1) * 128], LTn[:, h, :], M[:, h, :], True, True, ['LT%d' % nx, 'M%d' % cur], [bn[bM]])
                    k.tt('dve', Mn[:, 4 * hh:4 * hh + 4, :], M[:, 4 * hh:4 * hh + 4, :], c4(B[bM][:]), ALU.add,
                         [bn[bM], 'M%d' % cur], ['M%d' % nx])
                cur = nx
            Minv = Mb[cur]; Mn_ = 'M%d' % cur
            for h in range(8):
                k.mm(B[3][:, h * 64:(h + 1) * 64], Minv[:, h, :], X0[s][:, h * 64:(h + 1) * 64], True, True, [Mn_, 'X0%d' % s], [bn[3]])
            k.act(U0[s][:], B[3][:], AF.Copy, [bn[3]], ['U0%d' % s])
            for h in range(8):
                p = h // 2
                bi = 6 + h // 4
                k.mm(B[bi][:, (h % 4) * 128:(h % 4 + 1) * 128], at_[:, p * 128:(p + 1) * 128], Minv[:, h, :], True, True, [atn, Mn_], [bn[bi]])
            for hh in range(2):
                sv = B[6 + hh][:].rearrange("k (p e t) -> k p e t", p=2, e=2)
                k.act(WT[s][0:64, 2 * hh:2 * hh + 2, :], sv[0:64, :, 0, :], AF.Copy, [bn[6 + hh]], ['WT%d' % s])
                k.S.op('dve', lambda e, sv=sv, hh=hh, s=s: e.tensor_copy(out=WT[s][64:128, 2 * hh:2 * hh + 2, :], in_=sv[64:128, :, 1, :]), [bn[6 + hh]], ['WT%d' % s])
            Tc, Tn = Tb[c % 2], Tb[(c + 1) % 2]
            Tcn, Tnn = 'T%d' % (c % 2), 'T%d' % ((c + 1) % 2)
            for p in range(4):
                k.mm(B[3][:, p * 128:(p + 1) * 128], WT[s][:, p, :], Tc[:, p, :], True, True, ['WT%d' % s, Tcn], [bn[3]])
            k.tt('dve', U[s][:], B[3][:], U0[s][:], ALU.add, [bn[3], 'U0%d' % s], ['U%d' % s])
            if own:
                first = True
                for p in range(4):
                    k.mm(B[5][:, p * 128:(p + 1) * 128], ar[:, p, 1, :], Tc[:, p, :], first, False, [arn, Tcn], [bn[5]])
                    first = False
                for h in range(8):
                    k.mm(B[5][:, h * 64:(h + 1) * 64], a2[:, h, 0, :], U[s][:, h * 64:(h + 1) * 64], False, False, [a2n, 'U%d' % s], [bn[5]])
                    k.mm(B[5][:, h * 64:(h + 1) * 64], a2[:, h, 1, :], z[:, 1024 + h * 64:1024 + (h + 1) * 64], False, h == 7, [a2n, zn], [bn[5]])
            k.mm(B[4][:], k.ident[:], Tc[:].rearrange("k p n -> k (p n)"), True, False, ['ident', Tcn], [bn[4]])
            for p in range(4):
                k.mm(B[4][:, p * 128:(p + 1) * 128], kt_[:, p * 128:(p + 1) * 128], z[:, 1024 + p * 128:1024 + (p + 1) * 128], False, False, [ktn, zn], [bn[4]])
            for p in range(4):
                k.mm(B[4][:, p * 128:(p + 1) * 128], bt_[:, p * 128:(p + 1) * 128], U[s][:, p * 128:(p + 1) * 128], False, p == 3, [btn, 'U%d' % s], [bn[4]])
            k.tt('dve', Tn[:].rearrange("k p n -> k (p n)"), B[4][:], mP[:].rearrange("k p n -> k (p n)"), ALU.mult, [bn[4], mPn], [Tnn])
            if own:
                y, yc = t["ysb"], t["yc"]
                k.act(y[:], B[5][:], AF.Copy, [bn[5]], ['ysb'])
                k.S.op('dve', lambda e: e.tensor_reduce(out=sm["s1"][:], in_=h8(y[:]), axis=AX.X, op=ALU.add), ['ysb'], ['s1'])
                k.ts('dve', sm["s1"][:], sm["s1"][:], -1.0 / 64, None, ALU.mult, None, ['s1'], ['s1'])
                k.tt('dve', h8(yc[:]), h8(y[:]), bc8(sm["s1"][:]), ALU.add, ['ysb', 's1'], ['yc'])
                k.tt('dve', y[:], yc[:], yc[:], ALU.mult, ['yc'], ['ysb'])
                k.S.op('dve', lambda e: e.tensor_reduce(out=sm["s2"][:], in_=h8(y[:]), axis=AX.X, op=ALU.add), ['ysb'], ['s2'])
                k.ts('dve', sm["s2"][:], sm["s2"][:], 1.0 / 64, GN_EPS, ALU.mult, ALU.add, ['s2'], ['s2'])
                k.act(sm["s2"][:], sm["s2"][:], AF.Sqrt, ['s2'], ['s2'])
                k.S.op('dve', lambda e: e.reciprocal(out=sm["s2"][:], in_=sm["s2"][:]), ['s2'], ['s2'])
                k.tt('dve', h8(yc[:]), h8(yc[:]), bc8(sm["s2"][:]), ALU.mult, ['yc', 's2'], ['yc'])
                k.tt('dve', yc[:], yc[:], bc["ln_w"][:], ALU.mult, ['yc', 'cst'], ['yc'])
                k.tt('dve', yc[:], yc[:], bc["ln_b"][:], ALU.add, ['yc', 'cst'], ['yc'])
                k.tt('dve', yc[:], yc[:], bo[:], ALU.add, ['yc', bon], ['yc'])
                k.tt('dve', yc[:], yc[:], gs[:], ALU.mult, ['yc', gsn], ['yc'])
                for p in range(4):
                    k.tr(B[5][:, p * 128:(p + 1) * 128], yc[:, p * 128:(p + 1) * 128], ['yc'], [bn[5]])
                k.act(orT[:], c4(B[5][:]), AF.Copy, [bn[5]], ['orT'])
                o0 = (c - NCH // 2) * 128
                k.dma(D["S_orT"][:, o0:o0 + 128].rearrange("(c p) t -> p c t", p=128), orT[:], ['orT'], ['S_orT'])

        def capture(fn, c):
            S.begin_capture()
            fn(c)
            return S.end_capture()

        def merge(lb, la):
            out = []
            ia = 0
            nb, na = len(lb), len(la)
            for ib, it in enumerate(lb):
                out.append(it)
                tgt = ((ib + 1) * na) // max(nb, 1)
                while ia < tgt:
                    out.append(la[ia]); ia += 1
            out.extend(la[ia:])
            return out

        S.replay(capture(stage_a, 0))
        for c in range(NCH):
            lb = capture(stage_b, c)
            la = capture(stage_a, c + 1) if c + 1 < NCH else []
            S.replay(merge(lb, la))


def phase_d(nc, k, D):
    S = k.S
    TB = 256
    NB = NO // TB
    with ExitStack() as es:
        T = lambda name, shape: es.enter_context(nc.sbuf_tensor(name, shape, F32))
        P = lambda name: es.enter_context(nc.psum_tensor(name, [128, 512], F32))
        B = [P("d_b%d" % i) for i in range(8)]
        bn = ['db%d' % i for i in range(8)]
        NW = 2
        wst = [T("d_wst%d" % i, [128, 4096]) for i in range(NW)]
        wraw = [T("d_wraw%d" % i, [128, 4096]) for i in range(3)]
        oaTr = T("d_oaTr", [128, 2, 128]); orTr = T("d_orTr", [128, 4, 128])
        wau = T("d_wau", [128, 2, 1024]); wru = T("d_wru", [128, 4, 1024]); wple = T("d_wple", [128, 2, 1024])
        k.dma(wraw[0][:, 0:2048].rearrange("p (c n) -> p c n", c=2), D["w_au"].rearrange("(c p) n -> p c n", p=128), [], ['wraw0'])
        k.act(wau[:].bitcast(F32R), wraw[0][:, 0:2048].rearrange("p (c n) -> p c n", c=2), AF.Copy, ['wraw0'], ['cst'])
        k.dma(wraw[0][:, 0:4096].rearrange("p (c n) -> p c n", c=4), D["w_ru"].rearrange("(c p) n -> p c n", p=128), [], ['wraw0'])
        k.act(wru[:].bitcast(F32R), wraw[0][:, 0:4096].rearrange("p (c n) -> p c n", c=4), AF.Copy, ['wraw0'], ['cst'])
        k.dma(wraw[0][:, 0:2048].rearrange("p (c n) -> p c n", c=2), D["w_ple"].rearrange("(c p) n -> p c n", p=128), [], ['wraw0'])
        k.act(wple[:].bitcast(F32R), wraw[0][:, 0:2048].rearrange("p (c n) -> p c n", c=2), AF.Copy, ['wraw0'], ['cst'])
        gffn = T("d_gffn", [128, 8]); k.dma(gffn[:], D["gffn"][:, :], [], ['cst'])
        gm = T("d_gm", [128, 1024]); gf = T("d_gf", [128, 1024]); gp = T("d_gp", [128, 1024])
        k.dma(gm[:], D["g_mpost"].partition_broadcast(128), [], ['cst'])
        k.dma(gf[:], D["g_fpost"].partition_broadcast(128), [], ['cst'])
        k.dma(gp[:], D["g_ppost"].partition_broadcast(128), [], ['cst'])
        hT = T("d_hT", [128, 32, TB]); fT = T("d_fT", [128, 8, TB])
        hres = [T("d_h%d" % i, [128, 1024]) for i in range(2)]
        gates = T("d_gates", [128, 2048])
        oaT = T("d_oaT", [128, 2, 128]); orT = T("d_orT", [128, 4, 128])
        mer = T("d_mer", [128, 1024]); mer2 = T("d_mer2", [128, 1024]); junk = T("d_junk", [128, 1024])
        xT = T("d_xT", [128, 8, 128])
        pt_ = T("d_pt", [128, 256]); pT = T("d_pT", [128, 2, 128])
        ss = T("d_ss", [128, 1]); rstd = T("d_rstd", [128, 1])
        rel = [T("d_rel%d" % i, [128, TB]) for i in range(2)]
        wi = [0]

        def wload(src_ap, shape_kind):
            s = wi[0] % NW; s2 = wi[0] % 3; wi[0] += 1
            if shape_kind == 'col':
                k.dma(wraw[s2][:].rearrange("p (c n) -> p c n", c=8), src_ap.rearrange("(c p) n -> p c n", p=128), [], ['wraw%d' % s2])
            else:
                k.dma(wraw[s2][:].rearrange("p (c n) -> p c n", c=4), src_ap.rearrange("(c p) n -> p c n", p=128), [], ['wraw%d' % s2])
            if (wi[0] % 2) == 0:
                k.act(wst[s][:].bitcast(F32R), wraw[s2][:], AF.Copy, ['wraw%d' % s2], ['wst%d' % s])
            else:
                k.S.op('dve', lambda e: e.tensor_copy(out=wst[s][:].bitcast(F32R), in_=wraw[s2][:]), ['wraw%d' % s2], ['wst%d' % s])
            return s

        def transpose8(src, srcn, dst3, dstn, scale_ap=None):
            for c in range(8):
                k.tr(B[4 + c // 4][:, (c % 4) * 128:(c % 4 + 1) * 128], src[:, c * 128:(c + 1) * 128], [srcn], [bn[4 + c // 4]])
            for hf in range(2):
                pv = B[4 + hf][:].rearrange("p (c t) -> p c t", c=4)
                if scale_ap is None:
                    if hf == 0:
                        k.act(dst3[:, 0:4, :].bitcast(F32R), pv, AF.Copy, [bn[4]], [dstn])
                    else:
                        k.S.op('dve', lambda e, pv=pv: e.tensor_copy(out=dst3[:, 4:8, :].bitcast(F32R), in_=pv), [bn[5]], [dstn])
                else:
                    k.tt('dve', dst3[:, hf * 4:(hf + 1) * 4, :].bitcast(F32R), pv, scale_ap[:, hf * 4:(hf + 1) * 4].unsqueeze(2).to_broadcast([128, 4, 128]),
                         ALU.mult, [bn[4 + hf], 'cst'], [dstn])

        def dense1024(lhs3, lhsn, nk, wslots_or_res, banks):
            for hf in range(2):
                for c in range(nk):
                    wap, wn = wslots_or_res(c, hf)
                    k.mm(B[banks[hf]][:], lhs3[:, c, :], wap, c == 0, c == nk - 1, [lhsn, wn], [bn[banks[hf]]], fast=True)

        for blk in range(NB):
            ws_out = [wload(D["w_out"][:, hf * 512:(hf + 1) * 512], 'col') for hf in range(2)]
            for j in range(TB // 128):
                o0 = blk * TB + j * 128
                h_ = hres[j]; hn = 'h%d' % j
                k.dma(h_[:], D["xloc"][NPRE + o0:NPRE + o0 + 128, :], [], [hn])
                k.dma(oaTr[:], D["S_oaT"][:, o0:o0 + 128].rearrange("(c p) t -> p c t", p=128), ['S_oaT'], ['oaTr'])
                k.dma(orTr[:], D["S_orT"][:, o0:o0 + 128].rearrange("(c p) t -> p c t", p=128), ['S_orT'], ['orTr'])
                k.S.op('dve', lambda e: e.tensor_copy(out=oaT[:].bitcast(F32R), in_=oaTr[:]), ['oaTr'], ['oaT'])
                k.S.op('dve', lambda e: e.tensor_copy(out=orT[:].bitcast(F32R), in_=orTr[:]), ['orTr'], ['orT'])
                k.dma(gates[:], D["S_g"][o0:o0 + 128, :], ['S_g'], ['gates'])
                dense1024(oaT, 'oaT', 2, lambda c, hf: (wau[:, c, hf * 512:(hf + 1) * 512], 'cst'), (0, 1))
                dense1024(orT, 'orT', 4, lambda c, hf: (wru[:, c, hf * 512:(hf + 1) * 512], 'cst'), (2, 3))
                for hf in range(2):
                    sl = slice(hf * 512, (hf + 1) * 512)
                    k.tt('dve', mer[:, sl], B[hf][:], gates[:, sl], ALU.mult, [bn[hf], 'gates'], ['mer'])
                    k.tt('dve', mer2[:, sl], B[2 + hf][:], gates[:, 1024 + hf * 512:1024 + (hf + 1) * 512], ALU.mult, [bn[2 + hf], 'gates'], ['mer2'])
                k.tt('dve', mer[:], mer[:], mer2[:], ALU.add, ['mer', 'mer2'], ['mer'])
                transpose8(mer, 'mer', xT, 'xT')
                dense1024(xT, 'xT', 8, lambda c, hf: (wst[ws_out[hf]][:].rearrange("p (c n) -> p c n", c=8)[:, c, :], 'wst%d' % ws_out[hf]), (6, 7))
                pso = B[6][:]
                k.act(junk[:, 0:512], B[6][:], AF.Square, [bn[6]], ['junk', 'ss'], accum_out=ss[:])
                k.act(junk[:, 512:1024], B[7][:], AF.Square, [bn[7]], ['junk', 'rstd'], accum_out=rstd[:])
                k.tt('dve', ss[:], ss[:], rstd[:], ALU.add, ['ss', 'rstd'], ['ss'])
                k.ts('dve', rstd[:], ss[:], 1.0 / 1024, EPS, ALU.mult, ALU.add, ['ss'], ['rstd'])
                k.act(rstd[:], rstd[:], AF.Sqrt, ['rstd'], ['rstd'])
                k.S.op('dve', lambda e: e.reciprocal(out=rstd[:], in_=rstd[:]), ['rstd'], ['rstd'])
                for hf in range(2):
                    sl = slice(hf * 512, (hf + 1) * 512)
                    k.stt(mer[:, sl], B[6 + hf][:], rstd[:, 0:1], gm[:, sl], ALU.mult, ALU.mult, [bn[6 + hf], 'rstd', 'cst'], ['mer'])
                k.tt('dve', h_[:], h_[:], mer[:], ALU.add, [hn, 'mer'], [hn])
                k.rms_rstd(h_[:], junk[:], ss[:], rstd[:], 1024, EPS, [hn], ('junk', 'ss', 'rstd'))
                k.ts('dve', mer2[:], h_[:], rstd[:, 0:1], None, ALU.mult, None, [hn, 'rstd'], ['mer2'])
                transpose8(mer2, 'mer2', fT[:, :, j * 128:(j + 1) * 128], 'fT', scale_ap=gffn)
            for cb in range(8):
                s = wload(D["w_ffi"][:, cb * 512:(cb + 1) * 512], 'col')
                wv = wst[s][:].rearrange("p (c n) -> p c n", c=8)
                for q in range(4):
                    ffc = cb * 4 + q
                    bi = 4 + ffc % 2
                    for c in range(8):
                        k.mm(B[bi][:, 0:TB], wv[:, c, q * 128:(q + 1) * 128], fT[:, c, :], c == 0, c == 7, ['wst%d' % s, 'fT'], [bn[bi]], fast=True)
                    rl = rel[ffc % 2]; rln = 'rel%d' % (ffc % 2)
                    k.act(rl[:], B[bi][:, 0:TB], AF.Relu, [bn[bi]], [rln])
                    k.tt('dve', hT[:, ffc, :].bitcast(F32R), rl[:], rl[:], ALU.mult, [rln], ['hT'])
            for rb in range(8):
                s = wload(D["w_ffo"][rb * 512:(rb + 1) * 512, :], 'row')
                wv = wst[s][:].rearrange("p (c n) -> p c n", c=4)
                for j in range(TB // 128):
                    for hf in range(2):
                        bi = 2 * j + hf
                        for q in range(4):
                            ffc = rb * 4 + q
                            k.mm(B[bi][:], hT[:, ffc, j * 128:(j + 1) * 128], wv[:, q, hf * 512:(hf + 1) * 512], ffc == 0, ffc == 31,
                                 ['hT', 'wst%d' % s], [bn[bi]], fast=True)
            ws_pg = [wload(D["w_pg"][:, hf * 512:(hf + 1) * 512], 'col') for hf in range(2)]
            for j in range(TB // 128):
                o0 = blk * TB + j * 128
                h_ = hres[j]; hn = 'h%d' % j
                b0, b1 = 2 * j, 2 * j + 1
                k.act(junk[:, 0:512], B[b0][:], AF.Square, [bn[b0]], ['junk', 'ss'], accum_out=ss[:])
                k.act(junk[:, 512:1024], B[b1][:], AF.Square, [bn[b1]], ['junk', 'rstd'], accum_out=rstd[:])
                k.tt('dve', ss[:], ss[:], rstd[:], ALU.add, ['ss', 'rstd'], ['ss'])
                k.ts('dve', rstd[:], ss[:], 1.0 / 1024, EPS, ALU.mult, ALU.add, ['ss'], ['rstd'])
                k.act(rstd[:], rstd[:], AF.Sqrt, ['rstd'], ['rstd'])
                k.S.op('dve', lambda e: e.reciprocal(out=rstd[:], in_=rstd[:]), ['rstd'], ['rstd'])
                for hf, bb in enumerate((b0, b1)):
                    sl = slice(hf * 512, (hf + 1) * 512)
                    k.stt(mer[:, sl], B[bb][:], rstd[:, 0:1], gf[:, sl], ALU.mult, ALU.mult, [bn[bb], 'rstd', 'cst'], ['mer'])
                k.tt('dve', h_[:], h_[:], mer[:], ALU.add, [hn, 'mer'], [hn])
                transpose8(h_, hn, xT, 'xT')
                dense1024(xT, 'xT', 8, lambda c, hf: (wst[ws_pg[hf]][:].rearrange("p (c n) -> p c n", c=8)[:, c, :], 'wst%d' % ws_pg[hf]), (6, 7))
                for hf in range(2):
                    k.act(mer2[:, hf * 512:(hf + 1) * 512], B[6 + hf][:], AF.Sigmoid, [bn[6 + hf]], ['mer2'])
                k.dma(pt_[:], D["pown"][o0:o0 + 128, :], [], ['pt'])
                for c in range(2):
                    k.tr(B[4][:, c * 128:(c + 1) * 128], pt_[:, c * 128:(c + 1) * 128], ['pt'], [bn[4]])
                k.act(pT[:].bitcast(F32R), B[4][:, 0:256].rearrange("p (c t) -> p c t", c=2), AF.Copy, [bn[4]], ['pT'])
                dense1024(pT, 'pT', 2, lambda c, hf: (wple[:, c, hf * 512:(hf + 1) * 512], 'cst'), (6, 7))
                for hf in range(2):
                    sl = slice(hf * 512, (hf + 1) * 512)
                    k.tt('dve', mer[:, sl], B[6 + hf][:], mer2[:, sl], ALU.mult, [bn[6 + hf], 'mer2'], ['mer'])
                k.rms_rstd(mer[:], junk[:], ss[:], rstd[:], 1024, EPS, ['mer'], ('junk', 'ss', 'rstd'))
                k.stt(mer[:], mer[:], rstd[:, 0:1], gp[:], ALU.mult, ALU.mult, ['mer', 'rstd', 'cst'], ['mer'])
                k.tt('dve', mer[:], mer[:], h_[:], ALU.add, ['mer', hn], ['mer'])
                k.dma(D["out"][o0:o0 + 128, :], mer[:], ['mer'], ['out'])


def _consts():
    i = np.arange(128)
    kq = i[:, None]; qq = i[None, :]
    mprev = (kq >= qq).astype(np.float32); mcur = (kq <= qq).astype(np.float32)
    maskA = np.concatenate([mprev, mcur], 1)
    maskA4 = np.tile(maskA, (1, 4))
    strict = (kq < qq).astype(np.float32); incl = (kq <= qq).astype(np.float32)
    maskR = np.concatenate([strict, incl, strict, incl], 1)
    tri = (kq <= qq).astype(np.float32)
    bm = ((kq // 64) == (qq // 64)).astype(np.float32)
    bm4 = np.tile(bm, (1, 4))
    onesPad = np.zeros((128, 2, 128), np.float32)
    onesPad[:, 0, 0:64] = 1.0; onesPad[:, 1, 64:128] = 1.0
    return dict(ident=np.eye(128, dtype=np.float32), maskA4=maskA4, maskR=maskR, tri=tri, blockmask4=bm4,
                onesPad=onesPad.reshape(128, 256))


def make_in_maps(inp):
    f = lambda a: np.ascontiguousarray(np.asarray(a, dtype=np.float32))
    x = f(inp['x']); p = f(inp['p'])
    cst = _consts()
    shared = dict(
        w_in=f(inp['w_in'][0]), w_au=f(inp['w_attn_up'][0]), w_ru=f(inp['w_rwkv_up'][0]), w_out=f(inp['w_out'][0]),
        w_ffi=f(inp['w_ff_in'][0]), w_ffo=f(inp['w_ff_out'][0]), w_ple=f(inp['w_ple'][0]), w_pg=f(inp['w_ple_gate'][0]),
        gpre=f(np.asarray(inp['mix_pre_norm'][0]).reshape(8, 128).T), gffn=f(np.asarray(inp['ffn_pre_norm'][0]).reshape(8, 128).T),
        g_mpost=f(np.asarray(inp['mix_post_norm'][0]).reshape(1, 1024)), g_fpost=f(np.asarray(inp['ffn_post_norm'][0]).reshape(1, 1024)),
        g_ppost=f(np.asarray(inp['ple_post_norm'][0]).reshape(1, 1024)),
        mu=f(np.asarray(inp['rwkv_mu'][0]).reshape(1, 1696)),
        w0=f(np.asarray(inp['rwkv_w0'][0]).reshape(1, 512)), a0=f(np.asarray(inp['rwkv_a0'][0]).reshape(1, 512)),
        k_k=f(np.asarray(inp['rwkv_k_k'][0]).reshape(1, 512)), k_a=f(np.asarray(inp['rwkv_k_a'][0]).reshape(1, 512)),
        ln_w=f(np.asarray(inp['rwkv_ln_w'][0]).reshape(1, 512)), ln_b=f(np.asarray(inp['rwkv_ln_b'][0]).reshape(1, 512)),
        r_k=f(np.asarray(inp['rwkv_r_k'][0]).reshape(1, 512)),
        g2=f(inp['rwkv_g2'][0]),
    )
    w2pad = np.zeros((64, 512), np.float32); w2pad[0:32] = np.asarray(inp['rwkv_w2'][0])
    a2pad = np.zeros((64, 512), np.float32); a2pad[32:64] = np.asarray(inp['rwkv_a2'][0])
    shared.update(w2pad=w2pad, a2pad=a2pad)
    shared.update({kk: vv for kk, vv in cst.items()})
    inv_freq = (10000.0 ** (-np.arange(0, 64, 2, dtype=np.float32) / np.float32(64))).astype(np.float32)
    maps = []
    for c in range(8):
        b, half = c // 2, c % 2
        xloc = np.zeros((NL, 1024), np.float32)
        if half == 0:
            xloc[NPRE:] = x[b, 0:4096]
        else:
            xloc[:] = x[b]
        pos = (half * 4096 - 2048 + np.arange(NK)).astype(np.float32)
        ang = (pos[:, None] * inv_freq[None, :]).astype(np.float32)
        m = dict(shared)
        m.update(xloc=xloc, pown=f(p[0, b, half * 4096:(half + 1) * 4096]),
                 cosT=f(np.tile(np.cos(ang), (1, 12))), sinT=f(np.tile(np.sin(ang), (1, 12))),
                 onesPre=f(cst['onesPad'] * float(half)))
        maps.append(m)
    return maps


def kernel(**inputs):
    nc = build_program()
    maps = make_in_maps(inputs)
    res = run_bass_kernel_spmd(nc, maps, core_ids=list(range(8)))
    out = np.zeros((4, 8192, 1024), np.float32)
    for c in range(8):
        b, half = c // 2, c % 2
        out[b, half * 4096:(half + 1) * 4096] = res.results[c]["out"]
    return out
```
